# Optimizing a Trainium2 kernel written in Bass

```python
import jax, jax.numpy as jnp
from jax import lax
import numpy as np

D_MODEL = 2048
BATCH = 2
SEQ = 4096
DEPTH = 2

CTX_LEN = 256
GRID_W = 64
HG_HEADS = 8
HG_KEY_DIM = 128
HG_VAL_DIM = 128
HG_KEY_WIDTH = HG_HEADS * HG_KEY_DIM
HG_WIDTH = HG_HEADS * HG_VAL_DIM
CONV_CH = D_MODEL - HG_WIDTH
CONV_GROUPS = 8
CONV_WIDTH = 31
CHUNK = 64
D_FF = 5632
FFN_CONV = 3
N_MOD = 6
EPS = 1e-6
LN_EPS = 1e-5
IN_SIZES = (HG_KEY_WIDTH, HG_KEY_WIDTH, HG_KEY_WIDTH, HG_WIDTH, HG_WIDTH, CONV_CH, CONV_CH)
IN_COLS = sum(IN_SIZES)

kernel_name = "hgrn2_conformer_hybrid_dit"


def rms_norm(x, w):
    xf = x.astype(jnp.float32)
    y = xf * lax.rsqrt(jnp.mean(xf * xf, axis=-1, keepdims=True) + EPS)
    return (y * w.astype(jnp.float32)).astype(x.dtype)


def modulate(h, shift, scale):
    return h * (1 + scale) + shift


def dwconv1d(x, w, b):
    k, ch = w.shape
    pad = k // 2
    y = lax.conv_general_dilated(x, w[:, None, :].astype(x.dtype), window_strides=(1,),
                                 padding=((pad, pad),), dimension_numbers=('NWC', 'WIO', 'NWC'),
                                 feature_group_count=ch)
    return y + b.astype(x.dtype)


def dwconv2d_grid(x, w, b):
    bsz, length, ch = x.shape
    rows = length // GRID_W
    kh, kw = w.shape[0], w.shape[1]
    xg = x.reshape(bsz, rows, GRID_W, ch)
    y = lax.conv_general_dilated(xg, w[:, :, None, :].astype(x.dtype), window_strides=(1, 1),
                                 padding=((kh // 2, kh // 2), (kw // 2, kw // 2)),
                                 dimension_numbers=('NHWC', 'HWIO', 'NHWC'), feature_group_count=ch)
    return y.reshape(bsz, length, ch) + b.astype(x.dtype)


def group_layer_norm(u, w, b):
    bsz, length, ch = u.shape
    ug = u.astype(jnp.float32).reshape(bsz, length, CONV_GROUPS, ch // CONV_GROUPS)
    mu = jnp.mean(ug, axis=-1, keepdims=True)
    d = ug - mu
    var = jnp.mean(d * d, axis=-1, keepdims=True)
    y = (d * lax.rsqrt(var + LN_EPS)).reshape(bsz, length, ch)
    return (y * w.astype(jnp.float32) + b.astype(jnp.float32)).astype(u.dtype)


def hgrn2_chunk_scan(q, logf, k, v, s0):
    bsz, length, nh, dk = q.shape
    dv = v.shape[-1]
    n = length // CHUNK
    mask = jnp.tril(jnp.ones((CHUNK, CHUNK), dtype=bool))[:, :, None]

    def to_chunks(t):
        return t.reshape(bsz, n, CHUNK, nh, t.shape[-1]).transpose(1, 0, 3, 2, 4)

    def step(state, xs):
        qc, lc, kc, vc = xs
        bcum = jnp.cumsum(lc, axis=-2)
        b_last = bcum[:, :, -1:, :]
        o_inter = jnp.einsum('bhck,bhkv->bhcv', qc * jnp.exp(bcum), state)
        diff = bcum[:, :, :, None, :] - bcum[:, :, None, :, :]
        decay = jnp.exp(jnp.where(mask, diff, -jnp.inf))
        att = jnp.einsum('bhtk,bhsk,bhtsk->bhts', qc, kc, decay)
        o = o_inter + jnp.einsum('bhts,bhsv->bhtv', att, vc)
        new_state = jnp.exp(b_last[:, :, 0, :])[..., None] * state + \
            jnp.einsum('bhck,bhcv->bhkv', kc * jnp.exp(b_last - bcum), vc)
        return new_state, o

    s_fin, o = lax.scan(step, s0, (to_chunks(q), to_chunks(logf), to_chunks(k), to_chunks(v)))
    o = o.transpose(1, 0, 3, 2, 4).reshape(bsz, length, nh, dv)
    return o, s_fin


def hgrn2_direction(q, f_raw, v, lb, s0, reverse):
    if reverse:
        q, f_raw, v = jnp.flip(q, 1), jnp.flip(f_raw, 1), jnp.flip(v, 1)
    lb32 = lb.reshape(HG_HEADS, HG_KEY_DIM).astype(jnp.float32)
    f = lb32 + (1.0 - lb32) * jax.nn.sigmoid(f_raw.astype(jnp.float32))
    o, s_fin = hgrn2_chunk_scan(q.astype(jnp.float32), jnp.log(f), 1.0 - f, v.astype(jnp.float32), s0)
    if reverse:
        o = jnp.flip(o, 1)
    return o, s_fin


def hgrn2_readout(o, g, w):
    o = o * lax.rsqrt(jnp.mean(o * o, axis=-1, keepdims=True) + EPS) * w.astype(jnp.float32)
    bsz, length = o.shape[0], o.shape[1]
    return (o.reshape(bsz, length, HG_WIDTH) * jax.nn.silu(g.astype(jnp.float32))).astype(g.dtype)


def conformer_conv(a, b, conv_w, conv_b, ln_w, ln_b):
    u = a * jax.nn.sigmoid(b)
    u = dwconv1d(u, conv_w, conv_b)
    u = group_layer_norm(u, ln_w, ln_b)
    return jax.nn.silu(u)


def split_columns(p):
    points = []
    s = 0
    for n in IN_SIZES[:-1]:
        s += n
        points.append(s)
    return jnp.split(p, points, axis=-1)


def project(h, w_in):
    q, ff, fb, v, g, a, b = split_columns(h @ w_in)
    bsz, length = h.shape[0], h.shape[1]
    hk = lambda t: t.reshape(bsz, length, HG_HEADS, HG_KEY_DIM)
    hv = lambda t: t.reshape(bsz, length, HG_HEADS, HG_VAL_DIM)
    return hk(jax.nn.silu(q)), hk(ff), hk(fb), hv(v), g, a, b


def token_mixers(h, hc, w_in, lb_f, lb_b, hg_norm_w, conv_w, conv_b, conv_ln_w, conv_ln_b, w_out, ctx_out):
    q, ff, fb, v, g, a, b = project(h, w_in)
    qc, ffc, fbc, vc, gc, ac, bc = project(hc, w_in)
    zero = jnp.zeros((h.shape[0], HG_HEADS, HG_KEY_DIM, HG_VAL_DIM), jnp.float32)
    oc_f, sc_f = hgrn2_direction(qc, ffc, vc, lb_f, zero, False)
    oc_b, sc_b = hgrn2_direction(qc, fbc, vc, lb_b, zero, True)
    o_f, _ = hgrn2_direction(q, ff, v, lb_f, sc_f, False)
    o_b, _ = hgrn2_direction(q, fb, v, lb_b, sc_b, True)
    y = jnp.concatenate([hgrn2_readout(o_f + o_b, g, hg_norm_w),
                         conformer_conv(a, b, conv_w, conv_b, conv_ln_w, conv_ln_b)], axis=-1) @ w_out
    if not ctx_out:
        return y, None
    yc = jnp.concatenate([hgrn2_readout(oc_f + oc_b, gc, hg_norm_w),
                          conformer_conv(ac, bc, conv_w, conv_b, conv_ln_w, conv_ln_b)], axis=-1) @ w_out
    return y, yc


def conv_ffn(h, w_up, cw, cb, w_down, grid):
    gate, val = jnp.split(h @ w_up, 2, axis=-1)
    if grid:
        gate = dwconv2d_grid(gate, cw, cb)
    else:
        gate = dwconv1d(gate, cw[FFN_CONV // 2], cb)
    return (jax.nn.gelu(gate, approximate=False) * val) @ w_down


def setup_inputs(seed: int = 0) -> dict:
    key = jax.random.key(seed)
    ks = jax.random.split(key, 20)
    nrm = lambda k, shape, s: jax.random.normal(k, shape, jnp.float32) * s
    D = D_MODEL
    return {
        "x": nrm(ks[0], (BATCH, SEQ, D), 1.0),
        "c": nrm(ks[1], (BATCH, D), 1.0),
        "ctx": nrm(ks[2], (BATCH, CTX_LEN, D), 1.0),
        "c_ctx": nrm(ks[3], (D,), 1.0),
        "w_mod": nrm(ks[4], (DEPTH, D, N_MOD * D), 0.5 * D ** -0.5),
        "b_mod": nrm(ks[5], (DEPTH, N_MOD * D), 0.02),
        "norm_w": 1.0 + nrm(ks[6], (DEPTH, 4, D), 0.02),
        "w_in": nrm(ks[7], (DEPTH, D, IN_COLS), D ** -0.5),
        "lb_logits": nrm(ks[8], (2, DEPTH, HG_KEY_WIDTH), 1.0),
        "hg_norm_w": 1.0 + nrm(ks[9], (DEPTH, HG_VAL_DIM), 0.02),
        "conv_w": nrm(ks[10], (DEPTH, CONV_WIDTH, CONV_CH), CONV_WIDTH ** -0.5),
        "conv_b": nrm(ks[11], (DEPTH, CONV_CH), 0.02),
        "conv_ln_w": 1.0 + nrm(ks[12], (DEPTH, CONV_CH), 0.02),
        "conv_ln_b": nrm(ks[13], (DEPTH, CONV_CH), 0.02),
        "w_out": nrm(ks[14], (DEPTH, D, D), D ** -0.5),
        "ffn_up": nrm(ks[15], (DEPTH, D, 2 * D_FF), D ** -0.5),
        "ffn_conv_w": nrm(ks[16], (DEPTH, FFN_CONV, FFN_CONV, D_FF), (FFN_CONV * FFN_CONV) ** -0.5),
        "ffn_conv_b": nrm(ks[17], (DEPTH, D_FF), 0.02),
        "ffn_down": nrm(ks[18], (DEPTH, D_FF, D), D_FF ** -0.5),
    }


def reference(x, c, ctx, c_ctx, w_mod, b_mod, norm_w, w_in, lb_logits, hg_norm_w, conv_w, conv_b,
              conv_ln_w, conv_ln_b, w_out, ffn_up, ffn_conv_w, ffn_conv_b, ffn_down):
    p = jax.nn.softmax(lb_logits.astype(jnp.float32), axis=1)
    lbs = jnp.cumsum(p, axis=1) - p[:, :1]
    silu_c = jax.nn.silu(c)
    silu_cc = jax.nn.silu(c_ctx)
    xc = ctx
    for l in range(DEPTH):
        ctx_out = l < DEPTH - 1
        mod = (silu_c @ w_mod[l] + b_mod[l])[:, None, :]
        modc = silu_cc @ w_mod[l] + b_mod[l]
        sh1, sc1, g1, sh2, sc2, g2 = jnp.split(mod, N_MOD, axis=-1)
        sh1c, sc1c, g1c, sh2c, sc2c, g2c = jnp.split(modc, N_MOD, axis=-1)
        h = modulate(rms_norm(x, norm_w[l, 0]), sh1, sc1)
        hc = modulate(rms_norm(xc, norm_w[l, 0]), sh1c, sc1c)
        y, yc = token_mixers(h, hc, w_in[l], lbs[0, l], lbs[1, l], hg_norm_w[l], conv_w[l], conv_b[l],
                             conv_ln_w[l], conv_ln_b[l], w_out[l], ctx_out)
        x = x + g1 * rms_norm(y, norm_w[l, 1])
        h2 = modulate(rms_norm(x, norm_w[l, 2]), sh2, sc2)
        x = x + g2 * rms_norm(conv_ffn(h2, ffn_up[l], ffn_conv_w[l], ffn_conv_b[l], ffn_down[l], True),
                              norm_w[l, 3])
        if ctx_out:
            xc = xc + g1c * rms_norm(yc, norm_w[l, 1])
            h2c = modulate(rms_norm(xc, norm_w[l, 2]), sh2c, sc2c)
            xc = xc + g2c * rms_norm(conv_ffn(h2c, ffn_up[l], ffn_conv_w[l], ffn_conv_b[l], ffn_down[l], False),
                                     norm_w[l, 3])
    return x
```

```python
import numpy as np
from contextlib import ExitStack
import concourse.bass as bass
import concourse.mybir as mybir
from concourse.bass_utils import run_bass_kernel_spmd

F32 = mybir.dt.float32
BF16 = mybir.dt.bfloat16
ALU = mybir.AluOpType
AF = mybir.ActivationFunctionType

D = 2048
KC = 16
NCTX = 256
NLAT = 4096
NT = NCTX + NLAT
DFF = 5632
FC = 44
INC = 7168
DEPTH = 2
EPS = 1e-6
LN_EPS = 1e-5
TILES = [(0, 256)] + [(256 + 512 * i, 512) for i in range(8)]
NDS = 48


class Buf:
    __slots__ = ("w", "r")

    def __init__(self):
        self.w = None
        self.r = {}


class Sched:
    def __init__(self, nc, es):
        self.nc = nc
        self.eng = {"pe": nc.tensor, "act": nc.scalar, "dve": nc.vector, "pool": nc.gpsimd, "sp": nc.sync}
        self.semobj = {}
        for k in self.eng:
            self.semobj[k] = es.enter_context(nc.semaphore("s_" + k))
        for i in range(NDS):
            self.semobj[("d", i)] = es.enter_context(nc.semaphore("d%d" % i))
        self.cnt = {k: 0 for k in self.eng}
        self.seen = {k: {} for k in self.eng}
        self.dval = [0] * NDS
        self.dnext = 0
        self.ninst = 0
        self.log = {k: [] for k in self.eng}

    def _wait(self, e, deps):
        need = {}
        for d in deps:
            if d is None:
                continue
            k, v = d
            if k == e and v > self.cnt[e]:
                continue
            if v > need.get(k, 0):
                need[k] = v
        for k, v in need.items():
            if self.seen[e].get(k, 0) >= v:
                continue
            self.eng[e].wait_ge(self.semobj[k], v)
            self.log[e].append(("w", k, v))
            self.seen[e][k] = v

    def _deps(self, e, reads, writes, chain=False):
        deps = []
        for b in reads:
            if chain and b.w is not None and b.w[0] == e:
                continue
            deps.append(b.w)
        for b in writes:
            if b.w is not None and b.w[0] != e:
                deps.append(b.w)
            for k, v in b.r.items():
                if k != e:
                    deps.append((k, v))
        return deps

    def op(self, e, fn, reads=(), writes=(), inc=True, chain=False):
        self._wait(e, self._deps(e, reads, writes, chain))
        inst = fn(self.eng[e])
        ev = (e, self.cnt[e] + 1)
        if inc:
            inst.then_inc(self.semobj[e], 1)
            self.cnt[e] += 1
            self.log[e].append(("i", e, 1))
        else:
            self.log[e].append(("n", e, 0))
        for b in reads:
            if b.r.get(e, 0) < ev[1]:
                b.r[e] = ev[1]
        for b in writes:
            b.w = ev
            b.r = {}
        self.ninst += 1
        return inst

    def dma(self, q, out, in_, reads=(), writes=()):
        deps = self._deps(q, reads, writes)
        i = self.dnext
        self.dnext = (i + 1) % NDS
        key = ("d", i)
        if self.dval[i] > 0:
            deps.append((key, self.dval[i]))
        self._wait(q, deps)
        inst = self.eng[q].dma_start(out=out, in_=in_)
        self.dval[i] += 16
        inst.then_inc(self.semobj[key], 16)
        self.log[q].append(("i", key, 16))
        ev = (key, self.dval[i])
        for b in reads:
            b.r[key] = ev[1]
        for b in writes:
            b.w = ev
            b.r = {}
        self.ninst += 1
        return ev

    def barrier(self):
        evs = [(k, v) for k, v in self.cnt.items() if v > 0]
        evs += [(("d", i), v) for i, v in enumerate(self.dval) if v > 0]
        for e in self.eng:
            self._wait(e, [x for x in evs if x[0] != e])


class Ring:
    def __init__(self, items):
        self.items = items
        self.i = 0

    def next(self):
        it = self.items[self.i]
        self.i = (self.i + 1) % len(self.items)
        return it


def build_program(stop_after=None, debug=False):
    nc = bass.Bass("TRN2", target_bir_lowering=False)
    es = ExitStack()
    S = Sched(nc, es)

    def dram_in(name, shape, dt=F32):
        return nc.dram_tensor(name, list(shape), dt, kind="ExternalInput").ap()

    debug = set(debug or ())

    def dram_scr(name, shape, dt=F32):
        return nc.dram_tensor(name, list(shape), dt, kind=("ExternalOutput" if name in debug else "Internal")).ap()

    xT = dram_in("xT", [D, NT])
    cT = dram_in("cT", [128, KC, 2])
    w_mod = dram_in("w_mod", [DEPTH, D, 6 * D])
    b_modT = dram_in("b_modT", [128, DEPTH, 96])
    norm_wT = dram_in("norm_wT", [128, DEPTH, 4, KC])
    w_in = dram_in("w_in", [DEPTH, D, INC])
    lbT = dram_in("lbT", [128, 2, DEPTH, 8])
    hgwT = dram_in("hgwT", [128, DEPTH])
    conv_wT = dram_in("conv_wT", [128, DEPTH, 8, 31])
    conv_bT = dram_in("conv_bT", [128, DEPTH, 8])
    conv_lnwT = dram_in("conv_lnwT", [128, DEPTH, 8])
    conv_lnbT = dram_in("conv_lnbT", [128, DEPTH, 8])
    w_out = dram_in("w_out", [DEPTH, D, D])
    ffn_up = dram_in("ffn_up", [DEPTH, D, 2 * DFF])
    ffn_cwT = dram_in("ffn_cwT", [128, DEPTH, FC, 9])
    ffn_cbT = dram_in("ffn_cbT", [128, DEPTH, FC])
    ffn_down = dram_in("ffn_down", [DEPTH, DFF, D])
    consts = dram_in("consts", [128, 832])
    yT = nc.dram_tensor("yT", [D, NLAT], F32, kind="ExternalOutput").ap()

    PROJ = dram_scr("PROJ", [INC, NT])
    OF = dram_scr("OF", [1024, NT])
    Y = dram_scr("Y", [D, NT])
    X1 = dram_scr("X1", [D, NT])
    XB = dram_scr("XB", [D, NT])
    ACTD = dram_scr("ACTD", [DFF, NT], BF16)
    dbufs = {}

    def DB(*key):
        b = dbufs.get(key)
        if b is None:
            b = dbufs[key] = Buf()
        return b

    uid = [0]

    def sb(name, shape, dt, stack=None):
        uid[0] += 1
        return (stack or es).enter_context(nc.sbuf_tensor("%s_%d" % (name, uid[0]), list(shape), dt))

    def ps(name, shape, dt, stack):
        uid[0] += 1
        return stack.enter_context(nc.psum_tensor("%s_%d" % (name, uid[0]), list(shape), dt))

    H = sb("H", [128, KC, NT], BF16)
    HB = [[Buf() for _ in TILES] for _ in range(KC)]
    cst = sb("cst", [128, 832], F32)
    cstB = Buf()
    ident_bf = sb("ident_bf", [128, 128], BF16)
    ones_bf = sb("ones_bf", [128, 128], BF16)
    cbfB = Buf()
    modv = sb("modv", [128, DEPTH, 96, 2], F32)
    modB = Buf()
    vecs = sb("vecs", [128, DEPTH, 6, KC, 2], F32)
    vecB = Buf()
    nw = sb("nw", [128, DEPTH, 4, KC], F32)
    smallB = Buf()
    bmod = sb("bmod", [128, DEPTH, 96], F32)
    lbl = sb("lbl", [128, 2, DEPTH, 8], F32)
    lbv = sb("lbv", [128, 3, 2, DEPTH, 8], F32)
    hgw = sb("hgw", [128, DEPTH], F32)
    cb = sb("cb", [128, DEPTH, 8], F32)
    clw = sb("clw", [128, DEPTH, 8], F32)
    clb = sb("clb", [128, DEPTH, 8], F32)
    fcb = sb("fcb", [128, DEPTH, FC], F32)
    ctile = sb("ctile", [128, KC, 2], F32)
    sc_bf = sb("sc_bf", [128, KC, 2], BF16)

    mask_f = cst[0:32, 256:288]
    mask_b = cst[0:32, 288:320]
    rmask = cst[:, 320:832]

    S.dma("sp", cst[:], consts, writes=[cstB])
    for dst, src in ((nw, norm_wT), (bmod, b_modT), (lbl, lbT), (hgw, hgwT), (cb, conv_bT),
                     (clw, conv_lnwT), (clb, conv_lnbT), (fcb, ffn_cbT), (ctile, cT)):
        S.dma("sp", dst[:], src, writes=[smallB])
    S.op("dve", lambda e: e.tensor_copy(ident_bf[:], cst[:, 0:128]), reads=[cstB], writes=[cbfB])
    S.op("dve", lambda e: e.tensor_copy(ones_bf[:], cst[:, 128:256]), reads=[cstB], writes=[cbfB])
    S.op("dve", lambda e: e.memset(lbv[:, 0, :, 0, :], 0.0), writes=[smallB])
    S.op("dve", lambda e: e.tensor_tensor(lbv[:, 0, :, 1, :], lbl[:, :, 1, :], lbl[:, :, 0, :], ALU.subtract),
         reads=[smallB], writes=[smallB])
    S.op("act", lambda e: e.activation(lbv[:, 0, :, 1, :], lbv[:, 0, :, 1, :], AF.Sigmoid), reads=[smallB], writes=[smallB])
    S.op("dve", lambda e: e.tensor_scalar(lbv[:, 1], lbv[:, 0], -1.0, 1.0, ALU.mult, ALU.add), reads=[smallB], writes=[smallB])
    S.op("dve", lambda e: e.tensor_scalar(lbv[:, 2], lbv[:, 0], -1.0, None, ALU.add), reads=[smallB], writes=[smallB])
    S.op("act", lambda e: e.activation(sc_bf[:], ctile[:], AF.Silu), reads=[smallB], writes=[smallB])

    def mm(out, lhsT, rhs, start, stop):
        return lambda e: e.matmul(out, lhsT, rhs, start=start, stop=stop)

    def phase_mod(l):
        with ExitStack() as st:
            wb = [sb("modw%d" % i, [128, KC, 512], BF16, st) for i in range(2)]
            wbB = [Buf(), Buf()]
            pm = ps("pm", [128, 512], F32, st)
            pmB = Buf()
            for blk in range(24):
                w, wB = wb[blk % 2], wbB[blk % 2]
                S.dma("pool", w[:], w_mod[l][:, blk * 512:(blk + 1) * 512].rearrange("(kc p) n -> p kc n", p=128),
                      writes=[wB])
                for sub in range(4):
                    j = blk * 4 + sub
                    for kc in range(KC):
                        S.op("pe", mm(pm[:, 2 * j:2 * j + 2], w[:, kc, sub * 128:(sub + 1) * 128], sc_bf[:, kc, :],
                                      kc == 0, kc == KC - 1), reads=[wB, smallB], writes=[pmB])
            S.op("dve", lambda e: e.tensor_tensor(modv[:, l], pm[:, 0:192].rearrange("p (j s) -> p j s", s=2),
                                                  bmod[:, l, :].unsqueeze(2).broadcast_to([128, 96, 2]), ALU.add),
                 reads=[pmB, smallB], writes=[modB])
            mv = modv[:, l]

            def nwb(i):
                return nw[:, l, i, :].unsqueeze(2).broadcast_to([128, KC, 2])
            S.op("dve", lambda e: e.scalar_tensor_tensor(vecs[:, l, 0], mv[:, 16:32, :], 1.0, nwb(0), ALU.add, ALU.mult),
                 reads=[modB, smallB], writes=[vecB])
            S.op("dve", lambda e: e.tensor_copy(vecs[:, l, 1], mv[:, 0:16, :]), reads=[modB], writes=[vecB])
            S.op("dve", lambda e: e.tensor_tensor(vecs[:, l, 2], mv[:, 32:48, :], nwb(1), ALU.mult), reads=[modB, smallB], writes=[vecB])
            S.op("dve", lambda e: e.scalar_tensor_tensor(vecs[:, l, 3], mv[:, 64:80, :], 1.0, nwb(2), ALU.add, ALU.mult),
                 reads=[modB, smallB], writes=[vecB])
            S.op("dve", lambda e: e.tensor_copy(vecs[:, l, 4], mv[:, 48:64, :]), reads=[modB], writes=[vecB])
            S.op("dve", lambda e: e.tensor_tensor(vecs[:, l, 5], mv[:, 80:96, :], nwb(3), ALU.mult), reads=[modB, smallB], writes=[vecB])
            S.barrier()

    RT = 128
    NRT = NT // RT

    def phase_resnorm(l, x_src, xkey, y_src, ykey, gi, x_dst, dkey, ai, final, la=None):
        with ExitStack() as st:
            xt = [(sb("rn_x%d" % i, [128, KC, RT], F32, st), Buf()) for i in range(2)]
            yt = [(sb("rn_y%d" % i, [128, KC, RT], F32, st), Buf()) for i in range(2)]
            sq = Ring([(sb("rn_sq%d" % i, [128, KC, RT], BF16, st), Buf()) for i in range(2)])
            tmp = Ring([(sb("rn_t%d" % i, [128, KC, RT], F32, st), Buf()) for i in range(1)])
            rs = [(sb("rn_rs%d" % i, [128, RT], F32, st), Buf()) for i in range(2)]
            rs2 = [(sb("rn_rt%d" % i, [128, RT], F32, st), Buf()) for i in range(2)]
            pss = [(ps("rn_ps%d" % i, [128, 512], F32, st), Buf()) for i in range(2)]
            pss2 = [(ps("rn_pt%d" % i, [128, 512], F32, st), Buf()) for i in range(2)]

            def view(dram, t0):
                return dram[:, t0:t0 + RT].rearrange("(kc p) t -> p kc t", p=128)

            def bct(ap):
                return ap.unsqueeze(1).broadcast_to([128, KC, RT])

            la = l if la is None else la

            def bcv(i, s, ll=l):
                return vecs[:, ll, i, :, s:s + 1].broadcast_to([128, KC, RT])

            def rstd(src, srcB, pst, out):
                p, pB = pst
                o, oB = out
                q, qB = sq.next()
                S.op("act", lambda e: e.activation(q[:], src[:], AF.Square), reads=[srcB], writes=[qB])
                for kc in range(KC):
                    S.op("pe", mm(p[:, 0:RT], ones_bf[:], q[:, kc, :], kc == 0, kc == KC - 1), reads=[qB, cbfB], writes=[pB],
                         inc=(kc == KC - 1))
                S.op("act", lambda e: e.activation(o[:], p[:, 0:RT], AF.Ln, bias=EPS, scale=1.0 / D), reads=[pB], writes=[oB])
                S.op("act", lambda e: e.activation(o[:], o[:], AF.Exp, scale=-0.5), reads=[oB], writes=[oB])

            def rloads(ti):
                t0 = ti * RT
                x, xB = xt[ti % 2]
                S.dma("sp", x[:], view(x_src, t0), reads=[DB(xkey, ti)], writes=[xB])
                if y_src is not None:
                    y, yB = yt[ti % 2]
                    S.dma("sp", y[:], view(y_src, t0), reads=[DB(ykey, ti)], writes=[yB])

            rloads(0)
            if y_src is not None:
                rstd(yt[0][0], yt[0][1], pss[0], rs[0])
            for ti in range(NRT):
                t0 = ti * RT
                s = 1 if t0 < NCTX else 0
                bi = ti % 2
                x, xB = xt[bi]
                if ti + 1 < NRT:
                    rloads(ti + 1)
                if y_src is not None:
                    y, yB = yt[bi]
                    r, rB = rs[bi]
                    t, tB = tmp.next()
                    S.op("dve", lambda e: e.tensor_tensor(t[:], y[:], bct(r[:]), ALU.mult), reads=[yB, rB], writes=[tB])
                    S.op("dve", lambda e: e.tensor_tensor(t[:], t[:], bcv(gi, s), ALU.mult), reads=[tB, vecB], writes=[tB], chain=True)
                    S.op("dve", lambda e: e.tensor_tensor(x[:], x[:], t[:], ALU.add), reads=[tB, xB], writes=[xB], chain=True)
                    if x_dst is not None:
                        S.dma("sp", view(x_dst, t0), x[:], reads=[xB], writes=[DB(dkey, ti)])
                    if final and t0 >= NCTX:
                        S.dma("sp", view(yT, t0 - NCTX), x[:], reads=[xB], writes=[DB("yT", ti)])
                if ai is not None:
                    rstd(x, xB, pss2[bi], rs2[bi])
                if y_src is not None and ti + 1 < NRT:
                    nb = (ti + 1) % 2
                    rstd(yt[nb][0], yt[nb][1], pss[nb], rs[nb])
                if ai is not None:
                    r2, r2B = rs2[bi]
                    hti = 0 if t0 < NCTX else 1 + (t0 - 256) // 512
                    t, tB = tmp.next()
                    S.op("dve", lambda e: e.tensor_tensor(t[:], x[:], bct(r2[:]), ALU.mult), reads=[xB, r2B], writes=[tB], chain=True)
                    S.op("dve", lambda e: e.tensor_tensor(t[:], t[:], bcv(ai, s, la), ALU.mult), reads=[tB, vecB], writes=[tB], chain=True)
                    S.op("dve", lambda e: e.tensor_tensor(H[:, :, t0:t0 + RT], t[:], bcv(ai + 1, s, la), ALU.add), reads=[tB, vecB],
                         writes=[HB[kc][hti] for kc in range(KC)], chain=True)
            S.barrier()

    def phase_dense(wsrc, ncols, out_dram, okey, nkc=KC):
        with ExitStack() as st:
            wb = [sb("dw%d" % i, [128, nkc, 512], BF16, st) for i in range(2)]
            wbB = [Buf(), Buf()]
            pp = Ring([(ps("dps%d" % i, [128, 512], F32, st), Buf()) for i in range(4)])
            stg = Ring([(sb("dstg%d" % i, [128, 512], F32, st), Buf()) for i in range(4)])
            nblk = ncols // 512
            ev = 0
            for blk in range(nblk):
                w, wB = wb[blk % 2], wbB[blk % 2]
                S.dma("pool", w[:], wsrc[:, blk * 512:(blk + 1) * 512].rearrange("(kc p) n -> p kc n", p=128), writes=[wB])
                for sub in range(4):
                    cc = blk * 4 + sub
                    for ti, (t0, sz) in enumerate(TILES):
                        p, pB = pp.next()
                        for kc in range(nkc):
                            S.op("pe", mm(p[:, 0:sz], w[:, kc, sub * 128:(sub + 1) * 128], H[:, kc, t0:t0 + sz], kc == 0, kc == nkc - 1),
                                 reads=[wB, HB[kc][ti]], writes=[pB], inc=(kc == nkc - 1))
                        g, gB = stg.next()
                        if ev % 2 == 0:
                            S.op("act", lambda e: e.copy(g[:, 0:sz], p[:, 0:sz]), reads=[pB], writes=[gB])
                        else:
                            S.op("dve", lambda e: e.tensor_copy(g[:, 0:sz], p[:, 0:sz]), reads=[pB], writes=[gB])
                        ev += 1
                        S.dma("sp", out_dram[cc * 128:(cc + 1) * 128, t0:t0 + sz], g[:, 0:sz], reads=[gB], writes=[DB(okey, cc, ti)])
            S.barrier()

    HT = 256
    HTILES = [(0, 256)] + [(256 + HT * i, HT) for i in range(NLAT // HT)]
    NCHAIN = 4
    NCK = HT // 32
    NCHK = NT // 32
    HPS = KC * NT
    QTd = dram_scr("QTd", [2, 1024, NT], BF16)
    KTd = dram_scr("KTd", [2, 1024, NT], BF16)
    KXd = dram_scr("KXd", [2, 8, NCHK, 32, 128], BF16)
    VXd = dram_scr("VXd", [8, NCHK, 32, 128], BF16)
    DAd = dram_scr("DAd", [2, 1024, NCHK], F32)
    SGd = dram_scr("SGd", [1024, NT], F32)

    class HAlias:
        def __init__(self):
            self.off = 8 * NT

        def cm(self):
            ap = bass.AP(H, self.off, [[HPS, 128], [1, HT]])
            self.off += HT
            return ap

        def tm(self):
            ap = bass.AP(H, self.off, [[HPS, 32], [128, NCK], [1, 128]])
            self.off += NCK * 128
            return ap

    def h2t(t0):
        return 0 if t0 < NCTX else 1 + (t0 - NCTX) // 512

    def c3(ap):
        return ap.rearrange("p (c j) -> p c j", j=32)

    def phase_hgrn_pre(l):
        with ExitStack() as st:
            ha = HAlias()

            def r32(name, n):
                return Ring([(sb("%s%d" % (name, i), [128, HT], F32, st), Buf()) for i in range(n)])
            raw = r32("praw", 20)
            sgr = r32("psg", 4)
            logf = r32("plog", 4); kk = r32("pkk", 4); bc = r32("pbc", 4); bc2 = r32("pbc2", 2); d3 = r32("pd3", 4)
            ex = r32("pex", 6)
            qts = Ring([(ha.cm(), Buf()) for _ in range(6)])
            kts = Ring([(ha.cm(), Buf()) for _ in range(6)])
            khs = Ring([(ha.cm(), Buf()) for _ in range(6)])
            vbs = Ring([(ha.cm(), Buf()) for _ in range(3)])
            kxs = Ring([(ha.tm(), Buf()) for _ in range(4)])
            vxs = Ring([(ha.tm(), Buf()) for _ in range(3)])
            dAall = [(sb("pdA%d" % i, [128, NCHK], F32, st), Buf()) for i in range(4)]
            ptr = Ring([(ps("pptr%d" % i, [128, 1024], BF16, st), Buf()) for i in range(4)])

            def to_tokmajor(src, srcB, dst, dstB):
                p, pB = ptr.next()
                for c in range(NCK):
                    S.op("pe", lambda e: e.transpose(p[0:32, c * 128:(c + 1) * 128], src[:, c * 32:(c + 1) * 32], ident_bf[:]),
                         reads=[srcB, cbfB], writes=[pB], inc=(c == NCK - 1))
                S.op("dve", lambda e: e.tensor_copy(dst, p[0:32, 0:NCK * 128].rearrange("p (c v) -> p c v", v=128)), reads=[pB], writes=[dstB])

            def loadsP(h, ti):
                t0, sz = HTILES[ti]
                pt = h2t(t0)
                q_, qB = raw.next(); v_, vB = raw.next()
                S.dma("sp", q_[:], PROJ[h * 128:(h + 1) * 128, t0:t0 + sz], reads=[DB("proj", h, pt)], writes=[qB])
                vr = 3072 + h * 128
                S.dma("sp", v_[:], PROJ[vr:vr + 128, t0:t0 + sz], reads=[DB("proj", vr // 128, pt)], writes=[vB])
                F = []
                for d in range(2):
                    f_, fB = raw.next()
                    fr = 1024 * (1 + d) + h * 128
                    S.dma("sp", f_[:], PROJ[fr:fr + 128, t0:t0 + sz], reads=[DB("proj", fr // 128, pt)], writes=[fB])
                    F.append((f_, fB))
                g_, gB = raw.next()
                gr = 4096 + h * 128
                S.dma("sp", g_[:], PROJ[gr:gr + 128, t0:t0 + sz], reads=[DB("proj", gr // 128, pt)], writes=[gB])
                return (q_, qB, v_, vB, F, g_, gB)

            def stageA(h, ti, LD):
                t0, sz = HTILES[ti]
                pt = h2t(t0)
                c0 = t0 // 32
                q_, qB, v_, vB, F, g_, gB = LD
                for d in range(2):
                    f_, fB = F[d]
                    S.op("act", lambda e: e.activation(f_[:], f_[:], AF.Sigmoid), reads=[fB], writes=[fB])
                sq_, sqB_ = sgr.next(); sg_, sgB_ = sgr.next()
                S.op("act", lambda e: e.activation(sq_[:], q_[:], AF.Sigmoid), reads=[qB], writes=[sqB_])
                S.op("act", lambda e: e.activation(sg_[:], g_[:], AF.Sigmoid), reads=[gB], writes=[sgB_])
                S.op("dve", lambda e: e.tensor_tensor(q_[:], q_[:], sq_[:], ALU.mult), reads=[qB, sqB_], writes=[qB])
                S.op("dve", lambda e: e.tensor_tensor(g_[:], g_[:], sg_[:], ALU.mult), reads=[gB, sgB_], writes=[gB])
                S.dma("sp", SGd[h * 128:(h + 1) * 128, t0:t0 + sz], g_[:], reads=[gB], writes=[DB("sg", h, ti)])
                vb, vbB = vbs.next()
                S.op("pool", lambda e: e.tensor_copy(vb, v_[:]), reads=[vB], writes=[vbB])
                vx, vxB = vxs.next()
                to_tokmajor(vb, vbB, vx, vxB)
                S.dma("sp", VXd[h, c0:c0 + NCK].rearrange("c s k -> s c k"), vx, reads=[vxB], writes=[DB("vx", h, ti)])
                X = []
                for d in range(2):
                    f_, fB = F[d]
                    lb_ap = lbv[:, 0, d, l, h:h + 1]
                    oml_ap = lbv[:, 1, d, l, h:h + 1]
                    noml_ap = lbv[:, 2, d, l, h:h + 1]
                    lf, lfB = logf.next(); k_, kB = kk.next(); b_, bB = bc.next()
                    S.op("act", lambda e: e.activation(lf[:], f_[:], AF.Ln, bias=lb_ap, scale=oml_ap), reads=[fB, smallB], writes=[lfB])
                    S.op("dve", lambda e: e.tensor_scalar(k_[:], f_[:], noml_ap, oml_ap, ALU.mult, ALU.add), reads=[fB, smallB], writes=[kB])
                    S.op("dve", lambda e: e.tensor_tensor_scan(b_[:], rmask[:, 0:sz], lf[:], 0.0, ALU.mult, ALU.add),
                         reads=[lfB, cstB], writes=[bB])
                    tot = c3(b_[:])[:, :, 31:32]
                    if d == 0:
                        bx, bxB = b_, bB
                    else:
                        bx, bxB = bc2.next()
                        S.op("dve", lambda e: e.tensor_tensor(bx[:], lf[:], b_[:], ALU.subtract), reads=[lfB, bB], writes=[bxB])
                        S.op("dve", lambda e: e.tensor_tensor(c3(bx[:]), c3(bx[:]), tot.broadcast_to([128, NCK, 32]), ALU.add),
                             reads=[bxB, bB], writes=[bxB])
                    dd, ddB = d3.next()
                    S.op("dve", lambda e: e.tensor_tensor(c3(dd[:]), tot.broadcast_to([128, NCK, 32]), c3(bx[:]), ALU.subtract),
                         reads=[bxB, bB], writes=[ddB])
                    X.append(dict(k=k_, kB=kB, b=b_, bB=bB, bx=bx, bxB=bxB, dd=dd, ddB=ddB))
                return dict(h=h, ti=ti, q=q_, qB=qB, X=X)

            def stageB(C):
                h, ti, q_, qB, X = C["h"], C["ti"], C["q"], C["qB"], C["X"]
                t0, sz = HTILES[ti]
                c0 = t0 // 32
                for d in range(2):
                    Z = X[d]
                    da, daB = dAall[d + 2 * (h % 2)]
                    S.op("act", lambda e: e.activation(da[:, c0:c0 + NCK], c3(Z["b"][:])[:, :, 31], AF.Exp), reads=[Z["bB"]], writes=[daB])
                    qt_, qtB = qts.next(); kt_, ktB = kts.next(); kh_, khB = khs.next()
                    x1, x1B = ex.next()
                    S.op("act", lambda e: e.activation(x1[:], Z["bx"][:], AF.Exp), reads=[Z["bxB"]], writes=[x1B])
                    x2, x2B = ex.next()
                    S.op("act", lambda e: e.activation(x2[:], Z["bx"][:], AF.Exp, scale=-1.0), reads=[Z["bxB"]], writes=[x2B])
                    x3, x3B = ex.next()
                    S.op("act", lambda e: e.activation(x3[:], Z["dd"][:], AF.Exp), reads=[Z["ddB"]], writes=[x3B])
                    S.op("dve", lambda e: e.tensor_tensor(qt_, q_[:], x1[:], ALU.mult), reads=[qB, x1B], writes=[qtB])
                    S.op("dve", lambda e: e.tensor_tensor(kt_, Z["k"][:], x2[:], ALU.mult), reads=[Z["kB"], x2B], writes=[ktB])
                    S.op("dve", lambda e: e.tensor_tensor(kh_, Z["k"][:], x3[:], ALU.mult), reads=[Z["kB"], x3B], writes=[khB])
                    S.dma("sp", QTd[d, h * 128:(h + 1) * 128, t0:t0 + sz], qt_, reads=[qtB], writes=[DB("qt", d, h, ti)])
                    S.dma("sp", KTd[d, h * 128:(h + 1) * 128, t0:t0 + sz], kt_, reads=[ktB], writes=[DB("kt", d, h, ti)])
                    kx, kxB = kxs.next()
                    to_tokmajor(kh_, khB, kx, kxB)
                    S.dma("sp", KXd[d, h, c0:c0 + NCK].rearrange("c s k -> s c k"), kx, reads=[kxB], writes=[DB("kx", d, h, ti)])
                if ti == len(HTILES) - 1:
                    for d in range(2):
                        da, daB = dAall[d + 2 * (h % 2)]
                        S.dma("sp", DAd[d, h * 128:(h + 1) * 128, :], da[:], reads=[daB], writes=[DB("da", d, h)])

            items = [(h, ti) for h in range(8) for ti in range(len(HTILES))]
            lds = {}
            for i in range(min(2, len(items))):
                lds[i] = loadsP(*items[i])
            prev = None
            for i, (h, ti) in enumerate(items):
                if i + 2 < len(items):
                    lds[i + 2] = loadsP(*items[i + 2])
                cur = stageA(h, ti, lds.pop(i))
                if prev is not None:
                    stageB(prev)
                prev = cur
            stageB(prev)
            S.barrier()

    def phase_hgrn_scan(l):
        with ExitStack() as st:
            ha = HAlias()

            def r32(name, n):
                return Ring([(sb("%s%d" % (name, i), [128, HT], F32, st), Buf()) for i in range(n)])
            graw = r32("hg", 3 * NCHAIN); ofl = r32("ho", 2 * NCHAIN)
            osum = r32("hos", 2 * NCHAIN); rr = r32("hrr", 2); ofs = r32("hofs", 3)
            sqb = Ring([(sb("hsq%d" % i, [128, HT], BF16, st), Buf()) for i in range(2)])
            sets = [[dict(qt=(ha.cm(), Buf()), kt=(ha.cm(), Buf()), kx=(ha.tm(), Buf()), vx=(ha.tm(), Buf()))
                     for _ in range(NCHAIN)] for _ in range(2)]
            dAc = [(sb("hdA%d" % i, [128, NCHK], F32, st), Buf()) for i in range(NCHAIN)]
            Sst = [(sb("hS%d" % i, [128, 128], F32, st), Buf()) for i in range(NCHAIN)]
            Sbf2 = [[(sb("hSb%d_%d" % (i, k), [128, 128], BF16, st), Buf()) for k in range(2)] for i in range(NCHAIN)]
            par = [0] * NCHAIN
            attm = [Ring([(sb("hat%d_%d" % (i, k), [32, 32], BF16, st), Buf()) for k in range(2)]) for i in range(NCHAIN)]
            pob = [ps("hpo%d" % i, [128, 512], F32, st) for i in range(2)]
            pobB = [Buf(), Buf()]
            po = [(pob[i // 2][:, (i % 2) * HT:(i % 2 + 1) * HT], pobB[i // 2]) for i in range(NCHAIN)]
            pcb = [ps("hpc%d" % i, [128, 512], F32, st) for i in range(NCHAIN)]
            pcB = [Buf() for _ in range(NCHAIN)]
            pa = [Ring([(pcb[i][0:32, k * 32:(k + 1) * 32], pcB[i]) for k in range(2)]) for i in range(NCHAIN)]
            pp = [(pcb[i][:, 128:256], pcB[i]) for i in range(NCHAIN)]
            prb = ps("hpr", [128, 512], F32, st)
            prB = Buf()
            pR = Ring([(prb[:, k * HT:(k + 1) * HT], prB) for k in range(2)])

            for d in range(2):
                mask = mask_f if d == 0 else mask_b
                order = list(range(len(HTILES))) if d == 0 else [0] + list(range(len(HTILES) - 1, 0, -1))
                for hg in range(8 // NCHAIN):
                    heads = [hg * NCHAIN + i for i in range(NCHAIN)]
                    for ci in range(NCHAIN):
                        h = heads[ci]
                        S.op("dve", lambda e: e.memset(Sst[ci][0][:], 0.0), writes=[Sst[ci][1]])
                        S.op("dve", lambda e: e.memset(Sbf2[ci][par[ci]][0][:], 0.0), writes=[Sbf2[ci][par[ci]][1]])
                        S.dma("sp", dAc[ci][0][:], DAd[d, h * 128:(h + 1) * 128, :], reads=[DB("da", d, h)], writes=[dAc[ci][1]])
                    fins = {}

                    def loads(si):
                        ti = order[si]
                        t0, sz = HTILES[ti]
                        c0 = t0 // 32
                        pt = h2t(t0)
                        fin = {}
                        for ci in range(NCHAIN):
                            h = heads[ci]
                            Z = sets[si % 2][ci]
                            S.dma("sp", Z["qt"][0], QTd[d, h * 128:(h + 1) * 128, t0:t0 + sz], reads=[DB("qt", d, h, ti)], writes=[Z["qt"][1]])
                            S.dma("sp", Z["kt"][0], KTd[d, h * 128:(h + 1) * 128, t0:t0 + sz], reads=[DB("kt", d, h, ti)], writes=[Z["kt"][1]])
                            S.dma("sp", Z["kx"][0], KXd[d, h, c0:c0 + NCK].rearrange("c s k -> s c k"), reads=[DB("kx", d, h, ti)],
                                  writes=[Z["kx"][1]])
                            S.dma("sp", Z["vx"][0], VXd[h, c0:c0 + NCK].rearrange("c s k -> s c k"), reads=[DB("vx", h, ti)],
                                  writes=[Z["vx"][1]])
                            if d == 1:
                                ol, olB = ofl.next(); g_, gB = graw.next()
                                S.dma("sp", ol[:], OF[h * 128:(h + 1) * 128, t0:t0 + sz], reads=[DB("of", h, ti)], writes=[olB])
                                S.dma("sp", g_[:], SGd[h * 128:(h + 1) * 128, t0:t0 + sz], reads=[DB("sg", h, ti)], writes=[gB])
                                fin[ci] = dict(ol=ol, olB=olB, g=g_, gB=gB)
                        fins[si] = fin

                    pending = []

                    def readout(item):
                        h, t0, sz, pt, os_, osB, Fn = item
                        sq_, sqB = sqb.next()
                        S.op("act", lambda e: e.activation(sq_[:], os_[:], AF.Square), reads=[osB], writes=[sqB])
                        pr, prB_ = pR.next()
                        S.op("pe", mm(pr, ones_bf[:], sq_[:], True, True), reads=[sqB, cbfB], writes=[prB_])
                        r_, rB = rr.next()
                        S.op("act", lambda e: e.activation(r_[:], pr, AF.Ln, bias=EPS, scale=1.0 / 128), reads=[prB_], writes=[rB])
                        S.op("act", lambda e: e.activation(r_[:], r_[:], AF.Exp, scale=-0.5), reads=[rB], writes=[rB])
                        S.op("dve", lambda e: e.tensor_tensor(os_[:], os_[:], r_[:], ALU.mult), reads=[osB, rB], writes=[osB])
                        S.op("dve", lambda e: e.scalar_tensor_tensor(H[:, h, t0:t0 + sz], os_[:], hgw[:, l:l + 1], Fn["g"][:],
                                                                     ALU.mult, ALU.mult),
                             reads=[osB, Fn["gB"], smallB], writes=[HB[h][pt]])

                    loads(0)
                    for si, ti in enumerate(order):
                        t0, sz = HTILES[ti]
                        c0 = t0 // 32
                        pt = h2t(t0)
                        if si + 1 < len(order):
                            loads(si + 1)
                        fin = fins.pop(si)
                        Zs = sets[si % 2]
                        corder = range(NCK) if d == 0 else range(NCK - 1, -1, -1)
                        for c in corder:
                            cs = slice(c * 32, (c + 1) * 32)
                            cur = {}
                            if pending and (c % 2 == 1):
                                readout(pending.pop(0))
                            for ci in range(NCHAIN):
                                Z = Zs[ci]
                                pa_, paB = pa[ci].next()
                                S.op("pe", mm(pa_, Z["kt"][0][:, cs], Z["qt"][0][:, cs], True, True), reads=[Z["kt"][1], Z["qt"][1]], writes=[paB])
                                am, amB = attm[ci].next()
                                S.op("dve", lambda e: e.tensor_tensor(am[:], pa_, mask, ALU.mult), reads=[paB, cstB], writes=[amB])
                                cur[ci] = (am, amB)
                            for ci in range(NCHAIN):
                                Z = Zs[ci]
                                vt, vtB = Z["vx"]; kx, kxB = Z["kx"]
                                pp_, ppB = pp[ci]
                                S.op("pe", mm(pp_, kx[:, c, :], vt[:, c, :], True, True), reads=[kxB, vtB], writes=[ppB])
                            for ci in range(NCHAIN):
                                Z = Zs[ci]
                                am, amB = cur[ci]
                                po_, poB = po[ci]
                                vt, vtB = Z["vx"]
                                sb_, sbB = Sbf2[ci][par[ci]]
                                S.op("pe", mm(po_[:, cs], vt[:, c, :], am[:], True, False), reads=[vtB, amB], writes=[poB])
                                S.op("pe", mm(po_[:, cs], sb_[:], Z["qt"][0][:, cs], False, True), reads=[sbB, Z["qt"][1]], writes=[poB])
                            for ci in range(NCHAIN):
                                pp_, ppB = pp[ci]
                                S.op("dve", lambda e: e.scalar_tensor_tensor(Sst[ci][0][:], Sst[ci][0][:], dAc[ci][0][:, c0 + c:c0 + c + 1], pp_,
                                                                             ALU.mult, ALU.add),
                                     reads=[Sst[ci][1], dAc[ci][1], ppB], writes=[Sst[ci][1]])
                                par[ci] ^= 1
                                sn_, snB = Sbf2[ci][par[ci]]
                                if ci % 2 == 0:
                                    S.op("act", lambda e: e.copy(sn_[:], Sst[ci][0][:]), reads=[Sst[ci][1]], writes=[snB])
                                else:
                                    S.op("pool", lambda e: e.tensor_copy(sn_[:], Sst[ci][0][:]), reads=[Sst[ci][1]], writes=[snB])
                        for ci in range(NCHAIN):
                            h = heads[ci]
                            po_, poB = po[ci]
                            if d == 0:
                                o_, oB = ofs.next()
                                S.op("act", lambda e: e.copy(o_[:], po_), reads=[poB], writes=[oB])
                                S.dma("sp", OF[h * 128:(h + 1) * 128, t0:t0 + sz], o_[:], reads=[oB], writes=[DB("of", h, ti)])
                            else:
                                Fn = fin[ci]
                                os_, osB = osum.next()
                                S.op("dve", lambda e: e.tensor_tensor(os_[:], po_, Fn["ol"][:], ALU.add), reads=[poB, Fn["olB"]], writes=[osB])
                                pending.append((h, t0, sz, pt, os_, osB, Fn))
                    while pending:
                        readout(pending.pop(0))
            S.barrier()

    UOFF_C = 15
    UOFF_L = 15 + 256 + 15
    ULEN = 15 + 256 + 15 + 4096 + 15

    def phase_conv(l):
        with ExitStack() as st:
            Ub = [(sb("cvU%d" % i, [128, ULEN], BF16, st), Buf()) for i in range(2)]
            Dg = [(sb("cvD%d" % i, [128, 31, 128], BF16, st), Buf()) for i in range(2)]
            araw = Ring([(sb("cva%d" % i, [128, 512], F32, st), Buf()) for i in range(2)])
            braw = Ring([(sb("cvb%d" % i, [128, 512], F32, st), Buf()) for i in range(2)])
            accr = Ring([(sb("cvacc%d" % i, [128, 512], F32, st), Buf()) for i in range(2)])
            abf = Ring([(sb("cvab%d" % i, [128, 512], BF16, st), Buf()) for i in range(2)])
            dd = Ring([(sb("cvd%d" % i, [128, 512], F32, st), Buf()) for i in range(2)])
            sqb = Ring([(sb("cvs%d" % i, [128, 512], BF16, st), Buf()) for i in range(2)])
            rsd = Ring([(sb("cvr%d" % i, [128, 512], F32, st), Buf()) for i in range(2)])
            pc = Ring([(ps("cvpc%d" % i, [128, 512], F32, st), Buf()) for i in range(2)])
            pm = Ring([(ps("cvpm%d" % i, [128, 512], F32, st), Buf()) for i in range(2)])
            pv = Ring([(ps("cvpv%d" % i, [128, 512], F32, st), Buf()) for i in range(2)])
            cwl = sb("cwl", [128, 8, 31], F32, st)
            cwlB = Buf()
            S.dma("sp", cwl[:], conv_wT[:, l], writes=[cwlB])
            for i in range(2):
                S.op("dve", lambda e: e.memset(Ub[i][0][:], 0.0), writes=[Ub[i][1]])
            def diag(c2):
                Dm2, DB2 = Dg[c2 % 2]
                S.op("dve", lambda e: e.tensor_tensor(Dm2[:], ident_bf[:].unsqueeze(1).broadcast_to([128, 31, 128]),
                                                      cwl[:, c2, :].unsqueeze(2).broadcast_to([128, 31, 128]), ALU.mult),
                     reads=[cbfB, cwlB], writes=[DB2])

            def uprep(c2, ti):
                U2, UB2 = Ub[c2 % 2]
                t0, sz = TILES[ti]
                ar = 5120 + c2 * 128
                br = 6144 + c2 * 128
                a_, aB = araw.next(); b_, bB = braw.next()
                S.dma("sp", a_[:, 0:sz], PROJ[ar:ar + 128, t0:t0 + sz], reads=[DB("proj", ar // 128, ti)], writes=[aB])
                S.dma("sp", b_[:, 0:sz], PROJ[br:br + 128, t0:t0 + sz], reads=[DB("proj", br // 128, ti)], writes=[bB])
                S.op("act", lambda e: e.activation(b_[:, 0:sz], b_[:, 0:sz], AF.Sigmoid), reads=[bB], writes=[bB])
                uo = UOFF_C if ti == 0 else UOFF_L + (t0 - 256)
                S.op("dve", lambda e: e.tensor_tensor(U2[:, uo:uo + sz], a_[:, 0:sz], b_[:, 0:sz], ALU.mult), reads=[aB, bB], writes=[UB2])

            diag(0)
            for ti in range(len(TILES)):
                uprep(0, ti)
            for cc in range(8):
                U, UB = Ub[cc % 2]
                Dm, DB_ = Dg[cc % 2]
                nxt = cc + 1 < 8
                if nxt:
                    diag(cc + 1)

                def convmm(ti, hooks=()):
                    t0, sz = TILES[ti]
                    uo = UOFF_C if ti == 0 else UOFF_L + (t0 - 256)
                    p, pB = pc.next()
                    hooks = list(hooks)
                    for k in range(31):
                        S.op("pe", mm(p[:, 0:sz], Dm[:, k, :], U[:, uo + k - 15:uo + k - 15 + sz], k == 0, k == 30),
                             reads=[DB_, UB], writes=[pB], inc=(k == 30))
                        if k in (9, 19, 29) and hooks:
                            hooks.pop(0)()
                    while hooks:
                        hooks.pop(0)()
                    acc_, accB = accr.next()
                    S.op("act", lambda e: e.activation(acc_[:, 0:sz], p[:, 0:sz], AF.Identity, bias=cb[:, l, cc:cc + 1], scale=1.0),
                         reads=[pB, smallB], writes=[accB])
                    return (acc_, accB)

                def lnorm_parts(ti, A):
                    t0, sz = TILES[ti]
                    acc_, accB = A
                    acc = acc_[:, 0:sz]
                    ab_, abB = abf.next()
                    d_, dB = dd.next()
                    s_, sB = sqb.next()
                    r_, rB = rsd.next()
                    p1, p1B = pm.next()
                    p2, p2B = pv.next()
                    S.op("act", lambda e: e.copy(ab_[:, 0:sz], acc), reads=[accB], writes=[abB])

                    def part1():
                        S.op("pe", mm(p1[:, 0:sz], ones_bf[:], ab_[:, 0:sz], True, True), reads=[abB, cbfB], writes=[p1B])
                        S.op("dve", lambda e: e.scalar_tensor_tensor(d_[:, 0:sz], p1[:, 0:sz], -1.0 / 128, acc, ALU.mult, ALU.add),
                             reads=[p1B, accB], writes=[dB])
                        S.op("act", lambda e: e.activation(s_[:, 0:sz], d_[:, 0:sz], AF.Square), reads=[dB], writes=[sB])

                    def part2():
                        S.op("pe", mm(p2[:, 0:sz], ones_bf[:], s_[:, 0:sz], True, True), reads=[sB, cbfB], writes=[p2B])
                        S.op("act", lambda e: e.activation(r_[:, 0:sz], p2[:, 0:sz], AF.Ln, bias=LN_EPS, scale=1.0 / 128), reads=[p2B], writes=[rB])
                        S.op("act", lambda e: e.activation(r_[:, 0:sz], r_[:, 0:sz], AF.Exp, scale=-0.5), reads=[rB], writes=[rB])
                        S.op("dve", lambda e: e.tensor_tensor(d_[:, 0:sz], d_[:, 0:sz], r_[:, 0:sz], ALU.mult), reads=[dB, rB], writes=[dB])
                        S.op("dve", lambda e: e.tensor_scalar(d_[:, 0:sz], d_[:, 0:sz], clw[:, l, cc:cc + 1], clb[:, l, cc:cc + 1], ALU.mult, ALU.add),
                             reads=[dB, smallB], writes=[dB])
                        S.op("act", lambda e: e.activation(r_[:, 0:sz], d_[:, 0:sz], AF.Exp, scale=-1.0), reads=[dB], writes=[rB])

                    def part3():
                        S.op("dve", lambda e: e.tensor_scalar(r_[:, 0:sz], r_[:, 0:sz], 1.0, None, ALU.add), reads=[rB], writes=[rB])
                        S.op("dve", lambda e: e.reciprocal(r_[:, 0:sz], r_[:, 0:sz]), reads=[rB], writes=[rB])
                        S.op("dve", lambda e: e.tensor_tensor(H[:, 8 + cc, t0:t0 + sz], d_[:, 0:sz], r_[:, 0:sz], ALU.mult),
                             reads=[dB, rB], writes=[HB[8 + cc][ti]])
                    return [part1, part2, part3]

                prevA = convmm(0, [lambda: uprep(cc + 1, 0)] if nxt else [])
                for ti in range(1, len(TILES)):
                    hooks = lnorm_parts(ti - 1, prevA)
                    if nxt:
                        hooks.append(lambda ti=ti: uprep(cc + 1, ti))
                    prevA = convmm(ti, hooks)
                for part in lnorm_parts(len(TILES) - 1, prevA):
                    part()
            S.barrier()

    GOFF_C = 64
    GOFF_L = 64 + 256 + 64
    GLEN = 64 + 256 + 64 + 4096 + 128

    def phase_ffn_up(l):
        with ExitStack() as st:
            wg = [sb("fwg%d" % i, [128, KC, 256], BF16, st) for i in range(2)]
            wv = [sb("fwv%d" % i, [128, KC, 256], BF16, st) for i in range(2)]
            wgB = [Buf(), Buf()]; wvB = [Buf(), Buf()]
            G = [sb("fG%d" % i, [128, GLEN], F32, st) for i in range(1)]
            GB = [Buf()]
            gc = Ring([(sb("fgc%d" % i, [128, 512], F32, st), Buf()) for i in range(2)])
            gl = Ring([(sb("fgl%d" % i, [128, 512], F32, st), Buf()) for i in range(2)])
            ao = Ring([(sb("fao%d" % i, [128, 512], BF16, st), Buf()) for i in range(2)])
            pg = Ring([(ps("fpg%d" % i, [128, 512], F32, st), Buf()) for i in range(3)])
            pv = Ring([(ps("fpv%d" % i, [128, 512], F32, st), Buf()) for i in range(3)])
            fcwl = sb("fcwl", [128, FC, 9], F32, st)
            fcwB = Buf()
            S.dma("sp", fcwl[:], ffn_cwT[:, l], writes=[fcwB])
            GBt = [Buf() for _ in TILES]
            S.op("dve", lambda e: e.memset(G[0][:], 0.0), writes=GBt)
            g_ = G[0]
            for j in range(FC):
                bi = (j // 2) % 2
                wsub = slice((j % 2) * 128, (j % 2 + 1) * 128)
                if j % 2 == 0:
                    S.dma("pool", wg[bi][:], ffn_up[l][:, j * 128:(j + 2) * 128].rearrange("(kc p) n -> p kc n", p=128), writes=[wgB[bi]])
                    S.dma("pool", wv[bi][:], ffn_up[l][:, DFF + j * 128:DFF + (j + 2) * 128].rearrange("(kc p) n -> p kc n", p=128),
                          writes=[wvB[bi]])

                def gate(ti):
                    t0, sz = TILES[ti]
                    p, pB = pg.next()
                    for kc in range(KC):
                        S.op("pe", mm(p[:, 0:sz], wg[bi][:, kc, wsub], H[:, kc, t0:t0 + sz], kc == 0, kc == KC - 1),
                             reads=[wgB[bi], HB[kc][ti]], writes=[pB], inc=(kc == KC - 1))
                    go = GOFF_C if ti == 0 else GOFF_L + (t0 - 256)
                    S.op("act", lambda e: e.copy(g_[:, go:go + sz], p[:, 0:sz]), reads=[pB], writes=[GBt[ti]])

                def rest(ti):
                    t0, sz = TILES[ti]
                    c_, cB = gc.next()
                    go = GOFF_C if ti == 0 else GOFF_L + (t0 - 256)
                    gr = [GBt[x] for x in (ti - 1, ti, ti + 1) if 1 <= x <= 8] if ti > 0 else [GBt[0]]

                    def wt(k):
                        return fcwl[:, j, k:k + 1]
                    S.op("dve", lambda e: e.tensor_scalar(c_[:, 0:sz], g_[:, go:go + sz], wt(4), fcb[:, l, j:j + 1], ALU.mult, ALU.add),
                         reads=gr + [smallB, fcwB], writes=[cB])
                    if ti == 0:
                        for k, sh in ((3, -1), (5, 1)):
                            S.op("dve", lambda e: e.scalar_tensor_tensor(c_[:, 0:sz], g_[:, go + sh:go + sh + sz], wt(k), c_[:, 0:sz],
                                                                         ALU.mult, ALU.add), reads=gr + [smallB, fcwB, cB], writes=[cB])
                    else:
                        c3 = c_[:, 0:sz].rearrange("p (r w) -> p r w", w=64)
                        for dr in range(3):
                            for dw in range(3):
                                if dr == 1 and dw == 1:
                                    continue
                                wlo = 1 if dw == 0 else 0
                                whi = 63 if dw == 2 else 64
                                src = g_[:, go + (dr - 1) * 64 + (dw - 1) + wlo: go + (dr - 1) * 64 + (dw - 1) + wlo + sz]
                                src3 = src.rearrange("p (r w) -> p r w", w=64)[:, :, 0:whi - wlo]
                                S.op("dve", lambda e: e.scalar_tensor_tensor(c3[:, :, wlo:whi], src3, wt(dr * 3 + dw), c3[:, :, wlo:whi],
                                                                             ALU.mult, ALU.add), reads=gr + [smallB, fcwB, cB], writes=[cB],
                                     chain=True)
                    l_, lB = gl.next()
                    S.op("act", lambda e: e.activation(l_[:, 0:sz], c_[:, 0:sz], AF.Gelu), reads=[cB], writes=[lB])
                    p, pB = pv.next()
                    for kc in range(KC):
                        S.op("pe", mm(p[:, 0:sz], wv[bi][:, kc, wsub], H[:, kc, t0:t0 + sz], kc == 0, kc == KC - 1),
                             reads=[wvB[bi], HB[kc][ti]], writes=[pB], inc=(kc == KC - 1))
                    a_, aB = ao.next()
                    S.op("dve", lambda e: e.tensor_tensor(a_[:, 0:sz], p[:, 0:sz], l_[:, 0:sz], ALU.mult), reads=[pB, lB], writes=[aB])
                    S.dma("sp", ACTD[j * 128:(j + 1) * 128, t0:t0 + sz], a_[:, 0:sz], reads=[aB], writes=[DB("act", j, ti)])

                gate(0)
                gate(1)
                rest(0)
                for ti in range(2, 9):
                    gate(ti)
                    rest(ti - 1)
                rest(8)
            S.barrier()

    SUPER = [(0, 1280, [0, 1, 2]), (1280, 1024, [3, 4]), (2304, 1024, [5, 6]), (3328, 1024, [7, 8])]

    def phase_ffn_down(l):
        with ExitStack() as st:
            Htn = H
            pstride = H[:].ap[0][0]
            wd = [sb("fdw%d" % i, [128, FC, 256], BF16, st) for i in range(2)]
            wdB = [Buf(), Buf()]
            pp = Ring([(ps("fdp%d" % i, [128, 512], F32, st), Buf()) for i in range(4)])
            stg = Ring([(sb("fds%d" % i, [128, 512], F32, st), Buf()) for i in range(4)])
            AbT = {(j, ti): Buf() for j in range(FC) for ti in range(len(TILES))}
            ev = 0
            wi = 0
            for (s0, slen, tis) in SUPER:
                Ab = bass.AP(Htn, 0, [[pstride, 128], [slen, FC], [1, slen]])
                if S.cnt["pe"] > 0:
                    S._wait("sp", [("pe", S.cnt["pe"])])
                for ti in tis:
                    t0, sz = TILES[ti]
                    for j in range(FC):
                        S.dma("sp", Ab[:, j, t0 - s0:t0 - s0 + sz], ACTD[j * 128:(j + 1) * 128, t0:t0 + sz],
                              reads=[DB("act", j, ti)], writes=[AbT[(j, ti)]])
                for oc in range(KC):
                    if oc % 2 == 0:
                        w, wB = wd[wi % 2], wdB[wi % 2]
                        wi += 1
                        S.dma("pool", w[:], ffn_down[l][:, oc * 128:(oc + 2) * 128].rearrange("(kc p) n -> p kc n", p=128), writes=[wB])
                    osub = slice((oc % 2) * 128, (oc % 2 + 1) * 128)
                    for ti in tis:
                        t0, sz = TILES[ti]
                        p, pB = pp.next()
                        for j in range(FC):
                            S.op("pe", mm(p[:, 0:sz], w[:, j, osub], Ab[:, j, t0 - s0:t0 - s0 + sz], j == 0, j == FC - 1),
                                 reads=[wB, AbT[(j, ti)]], writes=[pB], inc=(j == FC - 1))
                        g, gB = stg.next()
                        if ev % 2 == 0:
                            S.op("act", lambda e: e.copy(g[:, 0:sz], p[:, 0:sz]), reads=[pB], writes=[gB])
                        else:
                            S.op("dve", lambda e: e.tensor_copy(g[:, 0:sz], p[:, 0:sz]), reads=[pB], writes=[gB])
                        ev += 1
                        S.dma("sp", Y[oc * 128:(oc + 1) * 128, t0:t0 + sz], g[:, 0:sz], reads=[gB], writes=[DB("y2", oc, ti)])
            S.barrier()

    phases = []
    for l in range(DEPTH):
        phases.append(("mod%d" % l, lambda l=l: phase_mod(l)))
    for l in range(DEPTH):
        xsrc = xT if l == 0 else XB
        if l == 0:
            phases.append(("norm1_%d" % l, lambda l=l, xsrc=xsrc: phase_resnorm(l, xsrc, "x%d" % l, None, None, None, None, None, 0, False)))
        phases.append(("win%d" % l, lambda l=l: phase_dense(w_in[l], INC, PROJ, "proj")))
        phases.append(("hgpre%d" % l, lambda l=l: phase_hgrn_pre(l)))
        phases.append(("hgrn%d" % l, lambda l=l: phase_hgrn_scan(l)))
        phases.append(("conv%d" % l, lambda l=l: phase_conv(l)))
        phases.append(("wout%d" % l, lambda l=l: phase_dense(w_out[l], D, Y, "y")))
        phases.append(("res1_%d" % l, lambda l=l, xsrc=xsrc: phase_resnorm(l, xsrc, "xa", Y, "ya", 2, X1, "x1", 3, False)))
        phases.append(("ffnup%d" % l, lambda l=l: phase_ffn_up(l)))
        phases.append(("ffndn%d" % l, lambda l=l: phase_ffn_down(l)))
        if l == 0:
            phases.append(("res2_%d" % l, lambda l=l: phase_resnorm(l, X1, "x1b", Y, "y2b", 5, XB, "xb", 0, False, la=1)))
        else:
            phases.append(("res2_%d" % l, lambda l=l: phase_resnorm(l, X1, "x1b", Y, "y2b", 5, None, None, None, True)))
    S.barrier()
    for name, fn in phases:
        fn()
        if stop_after is not None and name == stop_after:
            if "HD" in debug:
                HD = dram_scr("HD", [D, NT], BF16)
                for kc in range(KC):
                    S.dma("sp", HD[kc * 128:(kc + 1) * 128, :], H[:, kc, :], reads=[HB[kc][ti] for ti in range(9)], writes=[DB("hd", kc)])
            break
    S.barrier()
    es.close()
    return nc, S


def _pc(a, inner):
    a = np.asarray(a, np.float32)
    sh = a.shape
    a = a.reshape(sh[:-1] + (sh[-1] // 128, 128))
    return np.ascontiguousarray(np.moveaxis(a, -1, 0))


def make_consts():
    c = np.zeros((128, 832), np.float32)
    c[:, 0:128] = np.eye(128, dtype=np.float32)
    c[:, 128:256] = 1.0
    s = np.arange(32)[:, None]
    t = np.arange(32)[None, :]
    c[0:32, 256:288] = (s <= t)
    c[0:32, 288:320] = (s >= t)
    r = np.ones(512, np.float32)
    r[::32] = 0.0
    c[:, 320:832] = r[None, :]
    return c


def make_in_maps(inp):
    x = np.asarray(inp["x"], np.float32)
    ctx = np.asarray(inp["ctx"], np.float32)
    c = np.asarray(inp["c"], np.float32)
    c_ctx = np.asarray(inp["c_ctx"], np.float32)
    shared = {
        "w_mod": np.ascontiguousarray(inp["w_mod"], np.float32),
        "b_modT": _pc(inp["b_mod"], 96),
        "norm_wT": _pc(inp["norm_w"], KC),
        "w_in": np.ascontiguousarray(inp["w_in"], np.float32),
        "lbT": _pc(inp["lb_logits"], 8),
        "hgwT": np.ascontiguousarray(np.asarray(inp["hg_norm_w"], np.float32).T),
        "conv_wT": np.ascontiguousarray(np.moveaxis(_pc(inp["conv_w"], 8), 2, 3)),
        "conv_bT": _pc(inp["conv_b"], 8),
        "conv_lnwT": _pc(inp["conv_ln_w"], 8),
        "conv_lnbT": _pc(inp["conv_ln_b"], 8),
        "w_out": np.ascontiguousarray(inp["w_out"], np.float32),
        "ffn_up": np.ascontiguousarray(inp["ffn_up"], np.float32),
        "ffn_cwT": np.ascontiguousarray(np.moveaxis(_pc(np.asarray(inp["ffn_conv_w"], np.float32).reshape(2, 9, DFF), FC), 2, 3)),
        "ffn_cbT": _pc(inp["ffn_conv_b"], FC),
        "ffn_down": np.ascontiguousarray(inp["ffn_down"], np.float32),
        "consts": make_consts(),
    }
    maps = []
    for b in range(2):
        m = dict(shared)
        m["xT"] = np.ascontiguousarray(np.concatenate([ctx[b], x[b]], axis=0).T)
        cc = np.stack([c[b], c_ctx], axis=-1)
        m["cT"] = np.ascontiguousarray(np.moveaxis(cc.reshape(KC, 128, 2), 1, 0))
        maps.append(m)
    return maps


_CACHE = {}


def kernel(**inputs):
    if "nc" not in _CACHE:
        _CACHE["nc"] = build_program()[0]
    nc = _CACHE["nc"]
    maps = make_in_maps(inputs)
    res = run_bass_kernel_spmd(nc, maps, core_ids=[0, 1])
    out = np.stack([np.ascontiguousarray(res.results[b]["yT"].T) for b in range(2)], axis=0)
    return out.astype(np.float32)
```

```python
import numpy as np
from contextlib import ExitStack
import concourse.bass as bass
import concourse.mybir as mybir
from concourse.bass_utils import run_bass_kernel_spmd

F32 = mybir.dt.float32
BF16 = mybir.dt.bfloat16
ALU = mybir.AluOpType
AF = mybir.ActivationFunctionType

D = 2048
KC = 16
NCTX = 256
NLAT = 4096
NT = NCTX + NLAT
DFF = 5632
FC = 44
INC = 7168
DEPTH = 2
EPS = 1e-6
LN_EPS = 1e-5
TILES = [(0, 256)] + [(256 + 512 * i, 512) for i in range(8)]
NDS = 48


class Buf:
    __slots__ = ("w", "r")

    def __init__(self):
        self.w = None
        self.r = {}


class Sched:
    def __init__(self, nc, es):
        self.nc = nc
        self.eng = {"pe": nc.tensor, "act": nc.scalar, "dve": nc.vector, "pool": nc.gpsimd, "sp": nc.sync}
        self.semobj = {}
        for k in self.eng:
            self.semobj[k] = es.enter_context(nc.semaphore("s_" + k))
        for i in range(NDS):
            self.semobj[("d", i)] = es.enter_context(nc.semaphore("d%d" % i))
        self.cnt = {k: 0 for k in self.eng}
        self.seen = {k: {} for k in self.eng}
        self.dval = [0] * NDS
        self.dnext = 0
        self.ninst = 0
        self.log = {k: [] for k in self.eng}

    def _wait(self, e, deps):
        need = {}
        for d in deps:
            if d is None:
                continue
            k, v = d
            if k == e and v > self.cnt[e]:
                continue
            if v > need.get(k, 0):
                need[k] = v
        for k, v in need.items():
            if self.seen[e].get(k, 0) >= v:
                continue
            self.eng[e].wait_ge(self.semobj[k], v)
            self.log[e].append(("w", k, v))
            self.seen[e][k] = v

    def _deps(self, e, reads, writes, chain=False):
        deps = []
        for b in reads:
            if chain and b.w is not None and b.w[0] == e:
                continue
            deps.append(b.w)
        for b in writes:
            if b.w is not None and b.w[0] != e:
                deps.append(b.w)
            for k, v in b.r.items():
                if k != e:
                    deps.append((k, v))
        return deps

    def op(self, e, fn, reads=(), writes=(), inc=True, chain=False):
        self._wait(e, self._deps(e, reads, writes, chain))
        inst = fn(self.eng[e])
        ev = (e, self.cnt[e] + 1)
        if inc:
            inst.then_inc(self.semobj[e], 1)
            self.cnt[e] += 1
            self.log[e].append(("i", e, 1))
        else:
            self.log[e].append(("n", e, 0))
        for b in reads:
            if b.r.get(e, 0) < ev[1]:
                b.r[e] = ev[1]
        for b in writes:
            b.w = ev
            b.r = {}
        self.ninst += 1
        return inst

    def dma(self, q, out, in_, reads=(), writes=()):
        deps = self._deps(q, reads, writes)
        i = self.dnext
        self.dnext = (i + 1) % NDS
        key = ("d", i)
        if self.dval[i] > 0:
            deps.append((key, self.dval[i]))
        self._wait(q, deps)
        inst = self.eng[q].dma_start(out=out, in_=in_)
        self.dval[i] += 16
        inst.then_inc(self.semobj[key], 16)
        self.log[q].append(("i", key, 16))
        ev = (key, self.dval[i])
        for b in reads:
            b.r[key] = ev[1]
        for b in writes:
            b.w = ev
            b.r = {}
        self.ninst += 1
        return ev

    def barrier(self):
        evs = [(k, v) for k, v in self.cnt.items() if v > 0]
        evs += [(("d", i), v) for i, v in enumerate(self.dval) if v > 0]
        for e in self.eng:
            self._wait(e, [x for x in evs if x[0] != e])


class Ring:
    def __init__(self, items):
        self.items = items
        self.i = 0

    def next(self):
        it = self.items[self.i]
        self.i = (self.i + 1) % len(self.items)
        return it


def build_program(stop_after=None, debug=False):
    nc = bass.Bass("TRN2", target_bir_lowering=False)
    es = ExitStack()
    S = Sched(nc, es)

    def dram_in(name, shape, dt=F32):
        return nc.dram_tensor(name, list(shape), dt, kind="ExternalInput").ap()

    debug = set(debug or ())

    def dram_scr(name, shape, dt=F32):
        return nc.dram_tensor(name, list(shape), dt, kind=("ExternalOutput" if name in debug else "Internal")).ap()

    xT = dram_in("xT", [D, NT])
    cT = dram_in("cT", [128, KC, 2])
    w_mod = dram_in("w_mod", [DEPTH, D, 6 * D])
    b_modT = dram_in("b_modT", [128, DEPTH, 96])
    norm_wT = dram_in("norm_wT", [128, DEPTH, 4, KC])
    w_in = dram_in("w_in", [DEPTH, D, INC])
    lbT = dram_in("lbT", [128, 2, DEPTH, 8])
    hgwT = dram_in("hgwT", [128, DEPTH])
    conv_wT = dram_in("conv_wT", [128, DEPTH, 8, 31])
    conv_bT = dram_in("conv_bT", [128, DEPTH, 8])
    conv_lnwT = dram_in("conv_lnwT", [128, DEPTH, 8])
    conv_lnbT = dram_in("conv_lnbT", [128, DEPTH, 8])
    w_out = dram_in("w_out", [DEPTH, D, D])
    ffn_up = dram_in("ffn_up", [DEPTH, D, 2 * DFF])
    ffn_cwT = dram_in("ffn_cwT", [128, DEPTH, FC, 9])
    ffn_cbT = dram_in("ffn_cbT", [128, DEPTH, FC])
    ffn_down = dram_in("ffn_down", [DEPTH, DFF, D])
    consts = dram_in("consts", [128, 832])
    yT = nc.dram_tensor("yT", [D, NLAT], F32, kind="ExternalOutput").ap()

    PROJ = dram_scr("PROJ", [INC, NT])
    OF = dram_scr("OF", [1024, NT])
    Y = dram_scr("Y", [D, NT])
    X1 = dram_scr("X1", [D, NT])
    XB = dram_scr("XB", [D, NT])
    ACTD = dram_scr("ACTD", [DFF, NT], BF16)
    dbufs = {}

    def DB(*key):
        b = dbufs.get(key)
        if b is None:
            b = dbufs[key] = Buf()
        return b

    uid = [0]

    def sb(name, shape, dt, stack=None):
        uid[0] += 1
        return (stack or es).enter_context(nc.sbuf_tensor("%s_%d" % (name, uid[0]), list(shape), dt))

    def ps(name, shape, dt, stack):
        uid[0] += 1
        return stack.enter_context(nc.psum_tensor("%s_%d" % (name, uid[0]), list(shape), dt))

    H = sb("H", [128, KC, NT], BF16)
    HB = [[Buf() for _ in TILES] for _ in range(KC)]
    cst = sb("cst", [128, 832], F32)
    cstB = Buf()
    ident_bf = sb("ident_bf", [128, 128], BF16)
    ones_bf = sb("ones_bf", [128, 128], BF16)
    cbfB = Buf()
    modv = sb("modv", [128, DEPTH, 96, 2], F32)
    modB = Buf()
    vecs = sb("vecs", [128, DEPTH, 6, KC, 2], F32)
    vecB = Buf()
    nw = sb("nw", [128, DEPTH, 4, KC], F32)
    smallB = Buf()
    bmod = sb("bmod", [128, DEPTH, 96], F32)
    lbl = sb("lbl", [128, 2, DEPTH, 8], F32)
    lbv = sb("lbv", [128, 3, 2, DEPTH, 8], F32)
    hgw = sb("hgw", [128, DEPTH], F32)
    cb = sb("cb", [128, DEPTH, 8], F32)
    clw = sb("clw", [128, DEPTH, 8], F32)
    clb = sb("clb", [128, DEPTH, 8], F32)
    fcb = sb("fcb", [128, DEPTH, FC], F32)
    ctile = sb("ctile", [128, KC, 2], F32)
    sc_bf = sb("sc_bf", [128, KC, 2], BF16)

    mask_f = cst[0:32, 256:288]
    mask_b = cst[0:32, 288:320]
    rmask = cst[:, 320:832]

    S.dma("sp", cst[:], consts, writes=[cstB])
    for dst, src in ((nw, norm_wT), (bmod, b_modT), (lbl, lbT), (hgw, hgwT), (cb, conv_bT),
                     (clw, conv_lnwT), (clb, conv_lnbT), (fcb, ffn_cbT), (ctile, cT)):
        S.dma("sp", dst[:], src, writes=[smallB])
    S.op("dve", lambda e: e.tensor_copy(ident_bf[:], cst[:, 0:128]), reads=[cstB], writes=[cbfB])
    S.op("dve", lambda e: e.tensor_copy(ones_bf[:], cst[:, 128:256]), reads=[cstB], writes=[cbfB])
    S.op("dve", lambda e: e.memset(lbv[:, 0, :, 0, :], 0.0), writes=[smallB])
    S.op("dve", lambda e: e.tensor_tensor(lbv[:, 0, :, 1, :], lbl[:, :, 1, :], lbl[:, :, 0, :], ALU.subtract),
         reads=[smallB], writes=[smallB])
    S.op("act", lambda e: e.activation(lbv[:, 0, :, 1, :], lbv[:, 0, :, 1, :], AF.Sigmoid), reads=[smallB], writes=[smallB])
    S.op("dve", lambda e: e.tensor_scalar(lbv[:, 1], lbv[:, 0], -1.0, 1.0, ALU.mult, ALU.add), reads=[smallB], writes=[smallB])
    S.op("dve", lambda e: e.tensor_scalar(lbv[:, 2], lbv[:, 0], -1.0, None, ALU.add), reads=[smallB], writes=[smallB])
    S.op("act", lambda e: e.activation(sc_bf[:], ctile[:], AF.Silu), reads=[smallB], writes=[smallB])

    def mm(out, lhsT, rhs, start, stop):
        return lambda e: e.matmul(out, lhsT, rhs, start=start, stop=stop)

    def phase_mod(l):
        with ExitStack() as st:
            wb = [sb("modw%d" % i, [128, KC, 512], BF16, st) for i in range(2)]
            wbB = [Buf(), Buf()]
            pm = ps("pm", [128, 512], F32, st)
            pmB = Buf()
            for blk in range(24):
                w, wB = wb[blk % 2], wbB[blk % 2]
                S.dma("pool", w[:], w_mod[l][:, blk * 512:(blk + 1) * 512].rearrange("(kc p) n -> p kc n", p=128),
                      writes=[wB])
                for sub in range(4):
                    j = blk * 4 + sub
                    for kc in range(KC):
                        S.op("pe", mm(pm[:, 2 * j:2 * j + 2], w[:, kc, sub * 128:(sub + 1) * 128], sc_bf[:, kc, :],
                                      kc == 0, kc == KC - 1), reads=[wB, smallB], writes=[pmB])
            S.op("dve", lambda e: e.tensor_tensor(modv[:, l], pm[:, 0:192].rearrange("p (j s) -> p j s", s=2),
                                                  bmod[:, l, :].unsqueeze(2).broadcast_to([128, 96, 2]), ALU.add),
                 reads=[pmB, smallB], writes=[modB])
            mv = modv[:, l]

            def nwb(i):
                return nw[:, l, i, :].unsqueeze(2).broadcast_to([128, KC, 2])
            S.op("dve", lambda e: e.scalar_tensor_tensor(vecs[:, l, 0], mv[:, 16:32, :], 1.0, nwb(0), ALU.add, ALU.mult),
                 reads=[modB, smallB], writes=[vecB])
            S.op("dve", lambda e: e.tensor_copy(vecs[:, l, 1], mv[:, 0:16, :]), reads=[modB], writes=[vecB])
            S.op("dve", lambda e: e.tensor_tensor(vecs[:, l, 2], mv[:, 32:48, :], nwb(1), ALU.mult), reads=[modB, smallB], writes=[vecB])
            S.op("dve", lambda e: e.scalar_tensor_tensor(vecs[:, l, 3], mv[:, 64:80, :], 1.0, nwb(2), ALU.add, ALU.mult),
                 reads=[modB, smallB], writes=[vecB])
            S.op("dve", lambda e: e.tensor_copy(vecs[:, l, 4], mv[:, 48:64, :]), reads=[modB], writes=[vecB])
            S.op("dve", lambda e: e.tensor_tensor(vecs[:, l, 5], mv[:, 80:96, :], nwb(3), ALU.mult), reads=[modB, smallB], writes=[vecB])
            S.barrier()

    RT = 128
    NRT = NT // RT

    def phase_resnorm(l, x_src, xkey, y_src, ykey, gi, x_dst, dkey, ai, final, la=None):
        with ExitStack() as st:
            xt = [(sb("rn_x%d" % i, [128, KC, RT], F32, st), Buf()) for i in range(2)]
            yt = [(sb("rn_y%d" % i, [128, KC, RT], F32, st), Buf()) for i in range(2)]
            sq = Ring([(sb("rn_sq%d" % i, [128, KC, RT], BF16, st), Buf()) for i in range(2)])
            tmp = Ring([(sb("rn_t%d" % i, [128, KC, RT], F32, st), Buf()) for i in range(1)])
            rs = [(sb("rn_rs%d" % i, [128, RT], F32, st), Buf()) for i in range(2)]
            rs2 = [(sb("rn_rt%d" % i, [128, RT], F32, st), Buf()) for i in range(2)]
            pss = [(ps("rn_ps%d" % i, [128, 512], F32, st), Buf()) for i in range(2)]
            pss2 = [(ps("rn_pt%d" % i, [128, 512], F32, st), Buf()) for i in range(2)]

            def view(dram, t0):
                return dram[:, t0:t0 + RT].rearrange("(kc p) t -> p kc t", p=128)

            def bct(ap):
                return ap.unsqueeze(1).broadcast_to([128, KC, RT])

            la = l if la is None else la

            def bcv(i, s, ll=l):
                return vecs[:, ll, i, :, s:s + 1].broadcast_to([128, KC, RT])

            def rstd(src, srcB, pst, out):
                p, pB = pst
                o, oB = out
                q, qB = sq.next()
                S.op("act", lambda e: e.activation(q[:], src[:], AF.Square), reads=[srcB], writes=[qB])
                for kc in range(KC):
                    S.op("pe", mm(p[:, 0:RT], ones_bf[:], q[:, kc, :], kc == 0, kc == KC - 1), reads=[qB, cbfB], writes=[pB],
                         inc=(kc == KC - 1))
                S.op("act", lambda e: e.activation(o[:], p[:, 0:RT], AF.Ln, bias=EPS, scale=1.0 / D), reads=[pB], writes=[oB])
                S.op("act", lambda e: e.activation(o[:], o[:], AF.Exp, scale=-0.5), reads=[oB], writes=[oB])

            def rloads(ti):
                t0 = ti * RT
                x, xB = xt[ti % 2]
                S.dma("sp", x[:], view(x_src, t0), reads=[DB(xkey, ti)], writes=[xB])
                if y_src is not None:
                    y, yB = yt[ti % 2]
                    S.dma("sp", y[:], view(y_src, t0), reads=[DB(ykey, ti)], writes=[yB])

            rloads(0)
            if y_src is not None:
                rstd(yt[0][0], yt[0][1], pss[0], rs[0])
            for ti in range(NRT):
                t0 = ti * RT
                s = 1 if t0 < NCTX else 0
                bi = ti % 2
                x, xB = xt[bi]
                if ti + 1 < NRT:
                    rloads(ti + 1)
                if y_src is not None:
                    y, yB = yt[bi]
                    r, rB = rs[bi]
                    t, tB = tmp.next()
                    S.op("dve", lambda e: e.tensor_tensor(t[:], y[:], bct(r[:]), ALU.mult), reads=[yB, rB], writes=[tB])
                    S.op("dve", lambda e: e.tensor_tensor(t[:], t[:], bcv(gi, s), ALU.mult), reads=[tB, vecB], writes=[tB], chain=True)
                    S.op("dve", lambda e: e.tensor_tensor(x[:], x[:], t[:], ALU.add), reads=[tB, xB], writes=[xB], chain=True)
                    if x_dst is not None:
                        S.dma("sp", view(x_dst, t0), x[:], reads=[xB], writes=[DB(dkey, ti)])
                    if final and t0 >= NCTX:
                        S.dma("sp", view(yT, t0 - NCTX), x[:], reads=[xB], writes=[DB("yT", ti)])
                if ai is not None:
                    rstd(x, xB, pss2[bi], rs2[bi])
                if y_src is not None and ti + 1 < NRT:
                    nb = (ti + 1) % 2
                    rstd(yt[nb][0], yt[nb][1], pss[nb], rs[nb])
                if ai is not None:
                    r2, r2B = rs2[bi]
                    hti = 0 if t0 < NCTX else 1 + (t0 - 256) // 512
                    t, tB = tmp.next()
                    S.op("dve", lambda e: e.tensor_tensor(t[:], x[:], bct(r2[:]), ALU.mult), reads=[xB, r2B], writes=[tB], chain=True)
                    S.op("dve", lambda e: e.tensor_tensor(t[:], t[:], bcv(ai, s, la), ALU.mult), reads=[tB, vecB], writes=[tB], chain=True)
                    S.op("dve", lambda e: e.tensor_tensor(H[:, :, t0:t0 + RT], t[:], bcv(ai + 1, s, la), ALU.add), reads=[tB, vecB],
                         writes=[HB[kc][hti] for kc in range(KC)], chain=True)
            S.barrier()

    def phase_dense(wsrc, ncols, out_dram, okey, nkc=KC):
        with ExitStack() as st:
            wb = [sb("dw%d" % i, [128, nkc, 512], BF16, st) for i in range(2)]
            wbB = [Buf(), Buf()]
            pp = Ring([(ps("dps%d" % i, [128, 512], F32, st), Buf()) for i in range(4)])
            stg = Ring([(sb("dstg%d" % i, [128, 512], F32, st), Buf()) for i in range(4)])
            nblk = ncols // 512
            ev = 0
            for blk in range(nblk):
                w, wB = wb[blk % 2], wbB[blk % 2]
                S.dma("pool", w[:], wsrc[:, blk * 512:(blk + 1) * 512].rearrange("(kc p) n -> p kc n", p=128), writes=[wB])
                for sub in range(4):
                    cc = blk * 4 + sub
                    for ti, (t0, sz) in enumerate(TILES):
                        p, pB = pp.next()
                        for kc in range(nkc):
                            S.op("pe", mm(p[:, 0:sz], w[:, kc, sub * 128:(sub + 1) * 128], H[:, kc, t0:t0 + sz], kc == 0, kc == nkc - 1),
                                 reads=[wB, HB[kc][ti]], writes=[pB], inc=(kc == nkc - 1))
                        g, gB = stg.next()
                        if ev % 2 == 0:
                            S.op("act", lambda e: e.copy(g[:, 0:sz], p[:, 0:sz]), reads=[pB], writes=[gB])
                        else:
                            S.op("dve", lambda e: e.tensor_copy(g[:, 0:sz], p[:, 0:sz]), reads=[pB], writes=[gB])
                        ev += 1
                        S.dma("sp", out_dram[cc * 128:(cc + 1) * 128, t0:t0 + sz], g[:, 0:sz], reads=[gB], writes=[DB(okey, cc, ti)])
            S.barrier()

    HT = 256
    HTILES = [(0, 256)] + [(256 + HT * i, HT) for i in range(NLAT // HT)]
    NCHAIN = 4
    NCK = HT // 32
    NCHK = NT // 32
    HPS = KC * NT
    QTd = dram_scr("QTd", [2, 1024, NT], BF16)
    KTd = dram_scr("KTd", [2, 1024, NT], BF16)
    KXd = dram_scr("KXd", [2, 8, NCHK, 32, 128], BF16)
    VXd = dram_scr("VXd", [8, NCHK, 32, 128], BF16)
    DAd = dram_scr("DAd", [2, 1024, NCHK], F32)
    SGd = dram_scr("SGd", [1024, NT], F32)

    class HAlias:
        def __init__(self):
            self.off = 8 * NT

        def cm(self):
            ap = bass.AP(H, self.off, [[HPS, 128], [1, HT]])
            self.off += HT
            return ap

        def tm(self):
            ap = bass.AP(H, self.off, [[HPS, 32], [128, NCK], [1, 128]])
            self.off += NCK * 128
            return ap

    def h2t(t0):
        return 0 if t0 < NCTX else 1 + (t0 - NCTX) // 512

    def c3(ap):
        return ap.rearrange("p (c j) -> p c j", j=32)

    def phase_hgrn_pre(l):
        with ExitStack() as st:
            ha = HAlias()

            def r32(name, n):
                return Ring([(sb("%s%d" % (name, i), [128, HT], F32, st), Buf()) for i in range(n)])
            raw = r32("praw", 20)
            sgr = r32("psg", 4)
            logf = r32("plog", 4); kk = r32("pkk", 4); bc = r32("pbc", 4); bc2 = r32("pbc2", 2); d3 = r32("pd3", 4)
            ex = r32("pex", 6)
            qts = Ring([(ha.cm(), Buf()) for _ in range(6)])
            kts = Ring([(ha.cm(), Buf()) for _ in range(6)])
            khs = Ring([(ha.cm(), Buf()) for _ in range(6)])
            vbs = Ring([(ha.cm(), Buf()) for _ in range(3)])
            kxs = Ring([(ha.tm(), Buf()) for _ in range(4)])
            vxs = Ring([(ha.tm(), Buf()) for _ in range(3)])
            dAall = [(sb("pdA%d" % i, [128, NCHK], F32, st), Buf()) for i in range(4)]
            ptr = Ring([(ps("pptr%d" % i, [128, 1024], BF16, st), Buf()) for i in range(4)])

            def to_tokmajor(src, srcB, dst, dstB):
                p, pB = ptr.next()
                for c in range(NCK):
                    S.op("pe", lambda e: e.transpose(p[0:32, c * 128:(c + 1) * 128], src[:, c * 32:(c + 1) * 32], ident_bf[:]),
                         reads=[srcB, cbfB], writes=[pB], inc=(c == NCK - 1))
                S.op("dve", lambda e: e.tensor_copy(dst, p[0:32, 0:NCK * 128].rearrange("p (c v) -> p c v", v=128)), reads=[pB], writes=[dstB])

            def loadsP(h, ti):
                t0, sz = HTILES[ti]
                pt = h2t(t0)
                q_, qB = raw.next(); v_, vB = raw.next()
                S.dma("sp", q_[:], PROJ[h * 128:(h + 1) * 128, t0:t0 + sz], reads=[DB("proj", h, pt)], writes=[qB])
                vr = 3072 + h * 128
                S.dma("sp", v_[:], PROJ[vr:vr + 128, t0:t0 + sz], reads=[DB("proj", vr // 128, pt)], writes=[vB])
                F = []
                for d in range(2):
                    f_, fB = raw.next()
                    fr = 1024 * (1 + d) + h * 128
                    S.dma("sp", f_[:], PROJ[fr:fr + 128, t0:t0 + sz], reads=[DB("proj", fr // 128, pt)], writes=[fB])
                    F.append((f_, fB))
                g_, gB = raw.next()
                gr = 4096 + h * 128
                S.dma("sp", g_[:], PROJ[gr:gr + 128, t0:t0 + sz], reads=[DB("proj", gr // 128, pt)], writes=[gB])
                return (q_, qB, v_, vB, F, g_, gB)

            def stageA(h, ti, LD):
                t0, sz = HTILES[ti]
                pt = h2t(t0)
                c0 = t0 // 32
                q_, qB, v_, vB, F, g_, gB = LD
                for d in range(2):
                    f_, fB = F[d]
                    S.op("act", lambda e: e.activation(f_[:], f_[:], AF.Sigmoid), reads=[fB], writes=[fB])
                sq_, sqB_ = sgr.next(); sg_, sgB_ = sgr.next()
                S.op("act", lambda e: e.activation(sq_[:], q_[:], AF.Sigmoid), reads=[qB], writes=[sqB_])
                S.op("act", lambda e: e.activation(sg_[:], g_[:], AF.Sigmoid), reads=[gB], writes=[sgB_])
                S.op("dve", lambda e: e.tensor_tensor(q_[:], q_[:], sq_[:], ALU.mult), reads=[qB, sqB_], writes=[qB])
                S.op("dve", lambda e: e.tensor_tensor(g_[:], g_[:], sg_[:], ALU.mult), reads=[gB, sgB_], writes=[gB])
                S.dma("sp", SGd[h * 128:(h + 1) * 128, t0:t0 + sz], g_[:], reads=[gB], writes=[DB("sg", h, ti)])
                vb, vbB = vbs.next()
                S.op("pool", lambda e: e.tensor_copy(vb, v_[:]), reads=[vB], writes=[vbB])
                vx, vxB = vxs.next()
                to_tokmajor(vb, vbB, vx, vxB)
                S.dma("sp", VXd[h, c0:c0 + NCK].rearrange("c s k -> s c k"), vx, reads=[vxB], writes=[DB("vx", h, ti)])
                X = []
                for d in range(2):
                    f_, fB = F[d]
                    lb_ap = lbv[:, 0, d, l, h:h + 1]
                    oml_ap = lbv[:, 1, d, l, h:h + 1]
                    noml_ap = lbv[:, 2, d, l, h:h + 1]
                    lf, lfB = logf.next(); k_, kB = kk.next(); b_, bB = bc.next()
                    S.op("act", lambda e: e.activation(lf[:], f_[:], AF.Ln, bias=lb_ap, scale=oml_ap), reads=[fB, smallB], writes=[lfB])
                    S.op("dve", lambda e: e.tensor_scalar(k_[:], f_[:], noml_ap, oml_ap, ALU.mult, ALU.add), reads=[fB, smallB], writes=[kB])
                    S.op("dve", lambda e: e.tensor_tensor_scan(b_[:], rmask[:, 0:sz], lf[:], 0.0, ALU.mult, ALU.add),
                         reads=[lfB, cstB], writes=[bB])
                    tot = c3(b_[:])[:, :, 31:32]
                    if d == 0:
                        bx, bxB = b_, bB
                    else:
                        bx, bxB = bc2.next()
                        S.op("dve", lambda e: e.tensor_tensor(bx[:], lf[:], b_[:], ALU.subtract), reads=[lfB, bB], writes=[bxB], chain=True)
                        S.op("dve", lambda e: e.tensor_tensor(c3(bx[:]), c3(bx[:]), tot.broadcast_to([128, NCK, 32]), ALU.add),
                             reads=[bxB, bB], writes=[bxB], chain=True)
                    dd, ddB = d3.next()
                    S.op("dve", lambda e: e.tensor_tensor(c3(dd[:]), tot.broadcast_to([128, NCK, 32]), c3(bx[:]), ALU.subtract),
                         reads=[bxB, bB], writes=[ddB], chain=True)
                    X.append(dict(k=k_, kB=kB, b=b_, bB=bB, bx=bx, bxB=bxB, dd=dd, ddB=ddB))
                return dict(h=h, ti=ti, q=q_, qB=qB, X=X)

            def stageB(C):
                h, ti, q_, qB, X = C["h"], C["ti"], C["q"], C["qB"], C["X"]
                t0, sz = HTILES[ti]
                c0 = t0 // 32
                for d in range(2):
                    Z = X[d]
                    da, daB = dAall[d + 2 * (h % 2)]
                    S.op("act", lambda e: e.activation(da[:, c0:c0 + NCK], c3(Z["b"][:])[:, :, 31], AF.Exp), reads=[Z["bB"]], writes=[daB])
                    qt_, qtB = qts.next(); kt_, ktB = kts.next(); kh_, khB = khs.next()
                    x1, x1B = ex.next()
                    S.op("act", lambda e: e.activation(x1[:], Z["bx"][:], AF.Exp), reads=[Z["bxB"]], writes=[x1B])
                    x2, x2B = ex.next()
                    S.op("act", lambda e: e.activation(x2[:], Z["bx"][:], AF.Exp, scale=-1.0), reads=[Z["bxB"]], writes=[x2B])
                    x3, x3B = ex.next()
                    S.op("act", lambda e: e.activation(x3[:], Z["dd"][:], AF.Exp), reads=[Z["ddB"]], writes=[x3B])
                    S.op("dve", lambda e: e.tensor_tensor(qt_, q_[:], x1[:], ALU.mult), reads=[qB, x1B], writes=[qtB], chain=True)
                    S.op("dve", lambda e: e.tensor_tensor(kt_, Z["k"][:], x2[:], ALU.mult), reads=[Z["kB"], x2B], writes=[ktB], chain=True)
                    S.op("dve", lambda e: e.tensor_tensor(kh_, Z["k"][:], x3[:], ALU.mult), reads=[Z["kB"], x3B], writes=[khB], chain=True)
                    S.dma("sp", QTd[d, h * 128:(h + 1) * 128, t0:t0 + sz], qt_, reads=[qtB], writes=[DB("qt", d, h, ti)])
                    S.dma("sp", KTd[d, h * 128:(h + 1) * 128, t0:t0 + sz], kt_, reads=[ktB], writes=[DB("kt", d, h, ti)])
                    kx, kxB = kxs.next()
                    to_tokmajor(kh_, khB, kx, kxB)
                    S.dma("sp", KXd[d, h, c0:c0 + NCK].rearrange("c s k -> s c k"), kx, reads=[kxB], writes=[DB("kx", d, h, ti)])
                if ti == len(HTILES) - 1:
                    for d in range(2):
                        da, daB = dAall[d + 2 * (h % 2)]
                        S.dma("sp", DAd[d, h * 128:(h + 1) * 128, :], da[:], reads=[daB], writes=[DB("da", d, h)])

            items = [(h, ti) for h in range(8) for ti in range(len(HTILES))]
            lds = {}
            for i in range(min(2, len(items))):
                lds[i] = loadsP(*items[i])
            prev = None
            for i, (h, ti) in enumerate(items):
                if i + 2 < len(items):
                    lds[i + 2] = loadsP(*items[i + 2])
                cur = stageA(h, ti, lds.pop(i))
                if prev is not None:
                    stageB(prev)
                prev = cur
            stageB(prev)
            S.barrier()

    def phase_hgrn_scan(l):
        with ExitStack() as st:
            ha = HAlias()

            def r32(name, n):
                return Ring([(sb("%s%d" % (name, i), [128, HT], F32, st), Buf()) for i in range(n)])
            graw = r32("hg", 3 * NCHAIN); ofl = r32("ho", 2 * NCHAIN)
            osum = r32("hos", 2 * NCHAIN); rr = r32("hrr", 2); ofs = r32("hofs", 3)
            sqb = Ring([(sb("hsq%d" % i, [128, HT], BF16, st), Buf()) for i in range(2)])
            sets = [[dict(qt=(ha.cm(), Buf()), kt=(ha.cm(), Buf()), kx=(ha.tm(), Buf()), vx=(ha.tm(), Buf()))
                     for _ in range(NCHAIN)] for _ in range(2)]
            dAc = [(sb("hdA%d" % i, [128, NCHK], F32, st), Buf()) for i in range(NCHAIN)]
            Sst = [(sb("hS%d" % i, [128, 128], F32, st), Buf()) for i in range(NCHAIN)]
            Sbf2 = [[(sb("hSb%d_%d" % (i, k), [128, 128], BF16, st), Buf()) for k in range(2)] for i in range(NCHAIN)]
            par = [0] * NCHAIN
            attm = [Ring([(sb("hat%d_%d" % (i, k), [32, 32], BF16, st), Buf()) for k in range(2)]) for i in range(NCHAIN)]
            pob = [ps("hpo%d" % i, [128, 512], F32, st) for i in range(2)]
            pobB = [Buf(), Buf()]
            po = [(pob[i // 2][:, (i % 2) * HT:(i % 2 + 1) * HT], pobB[i // 2]) for i in range(NCHAIN)]
            pcb = [ps("hpc%d" % i, [128, 512], F32, st) for i in range(NCHAIN)]
            pcB = [Buf() for _ in range(NCHAIN)]
            pa = [Ring([(pcb[i][0:32, k * 32:(k + 1) * 32], pcB[i]) for k in range(2)]) for i in range(NCHAIN)]
            pp = [(pcb[i][:, 128:256], pcB[i]) for i in range(NCHAIN)]
            prb = ps("hpr", [128, 512], F32, st)
            prB = Buf()
            pR = Ring([(prb[:, k * HT:(k + 1) * HT], prB) for k in range(2)])

            for d in range(2):
                mask = mask_f if d == 0 else mask_b
                order = list(range(len(HTILES))) if d == 0 else [0] + list(range(len(HTILES) - 1, 0, -1))
                for hg in range(8 // NCHAIN):
                    heads = [hg * NCHAIN + i for i in range(NCHAIN)]
                    for ci in range(NCHAIN):
                        h = heads[ci]
                        S.op("dve", lambda e: e.memset(Sst[ci][0][:], 0.0), writes=[Sst[ci][1]])
                        S.op("dve", lambda e: e.memset(Sbf2[ci][par[ci]][0][:], 0.0), writes=[Sbf2[ci][par[ci]][1]])
                        S.dma("sp", dAc[ci][0][:], DAd[d, h * 128:(h + 1) * 128, :], reads=[DB("da", d, h)], writes=[dAc[ci][1]])
                    fins = {}

                    def loads(si):
                        ti = order[si]
                        t0, sz = HTILES[ti]
                        c0 = t0 // 32
                        pt = h2t(t0)
                        fin = {}
                        for ci in range(NCHAIN):
                            h = heads[ci]
                            Z = sets[si % 2][ci]
                            S.dma("sp", Z["qt"][0], QTd[d, h * 128:(h + 1) * 128, t0:t0 + sz], reads=[DB("qt", d, h, ti)], writes=[Z["qt"][1]])
                            S.dma("sp", Z["kt"][0], KTd[d, h * 128:(h + 1) * 128, t0:t0 + sz], reads=[DB("kt", d, h, ti)], writes=[Z["kt"][1]])
                            S.dma("sp", Z["kx"][0], KXd[d, h, c0:c0 + NCK].rearrange("c s k -> s c k"), reads=[DB("kx", d, h, ti)],
                                  writes=[Z["kx"][1]])
                            S.dma("sp", Z["vx"][0], VXd[h, c0:c0 + NCK].rearrange("c s k -> s c k"), reads=[DB("vx", h, ti)],
                                  writes=[Z["vx"][1]])
                            if d == 1:
                                ol, olB = ofl.next(); g_, gB = graw.next()
                                S.dma("sp", ol[:], OF[h * 128:(h + 1) * 128, t0:t0 + sz], reads=[DB("of", h, ti)], writes=[olB])
                                S.dma("sp", g_[:], SGd[h * 128:(h + 1) * 128, t0:t0 + sz], reads=[DB("sg", h, ti)], writes=[gB])
                                fin[ci] = dict(ol=ol, olB=olB, g=g_, gB=gB)
                        fins[si] = fin

                    pending = []

                    def readout(item):
                        h, t0, sz, pt, os_, osB, Fn = item
                        sq_, sqB = sqb.next()
                        S.op("act", lambda e: e.activation(sq_[:], os_[:], AF.Square), reads=[osB], writes=[sqB])
                        pr, prB_ = pR.next()
                        S.op("pe", mm(pr, ones_bf[:], sq_[:], True, True), reads=[sqB, cbfB], writes=[prB_])
                        r_, rB = rr.next()
                        S.op("act", lambda e: e.activation(r_[:], pr, AF.Ln, bias=EPS, scale=1.0 / 128), reads=[prB_], writes=[rB])
                        S.op("act", lambda e: e.activation(r_[:], r_[:], AF.Exp, scale=-0.5), reads=[rB], writes=[rB])
                        S.op("dve", lambda e: e.tensor_tensor(os_[:], os_[:], r_[:], ALU.mult), reads=[osB, rB], writes=[osB])
                        S.op("dve", lambda e: e.scalar_tensor_tensor(H[:, h, t0:t0 + sz], os_[:], hgw[:, l:l + 1], Fn["g"][:],
                                                                     ALU.mult, ALU.mult),
                             reads=[osB, Fn["gB"], smallB], writes=[HB[h][pt]])

                    loads(0)
                    for si, ti in enumerate(order):
                        t0, sz = HTILES[ti]
                        c0 = t0 // 32
                        pt = h2t(t0)
                        if si + 1 < len(order):
                            loads(si + 1)
                        fin = fins.pop(si)
                        Zs = sets[si % 2]
                        corder = range(NCK) if d == 0 else range(NCK - 1, -1, -1)
                        for c in corder:
                            cs = slice(c * 32, (c + 1) * 32)
                            cur = {}
                            if pending and (c % 2 == 1):
                                readout(pending.pop(0))
                            for ci in range(NCHAIN):
                                Z = Zs[ci]
                                pa_, paB = pa[ci].next()
                                S.op("pe", mm(pa_, Z["kt"][0][:, cs], Z["qt"][0][:, cs], True, True), reads=[Z["kt"][1], Z["qt"][1]], writes=[paB])
                                am, amB = attm[ci].next()
                                S.op("dve", lambda e: e.tensor_tensor(am[:], pa_, mask, ALU.mult), reads=[paB, cstB], writes=[amB])
                                cur[ci] = (am, amB)
                            for ci in range(NCHAIN):
                                Z = Zs[ci]
                                vt, vtB = Z["vx"]; kx, kxB = Z["kx"]
                                pp_, ppB = pp[ci]
                                S.op("pe", mm(pp_, kx[:, c, :], vt[:, c, :], True, True), reads=[kxB, vtB], writes=[ppB])
                            for ci in range(NCHAIN):
                                Z = Zs[ci]
                                am, amB = cur[ci]
                                po_, poB = po[ci]
                                vt, vtB = Z["vx"]
                                sb_, sbB = Sbf2[ci][par[ci]]
                                S.op("pe", mm(po_[:, cs], vt[:, c, :], am[:], True, False), reads=[vtB, amB], writes=[poB])
                                S.op("pe", mm(po_[:, cs], sb_[:], Z["qt"][0][:, cs], False, True), reads=[sbB, Z["qt"][1]], writes=[poB])
                            for ci in range(NCHAIN):
                                pp_, ppB = pp[ci]
                                S.op("dve", lambda e: e.scalar_tensor_tensor(Sst[ci][0][:], Sst[ci][0][:], dAc[ci][0][:, c0 + c:c0 + c + 1], pp_,
                                                                             ALU.mult, ALU.add),
                                     reads=[Sst[ci][1], dAc[ci][1], ppB], writes=[Sst[ci][1]], chain=True)
                                par[ci] ^= 1
                                sn_, snB = Sbf2[ci][par[ci]]
                                if ci % 2 == 0:
                                    S.op("act", lambda e: e.copy(sn_[:], Sst[ci][0][:]), reads=[Sst[ci][1]], writes=[snB])
                                else:
                                    S.op("pool", lambda e: e.tensor_copy(sn_[:], Sst[ci][0][:]), reads=[Sst[ci][1]], writes=[snB])
                        for ci in range(NCHAIN):
                            h = heads[ci]
                            po_, poB = po[ci]
                            if d == 0:
                                o_, oB = ofs.next()
                                S.op("act", lambda e: e.copy(o_[:], po_), reads=[poB], writes=[oB])
                                S.dma("sp", OF[h * 128:(h + 1) * 128, t0:t0 + sz], o_[:], reads=[oB], writes=[DB("of", h, ti)])
                            else:
                                Fn = fin[ci]
                                os_, osB = osum.next()
                                S.op("dve", lambda e: e.tensor_tensor(os_[:], po_, Fn["ol"][:], ALU.add), reads=[poB, Fn["olB"]], writes=[osB])
                                pending.append((h, t0, sz, pt, os_, osB, Fn))
                    while pending:
                        readout(pending.pop(0))
            S.barrier()

    UOFF_C = 15
    UOFF_L = 15 + 256 + 15
    ULEN = 15 + 256 + 15 + 4096 + 15

    def phase_conv(l):
        with ExitStack() as st:
            Ub = [(sb("cvU%d" % i, [128, ULEN], BF16, st), Buf()) for i in range(2)]
            Dg = [(sb("cvD%d" % i, [128, 31, 128], BF16, st), Buf()) for i in range(2)]
            araw = Ring([(sb("cva%d" % i, [128, 512], F32, st), Buf()) for i in range(2)])
            braw = Ring([(sb("cvb%d" % i, [128, 512], F32, st), Buf()) for i in range(2)])
            accr = Ring([(sb("cvacc%d" % i, [128, 512], F32, st), Buf()) for i in range(2)])
            abf = Ring([(sb("cvab%d" % i, [128, 512], BF16, st), Buf()) for i in range(2)])
            dd = Ring([(sb("cvd%d" % i, [128, 512], F32, st), Buf()) for i in range(2)])
            sqb = Ring([(sb("cvs%d" % i, [128, 512], BF16, st), Buf()) for i in range(2)])
            rsd = Ring([(sb("cvr%d" % i, [128, 512], F32, st), Buf()) for i in range(2)])
            pc = Ring([(ps("cvpc%d" % i, [128, 512], F32, st), Buf()) for i in range(2)])
            pm = Ring([(ps("cvpm%d" % i, [128, 512], F32, st), Buf()) for i in range(2)])
            pv = Ring([(ps("cvpv%d" % i, [128, 512], F32, st), Buf()) for i in range(2)])
            cwl = sb("cwl", [128, 8, 31], F32, st)
            cwlB = Buf()
            S.dma("sp", cwl[:], conv_wT[:, l], writes=[cwlB])
            for i in range(2):
                S.op("dve", lambda e: e.memset(Ub[i][0][:], 0.0), writes=[Ub[i][1]])
            def diag(c2):
                Dm2, DB2 = Dg[c2 % 2]
                S.op("dve", lambda e: e.tensor_tensor(Dm2[:], ident_bf[:].unsqueeze(1).broadcast_to([128, 31, 128]),
                                                      cwl[:, c2, :].unsqueeze(2).broadcast_to([128, 31, 128]), ALU.mult),
                     reads=[cbfB, cwlB], writes=[DB2])

            def uprep(c2, ti):
                U2, UB2 = Ub[c2 % 2]
                t0, sz = TILES[ti]
                ar = 5120 + c2 * 128
                br = 6144 + c2 * 128
                a_, aB = araw.next(); b_, bB = braw.next()
                S.dma("sp", a_[:, 0:sz], PROJ[ar:ar + 128, t0:t0 + sz], reads=[DB("proj", ar // 128, ti)], writes=[aB])
                S.dma("sp", b_[:, 0:sz], PROJ[br:br + 128, t0:t0 + sz], reads=[DB("proj", br // 128, ti)], writes=[bB])
                S.op("act", lambda e: e.activation(b_[:, 0:sz], b_[:, 0:sz], AF.Sigmoid), reads=[bB], writes=[bB])
                uo = UOFF_C if ti == 0 else UOFF_L + (t0 - 256)
                S.op("dve", lambda e: e.tensor_tensor(U2[:, uo:uo + sz], a_[:, 0:sz], b_[:, 0:sz], ALU.mult), reads=[aB, bB], writes=[UB2])

            diag(0)
            for ti in range(len(TILES)):
                uprep(0, ti)
            for cc in range(8):
                U, UB = Ub[cc % 2]
                Dm, DB_ = Dg[cc % 2]
                nxt = cc + 1 < 8
                if nxt:
                    diag(cc + 1)

                def convmm(ti, hooks=()):
                    t0, sz = TILES[ti]
                    uo = UOFF_C if ti == 0 else UOFF_L + (t0 - 256)
                    p, pB = pc.next()
                    hooks = list(hooks)
                    for k in range(31):
                        S.op("pe", mm(p[:, 0:sz], Dm[:, k, :], U[:, uo + k - 15:uo + k - 15 + sz], k == 0, k == 30),
                             reads=[DB_, UB], writes=[pB], inc=(k == 30))
                        if k in (9, 19, 29) and hooks:
                            hooks.pop(0)()
                    while hooks:
                        hooks.pop(0)()
                    acc_, accB = accr.next()
                    S.op("act", lambda e: e.activation(acc_[:, 0:sz], p[:, 0:sz], AF.Identity, bias=cb[:, l, cc:cc + 1], scale=1.0),
                         reads=[pB, smallB], writes=[accB])
                    return (acc_, accB)

                def lnorm_parts(ti, A):
                    t0, sz = TILES[ti]
                    acc_, accB = A
                    acc = acc_[:, 0:sz]
                    ab_, abB = abf.next()
                    d_, dB = dd.next()
                    s_, sB = sqb.next()
                    r_, rB = rsd.next()
                    p1, p1B = pm.next()
                    p2, p2B = pv.next()
                    S.op("act", lambda e: e.copy(ab_[:, 0:sz], acc), reads=[accB], writes=[abB])

                    def part1():
                        S.op("pe", mm(p1[:, 0:sz], ones_bf[:], ab_[:, 0:sz], True, True), reads=[abB, cbfB], writes=[p1B])
                        S.op("dve", lambda e: e.scalar_tensor_tensor(d_[:, 0:sz], p1[:, 0:sz], -1.0 / 128, acc, ALU.mult, ALU.add),
                             reads=[p1B, accB], writes=[dB])
                        S.op("act", lambda e: e.activation(s_[:, 0:sz], d_[:, 0:sz], AF.Square), reads=[dB], writes=[sB])

                    def part2():
                        S.op("pe", mm(p2[:, 0:sz], ones_bf[:], s_[:, 0:sz], True, True), reads=[sB, cbfB], writes=[p2B])
                        S.op("act", lambda e: e.activation(r_[:, 0:sz], p2[:, 0:sz], AF.Ln, bias=LN_EPS, scale=1.0 / 128), reads=[p2B], writes=[rB])
                        S.op("act", lambda e: e.activation(r_[:, 0:sz], r_[:, 0:sz], AF.Exp, scale=-0.5), reads=[rB], writes=[rB])
                        S.op("dve", lambda e: e.tensor_tensor(d_[:, 0:sz], d_[:, 0:sz], r_[:, 0:sz], ALU.mult), reads=[dB, rB], writes=[dB])
                        S.op("dve", lambda e: e.tensor_scalar(d_[:, 0:sz], d_[:, 0:sz], clw[:, l, cc:cc + 1], clb[:, l, cc:cc + 1], ALU.mult, ALU.add),
                             reads=[dB, smallB], writes=[dB])
                        S.op("act", lambda e: e.activation(r_[:, 0:sz], d_[:, 0:sz], AF.Exp, scale=-1.0), reads=[dB], writes=[rB])

                    def part3():
                        S.op("dve", lambda e: e.tensor_scalar(r_[:, 0:sz], r_[:, 0:sz], 1.0, None, ALU.add), reads=[rB], writes=[rB])
                        S.op("dve", lambda e: e.reciprocal(r_[:, 0:sz], r_[:, 0:sz]), reads=[rB], writes=[rB])
                        S.op("dve", lambda e: e.tensor_tensor(H[:, 8 + cc, t0:t0 + sz], d_[:, 0:sz], r_[:, 0:sz], ALU.mult),
                             reads=[dB, rB], writes=[HB[8 + cc][ti]])
                    return [part1, part2, part3]

                prevA = convmm(0, [lambda: uprep(cc + 1, 0)] if nxt else [])
                for ti in range(1, len(TILES)):
                    hooks = lnorm_parts(ti - 1, prevA)
                    if nxt:
                        hooks.append(lambda ti=ti: uprep(cc + 1, ti))
                    prevA = convmm(ti, hooks)
                for part in lnorm_parts(len(TILES) - 1, prevA):
                    part()
            S.barrier()

    GOFF_C = 64
    GOFF_L = 64 + 256 + 64
    GLEN = 64 + 256 + 64 + 4096 + 128

    def phase_ffn_up(l):
        with ExitStack() as st:
            wg = [sb("fwg%d" % i, [128, KC, 256], BF16, st) for i in range(2)]
            wv = [sb("fwv%d" % i, [128, KC, 256], BF16, st) for i in range(2)]
            wgB = [Buf(), Buf()]; wvB = [Buf(), Buf()]
            G = [sb("fG%d" % i, [128, GLEN], F32, st) for i in range(1)]
            GB = [Buf()]
            gc = Ring([(sb("fgc%d" % i, [128, 512], F32, st), Buf()) for i in range(2)])
            gl = Ring([(sb("fgl%d" % i, [128, 512], F32, st), Buf()) for i in range(2)])
            ao = Ring([(sb("fao%d" % i, [128, 512], BF16, st), Buf()) for i in range(2)])
            pg = Ring([(ps("fpg%d" % i, [128, 512], F32, st), Buf()) for i in range(3)])
            pv = Ring([(ps("fpv%d" % i, [128, 512], F32, st), Buf()) for i in range(3)])
            fcwl = sb("fcwl", [128, FC, 9], F32, st)
            fcwB = Buf()
            S.dma("sp", fcwl[:], ffn_cwT[:, l], writes=[fcwB])
            GBt = [Buf() for _ in TILES]
            S.op("dve", lambda e: e.memset(G[0][:], 0.0), writes=GBt)
            g_ = G[0]
            for j in range(FC):
                bi = (j // 2) % 2
                wsub = slice((j % 2) * 128, (j % 2 + 1) * 128)
                if j % 2 == 0:
                    S.dma("pool", wg[bi][:], ffn_up[l][:, j * 128:(j + 2) * 128].rearrange("(kc p) n -> p kc n", p=128), writes=[wgB[bi]])
                    S.dma("pool", wv[bi][:], ffn_up[l][:, DFF + j * 128:DFF + (j + 2) * 128].rearrange("(kc p) n -> p kc n", p=128),
                          writes=[wvB[bi]])

                def gate(ti):
                    t0, sz = TILES[ti]
                    p, pB = pg.next()
                    for kc in range(KC):
                        S.op("pe", mm(p[:, 0:sz], wg[bi][:, kc, wsub], H[:, kc, t0:t0 + sz], kc == 0, kc == KC - 1),
                             reads=[wgB[bi], HB[kc][ti]], writes=[pB], inc=(kc == KC - 1))
                    go = GOFF_C if ti == 0 else GOFF_L + (t0 - 256)
                    S.op("act", lambda e: e.copy(g_[:, go:go + sz], p[:, 0:sz]), reads=[pB], writes=[GBt[ti]])

                def rest(ti):
                    t0, sz = TILES[ti]
                    c_, cB = gc.next()
                    go = GOFF_C if ti == 0 else GOFF_L + (t0 - 256)
                    gr = [GBt[x] for x in (ti - 1, ti, ti + 1) if 1 <= x <= 8] if ti > 0 else [GBt[0]]

                    def wt(k):
                        return fcwl[:, j, k:k + 1]
                    S.op("dve", lambda e: e.tensor_scalar(c_[:, 0:sz], g_[:, go:go + sz], wt(4), fcb[:, l, j:j + 1], ALU.mult, ALU.add),
                         reads=gr + [smallB, fcwB], writes=[cB])
                    if ti == 0:
                        for k, sh in ((3, -1), (5, 1)):
                            S.op("dve", lambda e: e.scalar_tensor_tensor(c_[:, 0:sz], g_[:, go + sh:go + sh + sz], wt(k), c_[:, 0:sz],
                                                                         ALU.mult, ALU.add), reads=gr + [smallB, fcwB, cB], writes=[cB])
                    else:
                        c3 = c_[:, 0:sz].rearrange("p (r w) -> p r w", w=64)
                        for dr in range(3):
                            for dw in range(3):
                                if dr == 1 and dw == 1:
                                    continue
                                wlo = 1 if dw == 0 else 0
                                whi = 63 if dw == 2 else 64
                                src = g_[:, go + (dr - 1) * 64 + (dw - 1) + wlo: go + (dr - 1) * 64 + (dw - 1) + wlo + sz]
                                src3 = src.rearrange("p (r w) -> p r w", w=64)[:, :, 0:whi - wlo]
                                S.op("dve", lambda e: e.scalar_tensor_tensor(c3[:, :, wlo:whi], src3, wt(dr * 3 + dw), c3[:, :, wlo:whi],
                                                                             ALU.mult, ALU.add), reads=gr + [smallB, fcwB, cB], writes=[cB],
                                     chain=True)
                    l_, lB = gl.next()
                    S.op("act", lambda e: e.activation(l_[:, 0:sz], c_[:, 0:sz], AF.Gelu), reads=[cB], writes=[lB])
                    p, pB = pv.next()
                    for kc in range(KC):
                        S.op("pe", mm(p[:, 0:sz], wv[bi][:, kc, wsub], H[:, kc, t0:t0 + sz], kc == 0, kc == KC - 1),
                             reads=[wvB[bi], HB[kc][ti]], writes=[pB], inc=(kc == KC - 1))
                    a_, aB = ao.next()
                    S.op("dve", lambda e: e.tensor_tensor(a_[:, 0:sz], p[:, 0:sz], l_[:, 0:sz], ALU.mult), reads=[pB, lB], writes=[aB])
                    S.dma("sp", ACTD[j * 128:(j + 1) * 128, t0:t0 + sz], a_[:, 0:sz], reads=[aB], writes=[DB("act", j, ti)])

                gate(0)
                gate(1)
                rest(0)
                for ti in range(2, 9):
                    gate(ti)
                    rest(ti - 1)
                rest(8)
            S.barrier()

    SUPER = [(0, 1280, [0, 1, 2]), (1280, 1024, [3, 4]), (2304, 1024, [5, 6]), (3328, 1024, [7, 8])]

    def phase_ffn_down(l):
        with ExitStack() as st:
            Htn = H
            pstride = H[:].ap[0][0]
            wd = [sb("fdw%d" % i, [128, FC, 256], BF16, st) for i in range(2)]
            wdB = [Buf(), Buf()]
            pp = Ring([(ps("fdp%d" % i, [128, 512], F32, st), Buf()) for i in range(4)])
            stg = Ring([(sb("fds%d" % i, [128, 512], F32, st), Buf()) for i in range(4)])
            AbB = [Buf() for _ in range(FC)]
            ev = 0
            wi = 0
            for (s0, slen, tis) in SUPER:
                Ab = bass.AP(Htn, 0, [[pstride, 128], [slen, FC], [1, slen]])
                for j in range(FC):
                    S.dma("sp", Ab[:, j, :], ACTD[j * 128:(j + 1) * 128, s0:s0 + slen],
                          reads=[DB("act", j, ti) for ti in tis], writes=[AbB[j]])
                for oc in range(KC):
                    if oc % 2 == 0:
                        w, wB = wd[wi % 2], wdB[wi % 2]
                        wi += 1
                        S.dma("pool", w[:], ffn_down[l][:, oc * 128:(oc + 2) * 128].rearrange("(kc p) n -> p kc n", p=128), writes=[wB])
                    osub = slice((oc % 2) * 128, (oc % 2 + 1) * 128)
                    for ti in tis:
                        t0, sz = TILES[ti]
                        p, pB = pp.next()
                        for j in range(FC):
                            S.op("pe", mm(p[:, 0:sz], w[:, j, osub], Ab[:, j, t0 - s0:t0 - s0 + sz], j == 0, j == FC - 1),
                                 reads=[wB, AbB[j]], writes=[pB], inc=(j == FC - 1))
                        g, gB = stg.next()
                        if ev % 2 == 0:
                            S.op("act", lambda e: e.copy(g[:, 0:sz], p[:, 0:sz]), reads=[pB], writes=[gB])
                        else:
                            S.op("dve", lambda e: e.tensor_copy(g[:, 0:sz], p[:, 0:sz]), reads=[pB], writes=[gB])
                        ev += 1
                        S.dma("sp", Y[oc * 128:(oc + 1) * 128, t0:t0 + sz], g[:, 0:sz], reads=[gB], writes=[DB("y2", oc, ti)])
            S.barrier()

    phases = []
    for l in range(DEPTH):
        phases.append(("mod%d" % l, lambda l=l: phase_mod(l)))
    for l in range(DEPTH):
        xsrc = xT if l == 0 else XB
        if l == 0:
            phases.append(("norm1_%d" % l, lambda l=l, xsrc=xsrc: phase_resnorm(l, xsrc, "x%d" % l, None, None, None, None, None, 0, False)))
        phases.append(("win%d" % l, lambda l=l: phase_dense(w_in[l], INC, PROJ, "proj")))
        phases.append(("hgpre%d" % l, lambda l=l: phase_hgrn_pre(l)))
        phases.append(("hgrn%d" % l, lambda l=l: phase_hgrn_scan(l)))
        phases.append(("conv%d" % l, lambda l=l: phase_conv(l)))
        phases.append(("wout%d" % l, lambda l=l: phase_dense(w_out[l], D, Y, "y")))
        phases.append(("res1_%d" % l, lambda l=l, xsrc=xsrc: phase_resnorm(l, xsrc, "xa", Y, "ya", 2, X1, "x1", 3, False)))
        phases.append(("ffnup%d" % l, lambda l=l: phase_ffn_up(l)))
        phases.append(("ffndn%d" % l, lambda l=l: phase_ffn_down(l)))
        if l == 0:
            phases.append(("res2_%d" % l, lambda l=l: phase_resnorm(l, X1, "x1b", Y, "y2b", 5, XB, "xb", 0, False, la=1)))
        else:
            phases.append(("res2_%d" % l, lambda l=l: phase_resnorm(l, X1, "x1b", Y, "y2b", 5, None, None, None, True)))
    S.barrier()
    for name, fn in phases:
        fn()
        if stop_after is not None and name == stop_after:
            if "HD" in debug:
                HD = dram_scr("HD", [D, NT], BF16)
                for kc in range(KC):
                    S.dma("sp", HD[kc * 128:(kc + 1) * 128, :], H[:, kc, :], reads=[HB[kc][ti] for ti in range(9)], writes=[DB("hd", kc)])
            break
    S.barrier()
    es.close()
    return nc, S


def _pc(a, inner):
    a = np.asarray(a, np.float32)
    sh = a.shape
    a = a.reshape(sh[:-1] + (sh[-1] // 128, 128))
    return np.ascontiguousarray(np.moveaxis(a, -1, 0))


def make_consts():
    c = np.zeros((128, 832), np.float32)
    c[:, 0:128] = np.eye(128, dtype=np.float32)
    c[:, 128:256] = 1.0
    s = np.arange(32)[:, None]
    t = np.arange(32)[None, :]
    c[0:32, 256:288] = (s <= t)
    c[0:32, 288:320] = (s >= t)
    r = np.ones(512, np.float32)
    r[::32] = 0.0
    c[:, 320:832] = r[None, :]
    return c


def make_in_maps(inp):
    x = np.asarray(inp["x"], np.float32)
    ctx = np.asarray(inp["ctx"], np.float32)
    c = np.asarray(inp["c"], np.float32)
    c_ctx = np.asarray(inp["c_ctx"], np.float32)
    shared = {
        "w_mod": np.ascontiguousarray(inp["w_mod"], np.float32),
        "b_modT": _pc(inp["b_mod"], 96),
        "norm_wT": _pc(inp["norm_w"], KC),
        "w_in": np.ascontiguousarray(inp["w_in"], np.float32),
        "lbT": _pc(inp["lb_logits"], 8),
        "hgwT": np.ascontiguousarray(np.asarray(inp["hg_norm_w"], np.float32).T),
        "conv_wT": np.ascontiguousarray(np.moveaxis(_pc(inp["conv_w"], 8), 2, 3)),
        "conv_bT": _pc(inp["conv_b"], 8),
        "conv_lnwT": _pc(inp["conv_ln_w"], 8),
        "conv_lnbT": _pc(inp["conv_ln_b"], 8),
        "w_out": np.ascontiguousarray(inp["w_out"], np.float32),
        "ffn_up": np.ascontiguousarray(inp["ffn_up"], np.float32),
        "ffn_cwT": np.ascontiguousarray(np.moveaxis(_pc(np.asarray(inp["ffn_conv_w"], np.float32).reshape(2, 9, DFF), FC), 2, 3)),
        "ffn_cbT": _pc(inp["ffn_conv_b"], FC),
        "ffn_down": np.ascontiguousarray(inp["ffn_down"], np.float32),
        "consts": make_consts(),
    }
    maps = []
    for b in range(2):
        m = dict(shared)
        m["xT"] = np.ascontiguousarray(np.concatenate([ctx[b], x[b]], axis=0).T)
        cc = np.stack([c[b], c_ctx], axis=-1)
        m["cT"] = np.ascontiguousarray(np.moveaxis(cc.reshape(KC, 128, 2), 1, 0))
        maps.append(m)
    return maps


_CACHE = {}


def kernel(**inputs):
    if "nc" not in _CACHE:
        _CACHE["nc"] = build_program()[0]
    nc = _CACHE["nc"]
    maps = make_in_maps(inputs)
    res = run_bass_kernel_spmd(nc, maps, core_ids=[0, 1])
    out = np.stack([np.ascontiguousarray(res.results[b]["yT"].T) for b in range(2)], axis=0)
    return out.astype(np.float32)
```

```python
import numpy as np
from contextlib import ExitStack
import concourse.bass as bass
import concourse.mybir as mybir
from concourse.bass_utils import run_bass_kernel_spmd

F32 = mybir.dt.float32
BF16 = mybir.dt.bfloat16
ALU = mybir.AluOpType
AF = mybir.ActivationFunctionType

D = 2048
KC = 16
NCTX = 256
NLAT = 4096
NT = NCTX + NLAT
DFF = 5632
FC = 44
INC = 7168
DEPTH = 2
EPS = 1e-6
LN_EPS = 1e-5
TILES = [(0, 256)] + [(256 + 512 * i, 512) for i in range(8)]
NDS = 48


class Buf:
    __slots__ = ("w", "r")

    def __init__(self):
        self.w = None
        self.r = {}


class Sched:
    def __init__(self, nc, es):
        self.nc = nc
        self.eng = {"pe": nc.tensor, "act": nc.scalar, "dve": nc.vector, "pool": nc.gpsimd, "sp": nc.sync}
        self.semobj = {}
        for k in self.eng:
            self.semobj[k] = es.enter_context(nc.semaphore("s_" + k))
        for i in range(NDS):
            self.semobj[("d", i)] = es.enter_context(nc.semaphore("d%d" % i))
        self.cnt = {k: 0 for k in self.eng}
        self.seen = {k: {} for k in self.eng}
        self.dval = [0] * NDS
        self.dnext = 0
        self.ninst = 0
        self.log = {k: [] for k in self.eng}

    def _wait(self, e, deps):
        need = {}
        for d in deps:
            if d is None:
                continue
            k, v = d
            if k == e and v > self.cnt[e]:
                continue
            if v > need.get(k, 0):
                need[k] = v
        for k, v in need.items():
            if self.seen[e].get(k, 0) >= v:
                continue
            self.eng[e].wait_ge(self.semobj[k], v)
            self.log[e].append(("w", k, v))
            self.seen[e][k] = v

    def _deps(self, e, reads, writes, chain=False):
        deps = []
        for b in reads:
            if chain and b.w is not None and b.w[0] == e:
                continue
            deps.append(b.w)
        for b in writes:
            if b.w is not None and b.w[0] != e:
                deps.append(b.w)
            for k, v in b.r.items():
                if k != e:
                    deps.append((k, v))
        return deps

    def op(self, e, fn, reads=(), writes=(), inc=True, chain=False):
        self._wait(e, self._deps(e, reads, writes, chain))
        inst = fn(self.eng[e])
        ev = (e, self.cnt[e] + 1)
        if inc:
            inst.then_inc(self.semobj[e], 1)
            self.cnt[e] += 1
            self.log[e].append(("i", e, 1))
        else:
            self.log[e].append(("n", e, 0))
        for b in reads:
            if b.r.get(e, 0) < ev[1]:
                b.r[e] = ev[1]
        for b in writes:
            b.w = ev
            b.r = {}
        self.ninst += 1
        return inst

    def dma(self, q, out, in_, reads=(), writes=()):
        deps = self._deps(q, reads, writes)
        i = self.dnext
        self.dnext = (i + 1) % NDS
        key = ("d", i)
        if self.dval[i] > 0:
            deps.append((key, self.dval[i]))
        self._wait(q, deps)
        inst = self.eng[q].dma_start(out=out, in_=in_)
        self.dval[i] += 16
        inst.then_inc(self.semobj[key], 16)
        self.log[q].append(("i", key, 16))
        ev = (key, self.dval[i])
        for b in reads:
            b.r[key] = ev[1]
        for b in writes:
            b.w = ev
            b.r = {}
        self.ninst += 1
        return ev

    def barrier(self):
        evs = [(k, v) for k, v in self.cnt.items() if v > 0]
        evs += [(("d", i), v) for i, v in enumerate(self.dval) if v > 0]
        for e in self.eng:
            self._wait(e, [x for x in evs if x[0] != e])


class Ring:
    def __init__(self, items):
        self.items = items
        self.i = 0

    def next(self):
        it = self.items[self.i]
        self.i = (self.i + 1) % len(self.items)
        return it


def build_program(stop_after=None, debug=False):
    nc = bass.Bass("TRN2", target_bir_lowering=False)
    es = ExitStack()
    S = Sched(nc, es)

    def dram_in(name, shape, dt=F32):
        return nc.dram_tensor(name, list(shape), dt, kind="ExternalInput").ap()

    debug = set(debug or ())

    def dram_scr(name, shape, dt=F32):
        return nc.dram_tensor(name, list(shape), dt, kind=("ExternalOutput" if name in debug else "Internal")).ap()

    xT = dram_in("xT", [D, NT])
    cT = dram_in("cT", [128, KC, 2])
    w_mod = dram_in("w_mod", [DEPTH, D, 6 * D])
    b_modT = dram_in("b_modT", [128, DEPTH, 96])
    norm_wT = dram_in("norm_wT", [128, DEPTH, 4, KC])
    w_in = dram_in("w_in", [DEPTH, D, INC])
    lbT = dram_in("lbT", [128, 2, DEPTH, 8])
    hgwT = dram_in("hgwT", [128, DEPTH])
    conv_wT = dram_in("conv_wT", [128, DEPTH, 8, 31])
    conv_bT = dram_in("conv_bT", [128, DEPTH, 8])
    conv_lnwT = dram_in("conv_lnwT", [128, DEPTH, 8])
    conv_lnbT = dram_in("conv_lnbT", [128, DEPTH, 8])
    w_out = dram_in("w_out", [DEPTH, D, D])
    ffn_up = dram_in("ffn_up", [DEPTH, D, 2 * DFF])
    ffn_cwT = dram_in("ffn_cwT", [128, DEPTH, FC, 9])
    ffn_cbT = dram_in("ffn_cbT", [128, DEPTH, FC])
    ffn_down = dram_in("ffn_down", [DEPTH, DFF, D])
    consts = dram_in("consts", [128, 832])
    yT = nc.dram_tensor("yT", [D, NLAT], F32, kind="ExternalOutput").ap()

    PROJ = dram_scr("PROJ", [INC, NT])
    OF = dram_scr("OF", [1024, NT])
    Y = dram_scr("Y", [D, NT])
    X1 = dram_scr("X1", [D, NT])
    XB = dram_scr("XB", [D, NT])
    ACTD = dram_scr("ACTD", [DFF, NT], BF16)
    dbufs = {}

    def DB(*key):
        b = dbufs.get(key)
        if b is None:
            b = dbufs[key] = Buf()
        return b

    uid = [0]

    def sb(name, shape, dt, stack=None):
        uid[0] += 1
        return (stack or es).enter_context(nc.sbuf_tensor("%s_%d" % (name, uid[0]), list(shape), dt))

    def ps(name, shape, dt, stack):
        uid[0] += 1
        return stack.enter_context(nc.psum_tensor("%s_%d" % (name, uid[0]), list(shape), dt))

    H = sb("H", [128, KC, NT], BF16)
    HB = [[Buf() for _ in TILES] for _ in range(KC)]
    cst = sb("cst", [128, 832], F32)
    cstB = Buf()
    ident_bf = sb("ident_bf", [128, 128], BF16)
    ones_bf = sb("ones_bf", [128, 128], BF16)
    cbfB = Buf()
    modv = sb("modv", [128, DEPTH, 96, 2], F32)
    modB = Buf()
    vecs = sb("vecs", [128, DEPTH, 6, KC, 2], F32)
    vecB = Buf()
    nw = sb("nw", [128, DEPTH, 4, KC], F32)
    smallB = Buf()
    bmod = sb("bmod", [128, DEPTH, 96], F32)
    lbl = sb("lbl", [128, 2, DEPTH, 8], F32)
    lbv = sb("lbv", [128, 3, 2, DEPTH, 8], F32)
    hgw = sb("hgw", [128, DEPTH], F32)
    cb = sb("cb", [128, DEPTH, 8], F32)
    clw = sb("clw", [128, DEPTH, 8], F32)
    clb = sb("clb", [128, DEPTH, 8], F32)
    fcb = sb("fcb", [128, DEPTH, FC], F32)
    ctile = sb("ctile", [128, KC, 2], F32)
    sc_bf = sb("sc_bf", [128, KC, 2], BF16)

    mask_f = cst[0:32, 256:288]
    mask_b = cst[0:32, 288:320]
    rmask = cst[:, 320:832]

    S.dma("sp", cst[:], consts, writes=[cstB])
    for dst, src in ((nw, norm_wT), (bmod, b_modT), (lbl, lbT), (hgw, hgwT), (cb, conv_bT),
                     (clw, conv_lnwT), (clb, conv_lnbT), (fcb, ffn_cbT), (ctile, cT)):
        S.dma("sp", dst[:], src, writes=[smallB])
    S.op("dve", lambda e: e.tensor_copy(ident_bf[:], cst[:, 0:128]), reads=[cstB], writes=[cbfB])
    S.op("dve", lambda e: e.tensor_copy(ones_bf[:], cst[:, 128:256]), reads=[cstB], writes=[cbfB])
    S.op("dve", lambda e: e.memset(lbv[:, 0, :, 0, :], 0.0), writes=[smallB])
    S.op("dve", lambda e: e.tensor_tensor(lbv[:, 0, :, 1, :], lbl[:, :, 1, :], lbl[:, :, 0, :], ALU.subtract),
         reads=[smallB], writes=[smallB])
    S.op("act", lambda e: e.activation(lbv[:, 0, :, 1, :], lbv[:, 0, :, 1, :], AF.Sigmoid), reads=[smallB], writes=[smallB])
    S.op("dve", lambda e: e.tensor_scalar(lbv[:, 1], lbv[:, 0], -1.0, 1.0, ALU.mult, ALU.add), reads=[smallB], writes=[smallB])
    S.op("dve", lambda e: e.tensor_scalar(lbv[:, 2], lbv[:, 0], -1.0, None, ALU.add), reads=[smallB], writes=[smallB])
    S.op("act", lambda e: e.activation(sc_bf[:], ctile[:], AF.Silu), reads=[smallB], writes=[smallB])

    def mm(out, lhsT, rhs, start, stop):
        return lambda e: e.matmul(out, lhsT, rhs, start=start, stop=stop)

    def phase_mod(l):
        with ExitStack() as st:
            wb = [sb("modw%d" % i, [128, KC, 512], BF16, st) for i in range(2)]
            wbB = [Buf(), Buf()]
            pm = ps("pm", [128, 512], F32, st)
            pmB = Buf()
            for blk in range(24):
                w, wB = wb[blk % 2], wbB[blk % 2]
                S.dma("pool", w[:], w_mod[l][:, blk * 512:(blk + 1) * 512].rearrange("(kc p) n -> p kc n", p=128),
                      writes=[wB])
                for sub in range(4):
                    j = blk * 4 + sub
                    for kc in range(KC):
                        S.op("pe", mm(pm[:, 2 * j:2 * j + 2], w[:, kc, sub * 128:(sub + 1) * 128], sc_bf[:, kc, :],
                                      kc == 0, kc == KC - 1), reads=[wB, smallB], writes=[pmB])
            S.op("dve", lambda e: e.tensor_tensor(modv[:, l], pm[:, 0:192].rearrange("p (j s) -> p j s", s=2),
                                                  bmod[:, l, :].unsqueeze(2).broadcast_to([128, 96, 2]), ALU.add),
                 reads=[pmB, smallB], writes=[modB])
            mv = modv[:, l]

            def nwb(i):
                return nw[:, l, i, :].unsqueeze(2).broadcast_to([128, KC, 2])
            S.op("dve", lambda e: e.scalar_tensor_tensor(vecs[:, l, 0], mv[:, 16:32, :], 1.0, nwb(0), ALU.add, ALU.mult),
                 reads=[modB, smallB], writes=[vecB])
            S.op("dve", lambda e: e.tensor_copy(vecs[:, l, 1], mv[:, 0:16, :]), reads=[modB], writes=[vecB])
            S.op("dve", lambda e: e.tensor_tensor(vecs[:, l, 2], mv[:, 32:48, :], nwb(1), ALU.mult), reads=[modB, smallB], writes=[vecB])
            S.op("dve", lambda e: e.scalar_tensor_tensor(vecs[:, l, 3], mv[:, 64:80, :], 1.0, nwb(2), ALU.add, ALU.mult),
                 reads=[modB, smallB], writes=[vecB])
            S.op("dve", lambda e: e.tensor_copy(vecs[:, l, 4], mv[:, 48:64, :]), reads=[modB], writes=[vecB])
            S.op("dve", lambda e: e.tensor_tensor(vecs[:, l, 5], mv[:, 80:96, :], nwb(3), ALU.mult), reads=[modB, smallB], writes=[vecB])
            S.barrier()

    RT = 128
    NRT = NT // RT

    def phase_resnorm(l, x_src, xkey, y_src, ykey, gi, x_dst, dkey, ai, final, la=None):
        with ExitStack() as st:
            xt = [(sb("rn_x%d" % i, [128, KC, RT], F32, st), Buf()) for i in range(2)]
            yt = [(sb("rn_y%d" % i, [128, KC, RT], F32, st), Buf()) for i in range(2)]
            sq = Ring([(sb("rn_sq%d" % i, [128, KC, RT], BF16, st), Buf()) for i in range(2)])
            tmp = Ring([(sb("rn_t%d" % i, [128, KC, RT], F32, st), Buf()) for i in range(1)])
            rs = [(sb("rn_rs%d" % i, [128, RT], F32, st), Buf()) for i in range(2)]
            rs2 = [(sb("rn_rt%d" % i, [128, RT], F32, st), Buf()) for i in range(2)]
            pss = [(ps("rn_ps%d" % i, [128, 512], F32, st), Buf()) for i in range(2)]
            pss2 = [(ps("rn_pt%d" % i, [128, 512], F32, st), Buf()) for i in range(2)]

            def view(dram, t0):
                return dram[:, t0:t0 + RT].rearrange("(kc p) t -> p kc t", p=128)

            def bct(ap):
                return ap.unsqueeze(1).broadcast_to([128, KC, RT])

            la = l if la is None else la

            def bcv(i, s, ll=l):
                return vecs[:, ll, i, :, s:s + 1].broadcast_to([128, KC, RT])

            def rstd(src, srcB, pst, out):
                p, pB = pst
                o, oB = out
                q, qB = sq.next()
                S.op("act", lambda e: e.activation(q[:], src[:], AF.Square), reads=[srcB], writes=[qB])
                for kc in range(KC):
                    S.op("pe", mm(p[:, 0:RT], ones_bf[:], q[:, kc, :], kc == 0, kc == KC - 1), reads=[qB, cbfB], writes=[pB],
                         inc=(kc == KC - 1))
                S.op("act", lambda e: e.activation(o[:], p[:, 0:RT], AF.Ln, bias=EPS, scale=1.0 / D), reads=[pB], writes=[oB])
                S.op("act", lambda e: e.activation(o[:], o[:], AF.Exp, scale=-0.5), reads=[oB], writes=[oB])

            def rloads(ti):
                t0 = ti * RT
                x, xB = xt[ti % 2]
                S.dma("sp", x[:], view(x_src, t0), reads=[DB(xkey, ti)], writes=[xB])
                if y_src is not None:
                    y, yB = yt[ti % 2]
                    S.dma("sp", y[:], view(y_src, t0), reads=[DB(ykey, ti)], writes=[yB])

            rloads(0)
            if y_src is not None:
                rstd(yt[0][0], yt[0][1], pss[0], rs[0])
            for ti in range(NRT):
                t0 = ti * RT
                s = 1 if t0 < NCTX else 0
                bi = ti % 2
                x, xB = xt[bi]
                if ti + 1 < NRT:
                    rloads(ti + 1)
                if y_src is not None:
                    y, yB = yt[bi]
                    r, rB = rs[bi]
                    t, tB = tmp.next()
                    S.op("dve", lambda e: e.tensor_tensor(t[:], y[:], bct(r[:]), ALU.mult), reads=[yB, rB], writes=[tB])
                    S.op("dve", lambda e: e.tensor_tensor(t[:], t[:], bcv(gi, s), ALU.mult), reads=[tB, vecB], writes=[tB], chain=True)
                    S.op("dve", lambda e: e.tensor_tensor(x[:], x[:], t[:], ALU.add), reads=[tB, xB], writes=[xB], chain=True)
                    if x_dst is not None:
                        S.dma("sp", view(x_dst, t0), x[:], reads=[xB], writes=[DB(dkey, ti)])
                    if final and t0 >= NCTX:
                        S.dma("sp", view(yT, t0 - NCTX), x[:], reads=[xB], writes=[DB("yT", ti)])
                if ai is not None:
                    rstd(x, xB, pss2[bi], rs2[bi])
                if y_src is not None and ti + 1 < NRT:
                    nb = (ti + 1) % 2
                    rstd(yt[nb][0], yt[nb][1], pss[nb], rs[nb])
                if ai is not None:
                    r2, r2B = rs2[bi]
                    hti = 0 if t0 < NCTX else 1 + (t0 - 256) // 512
                    t, tB = tmp.next()
                    S.op("dve", lambda e: e.tensor_tensor(t[:], x[:], bct(r2[:]), ALU.mult), reads=[xB, r2B], writes=[tB], chain=True)
                    S.op("dve", lambda e: e.tensor_tensor(t[:], t[:], bcv(ai, s, la), ALU.mult), reads=[tB, vecB], writes=[tB], chain=True)
                    S.op("dve", lambda e: e.tensor_tensor(H[:, :, t0:t0 + RT], t[:], bcv(ai + 1, s, la), ALU.add), reads=[tB, vecB],
                         writes=[HB[kc][hti] for kc in range(KC)], chain=True)
            S.barrier()

    def phase_dense(wsrc, ncols, out_dram, okey, nkc=KC):
        with ExitStack() as st:
            wb = [sb("dw%d" % i, [128, nkc, 512], BF16, st) for i in range(2)]
            wbB = [Buf(), Buf()]
            pp = Ring([(ps("dps%d" % i, [128, 512], F32, st), Buf()) for i in range(4)])
            stg = Ring([(sb("dstg%d" % i, [128, 512], F32, st), Buf()) for i in range(4)])
            nblk = ncols // 512
            ev = 0
            for blk in range(nblk):
                w, wB = wb[blk % 2], wbB[blk % 2]
                S.dma("pool", w[:], wsrc[:, blk * 512:(blk + 1) * 512].rearrange("(kc p) n -> p kc n", p=128), writes=[wB])
                for sub in range(4):
                    cc = blk * 4 + sub
                    for ti, (t0, sz) in enumerate(TILES):
                        p, pB = pp.next()
                        for kc in range(nkc):
                            S.op("pe", mm(p[:, 0:sz], w[:, kc, sub * 128:(sub + 1) * 128], H[:, kc, t0:t0 + sz], kc == 0, kc == nkc - 1),
                                 reads=[wB, HB[kc][ti]], writes=[pB], inc=(kc == nkc - 1))
                        g, gB = stg.next()
                        if ev % 2 == 0:
                            S.op("act", lambda e: e.copy(g[:, 0:sz], p[:, 0:sz]), reads=[pB], writes=[gB])
                        else:
                            S.op("dve", lambda e: e.tensor_copy(g[:, 0:sz], p[:, 0:sz]), reads=[pB], writes=[gB])
                        ev += 1
                        S.dma("sp", out_dram[cc * 128:(cc + 1) * 128, t0:t0 + sz], g[:, 0:sz], reads=[gB], writes=[DB(okey, cc, ti)])
            S.barrier()

    HT = 256
    HTILES = [(0, 256)] + [(256 + HT * i, HT) for i in range(NLAT // HT)]
    NCHAIN = 4
    NCK = HT // 32
    NCHK = NT // 32
    HPS = KC * NT
    QTd = dram_scr("QTd", [2, 1024, NT], BF16)
    KTd = dram_scr("KTd", [2, 1024, NT], BF16)
    KXd = dram_scr("KXd", [2, 8, NCHK, 32, 128], BF16)
    VXd = dram_scr("VXd", [8, NCHK, 32, 128], BF16)
    DAd = dram_scr("DAd", [2, 1024, NCHK], F32)
    SGd = dram_scr("SGd", [1024, NT], F32)

    class HAlias:
        def __init__(self):
            self.off = 8 * NT

        def cm(self):
            ap = bass.AP(H, self.off, [[HPS, 128], [1, HT]])
            self.off += HT
            return ap

        def tm(self):
            ap = bass.AP(H, self.off, [[HPS, 32], [128, NCK], [1, 128]])
            self.off += NCK * 128
            return ap

    def h2t(t0):
        return 0 if t0 < NCTX else 1 + (t0 - NCTX) // 512

    def c3(ap):
        return ap.rearrange("p (c j) -> p c j", j=32)

    def phase_hgrn_pre(l):
        with ExitStack() as st:
            ha = HAlias()

            def r32(name, n):
                return Ring([(sb("%s%d" % (name, i), [128, HT], F32, st), Buf()) for i in range(n)])
            raw = r32("praw", 20)
            sgr = r32("psg", 4)
            logf = r32("plog", 4); kk = r32("pkk", 4); bc = r32("pbc", 4); bc2 = r32("pbc2", 2); d3 = r32("pd3", 4)
            ex = r32("pex", 6)
            qts = Ring([(ha.cm(), Buf()) for _ in range(6)])
            kts = Ring([(ha.cm(), Buf()) for _ in range(6)])
            khs = Ring([(ha.cm(), Buf()) for _ in range(6)])
            vbs = Ring([(ha.cm(), Buf()) for _ in range(3)])
            kxs = Ring([(ha.tm(), Buf()) for _ in range(4)])
            vxs = Ring([(ha.tm(), Buf()) for _ in range(3)])
            dAall = [(sb("pdA%d" % i, [128, NCHK], F32, st), Buf()) for i in range(4)]
            ptr = Ring([(ps("pptr%d" % i, [128, 1024], BF16, st), Buf()) for i in range(4)])

            def to_tokmajor(src, srcB, dst, dstB):
                p, pB = ptr.next()
                for c in range(NCK):
                    S.op("pe", lambda e: e.transpose(p[0:32, c * 128:(c + 1) * 128], src[:, c * 32:(c + 1) * 32], ident_bf[:]),
                         reads=[srcB, cbfB], writes=[pB], inc=(c == NCK - 1))
                S.op("dve", lambda e: e.tensor_copy(dst, p[0:32, 0:NCK * 128].rearrange("p (c v) -> p c v", v=128)), reads=[pB], writes=[dstB])

            def loadsP(h, ti):
                t0, sz = HTILES[ti]
                pt = h2t(t0)
                q_, qB = raw.next(); v_, vB = raw.next()
                S.dma("sp", q_[:], PROJ[h * 128:(h + 1) * 128, t0:t0 + sz], reads=[DB("proj", h, pt)], writes=[qB])
                vr = 3072 + h * 128
                S.dma("sp", v_[:], PROJ[vr:vr + 128, t0:t0 + sz], reads=[DB("proj", vr // 128, pt)], writes=[vB])
                F = []
                for d in range(2):
                    f_, fB = raw.next()
                    fr = 1024 * (1 + d) + h * 128
                    S.dma("sp", f_[:], PROJ[fr:fr + 128, t0:t0 + sz], reads=[DB("proj", fr // 128, pt)], writes=[fB])
                    F.append((f_, fB))
                g_, gB = raw.next()
                gr = 4096 + h * 128
                S.dma("sp", g_[:], PROJ[gr:gr + 128, t0:t0 + sz], reads=[DB("proj", gr // 128, pt)], writes=[gB])
                return (q_, qB, v_, vB, F, g_, gB)

            def stageA(h, ti, LD):
                t0, sz = HTILES[ti]
                pt = h2t(t0)
                c0 = t0 // 32
                q_, qB, v_, vB, F, g_, gB = LD
                for d in range(2):
                    f_, fB = F[d]
                    S.op("act", lambda e: e.activation(f_[:], f_[:], AF.Sigmoid), reads=[fB], writes=[fB])
                sq_, sqB_ = sgr.next(); sg_, sgB_ = sgr.next()
                S.op("act", lambda e: e.activation(sq_[:], q_[:], AF.Sigmoid), reads=[qB], writes=[sqB_])
                S.op("act", lambda e: e.activation(sg_[:], g_[:], AF.Sigmoid), reads=[gB], writes=[sgB_])
                S.op("dve", lambda e: e.tensor_tensor(q_[:], q_[:], sq_[:], ALU.mult), reads=[qB, sqB_], writes=[qB])
                S.op("dve", lambda e: e.tensor_tensor(g_[:], g_[:], sg_[:], ALU.mult), reads=[gB, sgB_], writes=[gB])
                S.dma("sp", SGd[h * 128:(h + 1) * 128, t0:t0 + sz], g_[:], reads=[gB], writes=[DB("sg", h, ti)])
                vb, vbB = vbs.next()
                S.op("pool", lambda e: e.tensor_copy(vb, v_[:]), reads=[vB], writes=[vbB])
                vx, vxB = vxs.next()
                to_tokmajor(vb, vbB, vx, vxB)
                S.dma("sp", VXd[h, c0:c0 + NCK].rearrange("c s k -> s c k"), vx, reads=[vxB], writes=[DB("vx", h, ti)])
                X = []
                for d in range(2):
                    f_, fB = F[d]
                    lb_ap = lbv[:, 0, d, l, h:h + 1]
                    oml_ap = lbv[:, 1, d, l, h:h + 1]
                    noml_ap = lbv[:, 2, d, l, h:h + 1]
                    lf, lfB = logf.next(); k_, kB = kk.next(); b_, bB = bc.next()
                    S.op("act", lambda e: e.activation(lf[:], f_[:], AF.Ln, bias=lb_ap, scale=oml_ap), reads=[fB, smallB], writes=[lfB])
                    S.op("dve", lambda e: e.tensor_scalar(k_[:], f_[:], noml_ap, oml_ap, ALU.mult, ALU.add), reads=[fB, smallB], writes=[kB])
                    S.op("dve", lambda e: e.tensor_tensor_scan(b_[:], rmask[:, 0:sz], lf[:], 0.0, ALU.mult, ALU.add),
                         reads=[lfB, cstB], writes=[bB])
                    tot = c3(b_[:])[:, :, 31:32]
                    if d == 0:
                        bx, bxB = b_, bB
                    else:
                        bx, bxB = bc2.next()
                        S.op("dve", lambda e: e.tensor_tensor(bx[:], lf[:], b_[:], ALU.subtract), reads=[lfB, bB], writes=[bxB], chain=True)
                        S.op("dve", lambda e: e.tensor_tensor(c3(bx[:]), c3(bx[:]), tot.broadcast_to([128, NCK, 32]), ALU.add),
                             reads=[bxB, bB], writes=[bxB], chain=True)
                    dd, ddB = d3.next()
                    S.op("dve", lambda e: e.tensor_tensor(c3(dd[:]), tot.broadcast_to([128, NCK, 32]), c3(bx[:]), ALU.subtract),
                         reads=[bxB, bB], writes=[ddB], chain=True)
                    X.append(dict(k=k_, kB=kB, b=b_, bB=bB, bx=bx, bxB=bxB, dd=dd, ddB=ddB))
                return dict(h=h, ti=ti, q=q_, qB=qB, X=X)

            def stageB(C):
                h, ti, q_, qB, X = C["h"], C["ti"], C["q"], C["qB"], C["X"]
                t0, sz = HTILES[ti]
                c0 = t0 // 32
                for d in range(2):
                    Z = X[d]
                    da, daB = dAall[d + 2 * (h % 2)]
                    S.op("act", lambda e: e.activation(da[:, c0:c0 + NCK], c3(Z["b"][:])[:, :, 31], AF.Exp), reads=[Z["bB"]], writes=[daB])
                    qt_, qtB = qts.next(); kt_, ktB = kts.next(); kh_, khB = khs.next()
                    x1, x1B = ex.next()
                    S.op("act", lambda e: e.activation(x1[:], Z["bx"][:], AF.Exp), reads=[Z["bxB"]], writes=[x1B])
                    x2, x2B = ex.next()
                    S.op("act", lambda e: e.activation(x2[:], Z["bx"][:], AF.Exp, scale=-1.0), reads=[Z["bxB"]], writes=[x2B])
                    x3, x3B = ex.next()
                    S.op("act", lambda e: e.activation(x3[:], Z["dd"][:], AF.Exp), reads=[Z["ddB"]], writes=[x3B])
                    S.op("dve", lambda e: e.tensor_tensor(qt_, q_[:], x1[:], ALU.mult), reads=[qB, x1B], writes=[qtB], chain=True)
                    S.op("dve", lambda e: e.tensor_tensor(kt_, Z["k"][:], x2[:], ALU.mult), reads=[Z["kB"], x2B], writes=[ktB], chain=True)
                    S.op("dve", lambda e: e.tensor_tensor(kh_, Z["k"][:], x3[:], ALU.mult), reads=[Z["kB"], x3B], writes=[khB], chain=True)
                    S.dma("sp", QTd[d, h * 128:(h + 1) * 128, t0:t0 + sz], qt_, reads=[qtB], writes=[DB("qt", d, h, ti)])
                    S.dma("sp", KTd[d, h * 128:(h + 1) * 128, t0:t0 + sz], kt_, reads=[ktB], writes=[DB("kt", d, h, ti)])
                    kx, kxB = kxs.next()
                    to_tokmajor(kh_, khB, kx, kxB)
                    S.dma("sp", KXd[d, h, c0:c0 + NCK].rearrange("c s k -> s c k"), kx, reads=[kxB], writes=[DB("kx", d, h, ti)])
                if ti == len(HTILES) - 1:
                    for d in range(2):
                        da, daB = dAall[d + 2 * (h % 2)]
                        S.dma("sp", DAd[d, h * 128:(h + 1) * 128, :], da[:], reads=[daB], writes=[DB("da", d, h)])

            items = [(h, ti) for h in range(8) for ti in range(len(HTILES))]
            lds = {}
            for i in range(min(2, len(items))):
                lds[i] = loadsP(*items[i])
            prev = None
            for i, (h, ti) in enumerate(items):
                if i + 2 < len(items):
                    lds[i + 2] = loadsP(*items[i + 2])
                cur = stageA(h, ti, lds.pop(i))
                if prev is not None:
                    stageB(prev)
                prev = cur
            stageB(prev)
            S.barrier()

    def phase_hgrn_scan(l):
        with ExitStack() as st:
            ha = HAlias()

            def r32(name, n):
                return Ring([(sb("%s%d" % (name, i), [128, HT], F32, st), Buf()) for i in range(n)])
            graw = r32("hg", 3 * NCHAIN); ofl = r32("ho", 2 * NCHAIN)
            osum = r32("hos", 2 * NCHAIN); rr = r32("hrr", 2); ofs = r32("hofs", 3)
            sqb = Ring([(sb("hsq%d" % i, [128, HT], BF16, st), Buf()) for i in range(2)])
            sets = [[dict(qt=(ha.cm(), Buf()), kt=(ha.cm(), Buf()), kx=(ha.tm(), Buf()), vx=(ha.tm(), Buf()))
                     for _ in range(NCHAIN)] for _ in range(2)]
            dAc = [(sb("hdA%d" % i, [128, NCHK], F32, st), Buf()) for i in range(NCHAIN)]
            Sst = [(sb("hS%d" % i, [128, 128], F32, st), Buf()) for i in range(NCHAIN)]
            Sbf2 = [[(sb("hSb%d_%d" % (i, k), [128, 128], BF16, st), Buf()) for k in range(2)] for i in range(NCHAIN)]
            par = [0] * NCHAIN
            attm = [Ring([(sb("hat%d_%d" % (i, k), [32, 32], BF16, st), Buf()) for k in range(2)]) for i in range(NCHAIN)]
            pob = [ps("hpo%d" % i, [128, 512], F32, st) for i in range(2)]
            pobB = [Buf(), Buf()]
            po = [(pob[i // 2][:, (i % 2) * HT:(i % 2 + 1) * HT], pobB[i // 2]) for i in range(NCHAIN)]
            pcb = [ps("hpc%d" % i, [128, 512], F32, st) for i in range(NCHAIN)]
            pcB = [Buf() for _ in range(NCHAIN)]
            pa = [Ring([(pcb[i][0:32, k * 32:(k + 1) * 32], pcB[i]) for k in range(2)]) for i in range(NCHAIN)]
            pp = [(pcb[i][:, 128:256], pcB[i]) for i in range(NCHAIN)]
            prb = ps("hpr", [128, 512], F32, st)
            prB = Buf()
            pR = Ring([(prb[:, k * HT:(k + 1) * HT], prB) for k in range(2)])

            for d in range(2):
                mask = mask_f if d == 0 else mask_b
                order = list(range(len(HTILES))) if d == 0 else [0] + list(range(len(HTILES) - 1, 0, -1))
                for hg in range(8 // NCHAIN):
                    heads = [hg * NCHAIN + i for i in range(NCHAIN)]
                    for ci in range(NCHAIN):
                        h = heads[ci]
                        S.op("dve", lambda e: e.memset(Sst[ci][0][:], 0.0), writes=[Sst[ci][1]])
                        S.op("dve", lambda e: e.memset(Sbf2[ci][par[ci]][0][:], 0.0), writes=[Sbf2[ci][par[ci]][1]])
                        S.dma("sp", dAc[ci][0][:], DAd[d, h * 128:(h + 1) * 128, :], reads=[DB("da", d, h)], writes=[dAc[ci][1]])
                    fins = {}

                    def loads(si):
                        ti = order[si]
                        t0, sz = HTILES[ti]
                        c0 = t0 // 32
                        pt = h2t(t0)
                        fin = {}
                        for ci in range(NCHAIN):
                            h = heads[ci]
                            Z = sets[si % 2][ci]
                            S.dma("sp", Z["qt"][0], QTd[d, h * 128:(h + 1) * 128, t0:t0 + sz], reads=[DB("qt", d, h, ti)], writes=[Z["qt"][1]])
                            S.dma("sp", Z["kt"][0], KTd[d, h * 128:(h + 1) * 128, t0:t0 + sz], reads=[DB("kt", d, h, ti)], writes=[Z["kt"][1]])
                            S.dma("sp", Z["kx"][0], KXd[d, h, c0:c0 + NCK].rearrange("c s k -> s c k"), reads=[DB("kx", d, h, ti)],
                                  writes=[Z["kx"][1]])
                            S.dma("sp", Z["vx"][0], VXd[h, c0:c0 + NCK].rearrange("c s k -> s c k"), reads=[DB("vx", h, ti)],
                                  writes=[Z["vx"][1]])
                            if d == 1:
                                ol, olB = ofl.next(); g_, gB = graw.next()
                                S.dma("sp", ol[:], OF[h * 128:(h + 1) * 128, t0:t0 + sz], reads=[DB("of", h, ti)], writes=[olB])
                                S.dma("sp", g_[:], SGd[h * 128:(h + 1) * 128, t0:t0 + sz], reads=[DB("sg", h, ti)], writes=[gB])
                                fin[ci] = dict(ol=ol, olB=olB, g=g_, gB=gB)
                        fins[si] = fin

                    pending = []

                    def readout(item):
                        h, t0, sz, pt, os_, osB, Fn = item
                        sq_, sqB = sqb.next()
                        S.op("act", lambda e: e.activation(sq_[:], os_[:], AF.Square), reads=[osB], writes=[sqB])
                        pr, prB_ = pR.next()
                        S.op("pe", mm(pr, ones_bf[:], sq_[:], True, True), reads=[sqB, cbfB], writes=[prB_])
                        r_, rB = rr.next()
                        S.op("act", lambda e: e.activation(r_[:], pr, AF.Ln, bias=EPS, scale=1.0 / 128), reads=[prB_], writes=[rB])
                        S.op("act", lambda e: e.activation(r_[:], r_[:], AF.Exp, scale=-0.5), reads=[rB], writes=[rB])
                        S.op("dve", lambda e: e.tensor_tensor(os_[:], os_[:], r_[:], ALU.mult), reads=[osB, rB], writes=[osB], chain=True)
                        S.op("dve", lambda e: e.scalar_tensor_tensor(H[:, h, t0:t0 + sz], os_[:], hgw[:, l:l + 1], Fn["g"][:],
                                                                     ALU.mult, ALU.mult),
                             reads=[osB, Fn["gB"], smallB], writes=[HB[h][pt]], chain=True)

                    loads(0)
                    for si, ti in enumerate(order):
                        t0, sz = HTILES[ti]
                        c0 = t0 // 32
                        pt = h2t(t0)
                        if si + 1 < len(order):
                            loads(si + 1)
                        fin = fins.pop(si)
                        Zs = sets[si % 2]
                        corder = range(NCK) if d == 0 else range(NCK - 1, -1, -1)
                        for c in corder:
                            cs = slice(c * 32, (c + 1) * 32)
                            cur = {}
                            if pending and (c % 2 == 1):
                                readout(pending.pop(0))
                            for ci in range(NCHAIN):
                                Z = Zs[ci]
                                pa_, paB = pa[ci].next()
                                S.op("pe", mm(pa_, Z["kt"][0][:, cs], Z["qt"][0][:, cs], True, True), reads=[Z["kt"][1], Z["qt"][1]], writes=[paB])
                                am, amB = attm[ci].next()
                                S.op("dve", lambda e: e.tensor_tensor(am[:], pa_, mask, ALU.mult), reads=[paB, cstB], writes=[amB])
                                cur[ci] = (am, amB)
                            for ci in range(NCHAIN):
                                Z = Zs[ci]
                                vt, vtB = Z["vx"]; kx, kxB = Z["kx"]
                                pp_, ppB = pp[ci]
                                S.op("pe", mm(pp_, kx[:, c, :], vt[:, c, :], True, True), reads=[kxB, vtB], writes=[ppB])
                            for ci in range(NCHAIN):
                                Z = Zs[ci]
                                am, amB = cur[ci]
                                po_, poB = po[ci]
                                vt, vtB = Z["vx"]
                                sb_, sbB = Sbf2[ci][par[ci]]
                                S.op("pe", mm(po_[:, cs], vt[:, c, :], am[:], True, False), reads=[vtB, amB], writes=[poB])
                                S.op("pe", mm(po_[:, cs], sb_[:], Z["qt"][0][:, cs], False, True), reads=[sbB, Z["qt"][1]], writes=[poB])
                            for ci in range(NCHAIN):
                                pp_, ppB = pp[ci]
                                S.op("dve", lambda e: e.scalar_tensor_tensor(Sst[ci][0][:], Sst[ci][0][:], dAc[ci][0][:, c0 + c:c0 + c + 1], pp_,
                                                                             ALU.mult, ALU.add),
                                     reads=[Sst[ci][1], dAc[ci][1], ppB], writes=[Sst[ci][1]], chain=True)
                                par[ci] ^= 1
                                sn_, snB = Sbf2[ci][par[ci]]
                                if ci % 2 == 0:
                                    S.op("act", lambda e: e.copy(sn_[:], Sst[ci][0][:]), reads=[Sst[ci][1]], writes=[snB])
                                else:
                                    S.op("pool", lambda e: e.tensor_copy(sn_[:], Sst[ci][0][:]), reads=[Sst[ci][1]], writes=[snB])
                        for ci in range(NCHAIN):
                            h = heads[ci]
                            po_, poB = po[ci]
                            if d == 0:
                                o_, oB = ofs.next()
                                S.op("act", lambda e: e.copy(o_[:], po_), reads=[poB], writes=[oB])
                                S.dma("sp", OF[h * 128:(h + 1) * 128, t0:t0 + sz], o_[:], reads=[oB], writes=[DB("of", h, ti)])
                            else:
                                Fn = fin[ci]
                                os_, osB = osum.next()
                                S.op("dve", lambda e: e.tensor_tensor(os_[:], po_, Fn["ol"][:], ALU.add), reads=[poB, Fn["olB"]], writes=[osB])
                                pending.append((h, t0, sz, pt, os_, osB, Fn))
                    while pending:
                        readout(pending.pop(0))
            S.barrier()

    UOFF_C = 15
    UOFF_L = 15 + 256 + 15
    ULEN = 15 + 256 + 15 + 4096 + 15

    def phase_conv(l):
        with ExitStack() as st:
            Ub = [(sb("cvU%d" % i, [128, ULEN], BF16, st), Buf()) for i in range(2)]
            Dg = [(sb("cvD%d" % i, [128, 31, 128], BF16, st), Buf()) for i in range(2)]
            araw = Ring([(sb("cva%d" % i, [128, 512], F32, st), Buf()) for i in range(2)])
            braw = Ring([(sb("cvb%d" % i, [128, 512], F32, st), Buf()) for i in range(2)])
            accr = Ring([(sb("cvacc%d" % i, [128, 512], F32, st), Buf()) for i in range(2)])
            abf = Ring([(sb("cvab%d" % i, [128, 512], BF16, st), Buf()) for i in range(2)])
            dd = Ring([(sb("cvd%d" % i, [128, 512], F32, st), Buf()) for i in range(2)])
            sqb = Ring([(sb("cvs%d" % i, [128, 512], BF16, st), Buf()) for i in range(2)])
            rsd = Ring([(sb("cvr%d" % i, [128, 512], F32, st), Buf()) for i in range(2)])
            pc = Ring([(ps("cvpc%d" % i, [128, 512], F32, st), Buf()) for i in range(2)])
            pm = Ring([(ps("cvpm%d" % i, [128, 512], F32, st), Buf()) for i in range(2)])
            pv = Ring([(ps("cvpv%d" % i, [128, 512], F32, st), Buf()) for i in range(2)])
            cwl = sb("cwl", [128, 8, 31], F32, st)
            cwlB = Buf()
            S.dma("sp", cwl[:], conv_wT[:, l], writes=[cwlB])
            for i in range(2):
                S.op("dve", lambda e: e.memset(Ub[i][0][:], 0.0), writes=[Ub[i][1]])
            def diag(c2):
                Dm2, DB2 = Dg[c2 % 2]
                S.op("dve", lambda e: e.tensor_tensor(Dm2[:], ident_bf[:].unsqueeze(1).broadcast_to([128, 31, 128]),
                                                      cwl[:, c2, :].unsqueeze(2).broadcast_to([128, 31, 128]), ALU.mult),
                     reads=[cbfB, cwlB], writes=[DB2])

            def uprep(c2, ti):
                U2, UB2 = Ub[c2 % 2]
                t0, sz = TILES[ti]
                ar = 5120 + c2 * 128
                br = 6144 + c2 * 128
                a_, aB = araw.next(); b_, bB = braw.next()
                S.dma("sp", a_[:, 0:sz], PROJ[ar:ar + 128, t0:t0 + sz], reads=[DB("proj", ar // 128, ti)], writes=[aB])
                S.dma("sp", b_[:, 0:sz], PROJ[br:br + 128, t0:t0 + sz], reads=[DB("proj", br // 128, ti)], writes=[bB])
                S.op("act", lambda e: e.activation(b_[:, 0:sz], b_[:, 0:sz], AF.Sigmoid), reads=[bB], writes=[bB])
                uo = UOFF_C if ti == 0 else UOFF_L + (t0 - 256)
                S.op("dve", lambda e: e.tensor_tensor(U2[:, uo:uo + sz], a_[:, 0:sz], b_[:, 0:sz], ALU.mult), reads=[aB, bB], writes=[UB2])

            diag(0)
            for ti in range(len(TILES)):
                uprep(0, ti)
            for cc in range(8):
                U, UB = Ub[cc % 2]
                Dm, DB_ = Dg[cc % 2]
                nxt = cc + 1 < 8
                if nxt:
                    diag(cc + 1)

                def convmm(ti, hooks=()):
                    t0, sz = TILES[ti]
                    uo = UOFF_C if ti == 0 else UOFF_L + (t0 - 256)
                    p, pB = pc.next()
                    hooks = list(hooks)
                    for k in range(31):
                        S.op("pe", mm(p[:, 0:sz], Dm[:, k, :], U[:, uo + k - 15:uo + k - 15 + sz], k == 0, k == 30),
                             reads=[DB_, UB], writes=[pB], inc=(k == 30))
                        if k in (9, 19, 29) and hooks:
                            hooks.pop(0)()
                    while hooks:
                        hooks.pop(0)()
                    acc_, accB = accr.next()
                    S.op("act", lambda e: e.activation(acc_[:, 0:sz], p[:, 0:sz], AF.Identity, bias=cb[:, l, cc:cc + 1], scale=1.0),
                         reads=[pB, smallB], writes=[accB])
                    return (acc_, accB)

                def lnorm_parts(ti, A):
                    t0, sz = TILES[ti]
                    acc_, accB = A
                    acc = acc_[:, 0:sz]
                    ab_, abB = abf.next()
                    d_, dB = dd.next()
                    s_, sB = sqb.next()
                    r_, rB = rsd.next()
                    p1, p1B = pm.next()
                    p2, p2B = pv.next()
                    S.op("act", lambda e: e.copy(ab_[:, 0:sz], acc), reads=[accB], writes=[abB])

                    def part1():
                        S.op("pe", mm(p1[:, 0:sz], ones_bf[:], ab_[:, 0:sz], True, True), reads=[abB, cbfB], writes=[p1B])
                        S.op("dve", lambda e: e.scalar_tensor_tensor(d_[:, 0:sz], p1[:, 0:sz], -1.0 / 128, acc, ALU.mult, ALU.add),
                             reads=[p1B, accB], writes=[dB])
                        S.op("act", lambda e: e.activation(s_[:, 0:sz], d_[:, 0:sz], AF.Square), reads=[dB], writes=[sB])

                    def part2():
                        S.op("pe", mm(p2[:, 0:sz], ones_bf[:], s_[:, 0:sz], True, True), reads=[sB, cbfB], writes=[p2B])
                        S.op("act", lambda e: e.activation(r_[:, 0:sz], p2[:, 0:sz], AF.Ln, bias=LN_EPS, scale=1.0 / 128), reads=[p2B], writes=[rB])
                        S.op("act", lambda e: e.activation(r_[:, 0:sz], r_[:, 0:sz], AF.Exp, scale=-0.5), reads=[rB], writes=[rB])
                        S.op("dve", lambda e: e.tensor_tensor(d_[:, 0:sz], d_[:, 0:sz], r_[:, 0:sz], ALU.mult), reads=[dB, rB], writes=[dB], chain=True)
                        S.op("dve", lambda e: e.tensor_scalar(d_[:, 0:sz], d_[:, 0:sz], clw[:, l, cc:cc + 1], clb[:, l, cc:cc + 1], ALU.mult, ALU.add),
                             reads=[dB, smallB], writes=[dB], chain=True)
                        S.op("act", lambda e: e.activation(r_[:, 0:sz], d_[:, 0:sz], AF.Exp, scale=-1.0), reads=[dB], writes=[rB])

                    def part3():
                        S.op("dve", lambda e: e.tensor_scalar(r_[:, 0:sz], r_[:, 0:sz], 1.0, None, ALU.add), reads=[rB], writes=[rB])
                        S.op("dve", lambda e: e.reciprocal(r_[:, 0:sz], r_[:, 0:sz]), reads=[rB], writes=[rB], chain=True)
                        S.op("dve", lambda e: e.tensor_tensor(H[:, 8 + cc, t0:t0 + sz], d_[:, 0:sz], r_[:, 0:sz], ALU.mult),
                             reads=[dB, rB], writes=[HB[8 + cc][ti]], chain=True)
                    return [part1, part2, part3]

                prevA = convmm(0, [lambda: uprep(cc + 1, 0)] if nxt else [])
                for ti in range(1, len(TILES)):
                    hooks = lnorm_parts(ti - 1, prevA)
                    if nxt:
                        hooks.append(lambda ti=ti: uprep(cc + 1, ti))
                    prevA = convmm(ti, hooks)
                for part in lnorm_parts(len(TILES) - 1, prevA):
                    part()
            S.barrier()

    GOFF_C = 64
    GOFF_L = 64 + 256 + 64
    GLEN = 64 + 256 + 64 + 4096 + 128

    def phase_ffn_up(l):
        with ExitStack() as st:
            wg = [sb("fwg%d" % i, [128, KC, 256], BF16, st) for i in range(2)]
            wv = [sb("fwv%d" % i, [128, KC, 256], BF16, st) for i in range(2)]
            wgB = [Buf(), Buf()]; wvB = [Buf(), Buf()]
            G = [sb("fG%d" % i, [128, GLEN], F32, st) for i in range(1)]
            GB = [Buf()]
            gc = Ring([(sb("fgc%d" % i, [128, 512], F32, st), Buf()) for i in range(2)])
            gl = Ring([(sb("fgl%d" % i, [128, 512], F32, st), Buf()) for i in range(2)])
            ao = Ring([(sb("fao%d" % i, [128, 512], BF16, st), Buf()) for i in range(2)])
            pg = Ring([(ps("fpg%d" % i, [128, 512], F32, st), Buf()) for i in range(3)])
            pv = Ring([(ps("fpv%d" % i, [128, 512], F32, st), Buf()) for i in range(3)])
            fcwl = sb("fcwl", [128, FC, 9], F32, st)
            fcwB = Buf()
            S.dma("sp", fcwl[:], ffn_cwT[:, l], writes=[fcwB])
            GBt = [Buf() for _ in TILES]
            S.op("dve", lambda e: e.memset(G[0][:], 0.0), writes=GBt)
            g_ = G[0]
            for j in range(FC):
                bi = (j // 2) % 2
                wsub = slice((j % 2) * 128, (j % 2 + 1) * 128)
                if j % 2 == 0:
                    S.dma("pool", wg[bi][:], ffn_up[l][:, j * 128:(j + 2) * 128].rearrange("(kc p) n -> p kc n", p=128), writes=[wgB[bi]])
                    S.dma("pool", wv[bi][:], ffn_up[l][:, DFF + j * 128:DFF + (j + 2) * 128].rearrange("(kc p) n -> p kc n", p=128),
                          writes=[wvB[bi]])

                def gate(ti):
                    t0, sz = TILES[ti]
                    p, pB = pg.next()
                    for kc in range(KC):
                        S.op("pe", mm(p[:, 0:sz], wg[bi][:, kc, wsub], H[:, kc, t0:t0 + sz], kc == 0, kc == KC - 1),
                             reads=[wgB[bi], HB[kc][ti]], writes=[pB], inc=(kc == KC - 1))
                    go = GOFF_C if ti == 0 else GOFF_L + (t0 - 256)
                    S.op("act", lambda e: e.copy(g_[:, go:go + sz], p[:, 0:sz]), reads=[pB], writes=[GBt[ti]])

                def rest(ti):
                    t0, sz = TILES[ti]
                    c_, cB = gc.next()
                    go = GOFF_C if ti == 0 else GOFF_L + (t0 - 256)
                    gr = [GBt[x] for x in (ti - 1, ti, ti + 1) if 1 <= x <= 8] if ti > 0 else [GBt[0]]

                    def wt(k):
                        return fcwl[:, j, k:k + 1]
                    S.op("dve", lambda e: e.tensor_scalar(c_[:, 0:sz], g_[:, go:go + sz], wt(4), fcb[:, l, j:j + 1], ALU.mult, ALU.add),
                         reads=gr + [smallB, fcwB], writes=[cB])
                    if ti == 0:
                        for k, sh in ((3, -1), (5, 1)):
                            S.op("dve", lambda e: e.scalar_tensor_tensor(c_[:, 0:sz], g_[:, go + sh:go + sh + sz], wt(k), c_[:, 0:sz],
                                                                         ALU.mult, ALU.add), reads=gr + [smallB, fcwB, cB], writes=[cB])
                    else:
                        c3 = c_[:, 0:sz].rearrange("p (r w) -> p r w", w=64)
                        for dr in range(3):
                            for dw in range(3):
                                if dr == 1 and dw == 1:
                                    continue
                                wlo = 1 if dw == 0 else 0
                                whi = 63 if dw == 2 else 64
                                src = g_[:, go + (dr - 1) * 64 + (dw - 1) + wlo: go + (dr - 1) * 64 + (dw - 1) + wlo + sz]
                                src3 = src.rearrange("p (r w) -> p r w", w=64)[:, :, 0:whi - wlo]
                                S.op("dve", lambda e: e.scalar_tensor_tensor(c3[:, :, wlo:whi], src3, wt(dr * 3 + dw), c3[:, :, wlo:whi],
                                                                             ALU.mult, ALU.add), reads=gr + [smallB, fcwB, cB], writes=[cB],
                                     chain=True)
                    l_, lB = gl.next()
                    S.op("act", lambda e: e.activation(l_[:, 0:sz], c_[:, 0:sz], AF.Gelu), reads=[cB], writes=[lB])
                    p, pB = pv.next()
                    for kc in range(KC):
                        S.op("pe", mm(p[:, 0:sz], wv[bi][:, kc, wsub], H[:, kc, t0:t0 + sz], kc == 0, kc == KC - 1),
                             reads=[wvB[bi], HB[kc][ti]], writes=[pB], inc=(kc == KC - 1))
                    a_, aB = ao.next()
                    S.op("dve", lambda e: e.tensor_tensor(a_[:, 0:sz], p[:, 0:sz], l_[:, 0:sz], ALU.mult), reads=[pB, lB], writes=[aB])
                    S.dma("sp", ACTD[j * 128:(j + 1) * 128, t0:t0 + sz], a_[:, 0:sz], reads=[aB], writes=[DB("act", j, ti)])

                gate(0)
                gate(1)
                rest(0)
                for ti in range(2, 9):
                    gate(ti)
                    rest(ti - 1)
                rest(8)
            S.barrier()

    SUPER = [(0, 1280, [0, 1, 2]), (1280, 1024, [3, 4]), (2304, 1024, [5, 6]), (3328, 1024, [7, 8])]

    def phase_ffn_down(l):
        with ExitStack() as st:
            Htn = H
            pstride = H[:].ap[0][0]
            wd = [sb("fdw%d" % i, [128, FC, 256], BF16, st) for i in range(2)]
            wdB = [Buf(), Buf()]
            pp = Ring([(ps("fdp%d" % i, [128, 512], F32, st), Buf()) for i in range(4)])
            stg = Ring([(sb("fds%d" % i, [128, 512], F32, st), Buf()) for i in range(4)])
            AbB = [Buf() for _ in range(FC)]
            ev = 0
            wi = 0
            for (s0, slen, tis) in SUPER:
                Ab = bass.AP(Htn, 0, [[pstride, 128], [slen, FC], [1, slen]])
                for j in range(FC):
                    S.dma("sp", Ab[:, j, :], ACTD[j * 128:(j + 1) * 128, s0:s0 + slen],
                          reads=[DB("act", j, ti) for ti in tis], writes=[AbB[j]])
                for oc in range(KC):
                    if oc % 2 == 0:
                        w, wB = wd[wi % 2], wdB[wi % 2]
                        wi += 1
                        S.dma("pool", w[:], ffn_down[l][:, oc * 128:(oc + 2) * 128].rearrange("(kc p) n -> p kc n", p=128), writes=[wB])
                    osub = slice((oc % 2) * 128, (oc % 2 + 1) * 128)
                    for ti in tis:
                        t0, sz = TILES[ti]
                        p, pB = pp.next()
                        for j in range(FC):
                            S.op("pe", mm(p[:, 0:sz], w[:, j, osub], Ab[:, j, t0 - s0:t0 - s0 + sz], j == 0, j == FC - 1),
                                 reads=[wB, AbB[j]], writes=[pB], inc=(j == FC - 1))
                        g, gB = stg.next()
                        if ev % 2 == 0:
                            S.op("act", lambda e: e.copy(g[:, 0:sz], p[:, 0:sz]), reads=[pB], writes=[gB])
                        else:
                            S.op("dve", lambda e: e.tensor_copy(g[:, 0:sz], p[:, 0:sz]), reads=[pB], writes=[gB])
                        ev += 1
                        S.dma("sp", Y[oc * 128:(oc + 1) * 128, t0:t0 + sz], g[:, 0:sz], reads=[gB], writes=[DB("y2", oc, ti)])
            S.barrier()

    phases = []
    for l in range(DEPTH):
        phases.append(("mod%d" % l, lambda l=l: phase_mod(l)))
    for l in range(DEPTH):
        xsrc = xT if l == 0 else XB
        if l == 0:
            phases.append(("norm1_%d" % l, lambda l=l, xsrc=xsrc: phase_resnorm(l, xsrc, "x%d" % l, None, None, None, None, None, 0, False)))
        phases.append(("win%d" % l, lambda l=l: phase_dense(w_in[l], INC, PROJ, "proj")))
        phases.append(("hgpre%d" % l, lambda l=l: phase_hgrn_pre(l)))
        phases.append(("hgrn%d" % l, lambda l=l: phase_hgrn_scan(l)))
        phases.append(("conv%d" % l, lambda l=l: phase_conv(l)))
        phases.append(("wout%d" % l, lambda l=l: phase_dense(w_out[l], D, Y, "y")))
        phases.append(("res1_%d" % l, lambda l=l, xsrc=xsrc: phase_resnorm(l, xsrc, "xa", Y, "ya", 2, X1, "x1", 3, False)))
        phases.append(("ffnup%d" % l, lambda l=l: phase_ffn_up(l)))
        phases.append(("ffndn%d" % l, lambda l=l: phase_ffn_down(l)))
        if l == 0:
            phases.append(("res2_%d" % l, lambda l=l: phase_resnorm(l, X1, "x1b", Y, "y2b", 5, XB, "xb", 0, False, la=1)))
        else:
            phases.append(("res2_%d" % l, lambda l=l: phase_resnorm(l, X1, "x1b", Y, "y2b", 5, None, None, None, True)))
    S.barrier()
    for name, fn in phases:
        fn()
        if stop_after is not None and name == stop_after:
            if "HD" in debug:
                HD = dram_scr("HD", [D, NT], BF16)
                for kc in range(KC):
                    S.dma("sp", HD[kc * 128:(kc + 1) * 128, :], H[:, kc, :], reads=[HB[kc][ti] for ti in range(9)], writes=[DB("hd", kc)])
            break
    S.barrier()
    es.close()
    return nc, S


def _pc(a, inner):
    a = np.asarray(a, np.float32)
    sh = a.shape
    a = a.reshape(sh[:-1] + (sh[-1] // 128, 128))
    return np.ascontiguousarray(np.moveaxis(a, -1, 0))


def make_consts():
    c = np.zeros((128, 832), np.float32)
    c[:, 0:128] = np.eye(128, dtype=np.float32)
    c[:, 128:256] = 1.0
    s = np.arange(32)[:, None]
    t = np.arange(32)[None, :]
    c[0:32, 256:288] = (s <= t)
    c[0:32, 288:320] = (s >= t)
    r = np.ones(512, np.float32)
    r[::32] = 0.0
    c[:, 320:832] = r[None, :]
    return c


def make_in_maps(inp):
    x = np.asarray(inp["x"], np.float32)
    ctx = np.asarray(inp["ctx"], np.float32)
    c = np.asarray(inp["c"], np.float32)
    c_ctx = np.asarray(inp["c_ctx"], np.float32)
    shared = {
        "w_mod": np.ascontiguousarray(inp["w_mod"], np.float32),
        "b_modT": _pc(inp["b_mod"], 96),
        "norm_wT": _pc(inp["norm_w"], KC),
        "w_in": np.ascontiguousarray(inp["w_in"], np.float32),
        "lbT": _pc(inp["lb_logits"], 8),
        "hgwT": np.ascontiguousarray(np.asarray(inp["hg_norm_w"], np.float32).T),
        "conv_wT": np.ascontiguousarray(np.moveaxis(_pc(inp["conv_w"], 8), 2, 3)),
        "conv_bT": _pc(inp["conv_b"], 8),
        "conv_lnwT": _pc(inp["conv_ln_w"], 8),
        "conv_lnbT": _pc(inp["conv_ln_b"], 8),
        "w_out": np.ascontiguousarray(inp["w_out"], np.float32),
        "ffn_up": np.ascontiguousarray(inp["ffn_up"], np.float32),
        "ffn_cwT": np.ascontiguousarray(np.moveaxis(_pc(np.asarray(inp["ffn_conv_w"], np.float32).reshape(2, 9, DFF), FC), 2, 3)),
        "ffn_cbT": _pc(inp["ffn_conv_b"], FC),
        "ffn_down": np.ascontiguousarray(inp["ffn_down"], np.float32),
        "consts": make_consts(),
    }
    maps = []
    for b in range(2):
        m = dict(shared)
        m["xT"] = np.ascontiguousarray(np.concatenate([ctx[b], x[b]], axis=0).T)
        cc = np.stack([c[b], c_ctx], axis=-1)
        m["cT"] = np.ascontiguousarray(np.moveaxis(cc.reshape(KC, 128, 2), 1, 0))
        maps.append(m)
    return maps


_CACHE = {}


def kernel(**inputs):
    if "nc" not in _CACHE:
        _CACHE["nc"] = build_program()[0]
    nc = _CACHE["nc"]
    maps = make_in_maps(inputs)
    res = run_bass_kernel_spmd(nc, maps, core_ids=[0, 1])
    out = np.stack([np.ascontiguousarray(res.results[b]["yT"].T) for b in range(2)], axis=0)
    return out.astype(np.float32)
```

```python
import numpy as np
from contextlib import ExitStack
import concourse.bass as bass
import concourse.mybir as mybir
from concourse.bass_utils import run_bass_kernel_spmd

F32 = mybir.dt.float32
BF16 = mybir.dt.bfloat16
ALU = mybir.AluOpType
AF = mybir.ActivationFunctionType

D = 2048
KC = 16
NCTX = 256
NLAT = 4096
NT = NCTX + NLAT
DFF = 5632
FC = 44
INC = 7168
DEPTH = 2
EPS = 1e-6
LN_EPS = 1e-5
TILES = [(0, 256)] + [(256 + 512 * i, 512) for i in range(8)]
NDS = 48


class Buf:
    __slots__ = ("w", "r")

    def __init__(self):
        self.w = None
        self.r = {}


class Sched:
    def __init__(self, nc, es):
        self.nc = nc
        self.eng = {"pe": nc.tensor, "act": nc.scalar, "dve": nc.vector, "pool": nc.gpsimd, "sp": nc.sync}
        self.semobj = {}
        for k in self.eng:
            self.semobj[k] = es.enter_context(nc.semaphore("s_" + k))
        for i in range(NDS):
            self.semobj[("d", i)] = es.enter_context(nc.semaphore("d%d" % i))
        self.cnt = {k: 0 for k in self.eng}
        self.seen = {k: {} for k in self.eng}
        self.dval = [0] * NDS
        self.dnext = 0
        self.ninst = 0
        self.log = {k: [] for k in self.eng}

    def _wait(self, e, deps):
        need = {}
        for d in deps:
            if d is None:
                continue
            k, v = d
            if k == e and v > self.cnt[e]:
                continue
            if v > need.get(k, 0):
                need[k] = v
        for k, v in need.items():
            if self.seen[e].get(k, 0) >= v:
                continue
            self.eng[e].wait_ge(self.semobj[k], v)
            self.log[e].append(("w", k, v))
            self.seen[e][k] = v

    def _deps(self, e, reads, writes, chain=False):
        deps = []
        for b in reads:
            if chain and b.w is not None and b.w[0] == e:
                continue
            deps.append(b.w)
        for b in writes:
            if b.w is not None and b.w[0] != e:
                deps.append(b.w)
            for k, v in b.r.items():
                if k != e:
                    deps.append((k, v))
        return deps

    def op(self, e, fn, reads=(), writes=(), inc=True, chain=False):
        self._wait(e, self._deps(e, reads, writes, chain))
        inst = fn(self.eng[e])
        ev = (e, self.cnt[e] + 1)
        if inc:
            inst.then_inc(self.semobj[e], 1)
            self.cnt[e] += 1
            self.log[e].append(("i", e, 1))
        else:
            self.log[e].append(("n", e, 0))
        for b in reads:
            if b.r.get(e, 0) < ev[1]:
                b.r[e] = ev[1]
        for b in writes:
            b.w = ev
            b.r = {}
        self.ninst += 1
        return inst

    def dma(self, q, out, in_, reads=(), writes=()):
        deps = self._deps(q, reads, writes)
        i = self.dnext
        self.dnext = (i + 1) % NDS
        key = ("d", i)
        if self.dval[i] > 0:
            deps.append((key, self.dval[i]))
        self._wait(q, deps)
        inst = self.eng[q].dma_start(out=out, in_=in_)
        self.dval[i] += 16
        inst.then_inc(self.semobj[key], 16)
        self.log[q].append(("i", key, 16))
        ev = (key, self.dval[i])
        for b in reads:
            b.r[key] = ev[1]
        for b in writes:
            b.w = ev
            b.r = {}
        self.ninst += 1
        return ev

    def barrier(self):
        evs = [(k, v) for k, v in self.cnt.items() if v > 0]
        evs += [(("d", i), v) for i, v in enumerate(self.dval) if v > 0]
        for e in self.eng:
            self._wait(e, [x for x in evs if x[0] != e])


class Ring:
    def __init__(self, items):
        self.items = items
        self.i = 0

    def next(self):
        it = self.items[self.i]
        self.i = (self.i + 1) % len(self.items)
        return it


def build_program(stop_after=None, debug=False):
    nc = bass.Bass("TRN2", target_bir_lowering=False)
    es = ExitStack()
    S = Sched(nc, es)

    def dram_in(name, shape, dt=F32):
        return nc.dram_tensor(name, list(shape), dt, kind="ExternalInput").ap()

    debug = set(debug or ())

    def dram_scr(name, shape, dt=F32):
        return nc.dram_tensor(name, list(shape), dt, kind=("ExternalOutput" if name in debug else "Internal")).ap()

    xT = dram_in("xT", [D, NT])
    cT = dram_in("cT", [128, KC, 2])
    w_mod = dram_in("w_mod", [DEPTH, D, 6 * D])
    b_modT = dram_in("b_modT", [128, DEPTH, 96])
    norm_wT = dram_in("norm_wT", [128, DEPTH, 4, KC])
    w_in = dram_in("w_in", [DEPTH, D, INC])
    lbT = dram_in("lbT", [128, 2, DEPTH, 8])
    hgwT = dram_in("hgwT", [128, DEPTH])
    conv_wT = dram_in("conv_wT", [128, DEPTH, 8, 31])
    conv_bT = dram_in("conv_bT", [128, DEPTH, 8])
    conv_lnwT = dram_in("conv_lnwT", [128, DEPTH, 8])
    conv_lnbT = dram_in("conv_lnbT", [128, DEPTH, 8])
    w_out = dram_in("w_out", [DEPTH, D, D])
    ffn_up = dram_in("ffn_up", [DEPTH, D, 2 * DFF])
    ffn_cwT = dram_in("ffn_cwT", [128, DEPTH, FC, 9])
    ffn_cbT = dram_in("ffn_cbT", [128, DEPTH, FC])
    ffn_down = dram_in("ffn_down", [DEPTH, DFF, D])
    consts = dram_in("consts", [128, 832])
    yT = nc.dram_tensor("yT", [D, NLAT], F32, kind="ExternalOutput").ap()

    PROJ = dram_scr("PROJ", [INC, NT])
    OF = dram_scr("OF", [1024, NT])
    Y = dram_scr("Y", [D, NT])
    X1 = dram_scr("X1", [D, NT])
    XB = dram_scr("XB", [D, NT])
    ACTD = dram_scr("ACTD", [DFF, NT], BF16)
    dbufs = {}

    def DB(*key):
        b = dbufs.get(key)
        if b is None:
            b = dbufs[key] = Buf()
        return b

    uid = [0]

    def sb(name, shape, dt, stack=None):
        uid[0] += 1
        return (stack or es).enter_context(nc.sbuf_tensor("%s_%d" % (name, uid[0]), list(shape), dt))

    def ps(name, shape, dt, stack):
        uid[0] += 1
        return stack.enter_context(nc.psum_tensor("%s_%d" % (name, uid[0]), list(shape), dt))

    H = sb("H", [128, KC, NT], BF16)
    HB = [[Buf() for _ in TILES] for _ in range(KC)]
    cst = sb("cst", [128, 832], F32)
    cstB = Buf()
    ident_bf = sb("ident_bf", [128, 128], BF16)
    ones_bf = sb("ones_bf", [128, 128], BF16)
    cbfB = Buf()
    modv = sb("modv", [128, DEPTH, 96, 2], F32)
    modB = Buf()
    vecs = sb("vecs", [128, DEPTH, 6, KC, 2], F32)
    vecB = Buf()
    nw = sb("nw", [128, DEPTH, 4, KC], F32)
    smallB = Buf()
    bmod = sb("bmod", [128, DEPTH, 96], F32)
    lbl = sb("lbl", [128, 2, DEPTH, 8], F32)
    lbv = sb("lbv", [128, 3, 2, DEPTH, 8], F32)
    hgw = sb("hgw", [128, DEPTH], F32)
    cb = sb("cb", [128, DEPTH, 8], F32)
    clw = sb("clw", [128, DEPTH, 8], F32)
    clb = sb("clb", [128, DEPTH, 8], F32)
    fcb = sb("fcb", [128, DEPTH, FC], F32)
    ctile = sb("ctile", [128, KC, 2], F32)
    sc_bf = sb("sc_bf", [128, KC, 2], BF16)

    mask_f = cst[0:32, 256:288]
    mask_b = cst[0:32, 288:320]
    rmask = cst[:, 320:832]

    S.dma("sp", cst[:], consts, writes=[cstB])
    for dst, src in ((nw, norm_wT), (bmod, b_modT), (lbl, lbT), (hgw, hgwT), (cb, conv_bT),
                     (clw, conv_lnwT), (clb, conv_lnbT), (fcb, ffn_cbT), (ctile, cT)):
        S.dma("sp", dst[:], src, writes=[smallB])
    S.op("dve", lambda e: e.tensor_copy(ident_bf[:], cst[:, 0:128]), reads=[cstB], writes=[cbfB])
    S.op("dve", lambda e: e.tensor_copy(ones_bf[:], cst[:, 128:256]), reads=[cstB], writes=[cbfB])
    S.op("dve", lambda e: e.memset(lbv[:, 0, :, 0, :], 0.0), writes=[smallB])
    S.op("dve", lambda e: e.tensor_tensor(lbv[:, 0, :, 1, :], lbl[:, :, 1, :], lbl[:, :, 0, :], ALU.subtract),
         reads=[smallB], writes=[smallB])
    S.op("act", lambda e: e.activation(lbv[:, 0, :, 1, :], lbv[:, 0, :, 1, :], AF.Sigmoid), reads=[smallB], writes=[smallB])
    S.op("dve", lambda e: e.tensor_scalar(lbv[:, 1], lbv[:, 0], -1.0, 1.0, ALU.mult, ALU.add), reads=[smallB], writes=[smallB])
    S.op("dve", lambda e: e.tensor_scalar(lbv[:, 2], lbv[:, 0], -1.0, None, ALU.add), reads=[smallB], writes=[smallB])
    S.op("act", lambda e: e.activation(sc_bf[:], ctile[:], AF.Silu), reads=[smallB], writes=[smallB])

    def mm(out, lhsT, rhs, start, stop):
        return lambda e: e.matmul(out, lhsT, rhs, start=start, stop=stop)

    def phase_mod(l):
        with ExitStack() as st:
            wb = [sb("modw%d" % i, [128, KC, 512], BF16, st) for i in range(2)]
            wbB = [Buf(), Buf()]
            pm = ps("pm", [128, 512], F32, st)
            pmB = Buf()
            for blk in range(24):
                w, wB = wb[blk % 2], wbB[blk % 2]
                S.dma("pool", w[:], w_mod[l][:, blk * 512:(blk + 1) * 512].rearrange("(kc p) n -> p kc n", p=128),
                      writes=[wB])
                for sub in range(4):
                    j = blk * 4 + sub
                    for kc in range(KC):
                        S.op("pe", mm(pm[:, 2 * j:2 * j + 2], w[:, kc, sub * 128:(sub + 1) * 128], sc_bf[:, kc, :],
                                      kc == 0, kc == KC - 1), reads=[wB, smallB], writes=[pmB])
            S.op("dve", lambda e: e.tensor_tensor(modv[:, l], pm[:, 0:192].rearrange("p (j s) -> p j s", s=2),
                                                  bmod[:, l, :].unsqueeze(2).broadcast_to([128, 96, 2]), ALU.add),
                 reads=[pmB, smallB], writes=[modB])
            mv = modv[:, l]

            def nwb(i):
                return nw[:, l, i, :].unsqueeze(2).broadcast_to([128, KC, 2])
            S.op("dve", lambda e: e.scalar_tensor_tensor(vecs[:, l, 0], mv[:, 16:32, :], 1.0, nwb(0), ALU.add, ALU.mult),
                 reads=[modB, smallB], writes=[vecB])
            S.op("dve", lambda e: e.tensor_copy(vecs[:, l, 1], mv[:, 0:16, :]), reads=[modB], writes=[vecB])
            S.op("dve", lambda e: e.tensor_tensor(vecs[:, l, 2], mv[:, 32:48, :], nwb(1), ALU.mult), reads=[modB, smallB], writes=[vecB])
            S.op("dve", lambda e: e.scalar_tensor_tensor(vecs[:, l, 3], mv[:, 64:80, :], 1.0, nwb(2), ALU.add, ALU.mult),
                 reads=[modB, smallB], writes=[vecB])
            S.op("dve", lambda e: e.tensor_copy(vecs[:, l, 4], mv[:, 48:64, :]), reads=[modB], writes=[vecB])
            S.op("dve", lambda e: e.tensor_tensor(vecs[:, l, 5], mv[:, 80:96, :], nwb(3), ALU.mult), reads=[modB, smallB], writes=[vecB])
            S.barrier()

    RT = 128
    NRT = NT // RT

    def phase_resnorm(l, x_src, xkey, y_src, ykey, gi, x_dst, dkey, ai, final, la=None):
        with ExitStack() as st:
            xt = [(sb("rn_x%d" % i, [128, KC, RT], F32, st), Buf()) for i in range(2)]
            yt = [(sb("rn_y%d" % i, [128, KC, RT], F32, st), Buf()) for i in range(2)]
            sq = Ring([(sb("rn_sq%d" % i, [128, KC, RT], BF16, st), Buf()) for i in range(2)])
            tmp = Ring([(sb("rn_t%d" % i, [128, KC, RT], F32, st), Buf()) for i in range(1)])
            rs = [(sb("rn_rs%d" % i, [128, RT], F32, st), Buf()) for i in range(2)]
            rs2 = [(sb("rn_rt%d" % i, [128, RT], F32, st), Buf()) for i in range(2)]
            pss = [(ps("rn_ps%d" % i, [128, 512], F32, st), Buf()) for i in range(2)]
            pss2 = [(ps("rn_pt%d" % i, [128, 512], F32, st), Buf()) for i in range(2)]

            def view(dram, t0):
                return dram[:, t0:t0 + RT].rearrange("(kc p) t -> p kc t", p=128)

            def bct(ap):
                return ap.unsqueeze(1).broadcast_to([128, KC, RT])

            la = l if la is None else la

            def bcv(i, s, ll=l):
                return vecs[:, ll, i, :, s:s + 1].broadcast_to([128, KC, RT])

            def rstd(src, srcB, pst, out):
                p, pB = pst
                o, oB = out
                q, qB = sq.next()
                S.op("act", lambda e: e.activation(q[:], src[:], AF.Square), reads=[srcB], writes=[qB])
                for kc in range(KC):
                    S.op("pe", mm(p[:, 0:RT], ones_bf[:], q[:, kc, :], kc == 0, kc == KC - 1), reads=[qB, cbfB], writes=[pB],
                         inc=(kc == KC - 1))
                S.op("act", lambda e: e.activation(o[:], p[:, 0:RT], AF.Ln, bias=EPS, scale=1.0 / D), reads=[pB], writes=[oB])
                S.op("act", lambda e: e.activation(o[:], o[:], AF.Exp, scale=-0.5), reads=[oB], writes=[oB])

            def rloads(ti):
                t0 = ti * RT
                x, xB = xt[ti % 2]
                S.dma("sp", x[:], view(x_src, t0), reads=[DB(xkey, ti)], writes=[xB])
                if y_src is not None:
                    y, yB = yt[ti % 2]
                    S.dma("sp", y[:], view(y_src, t0), reads=[DB(ykey, ti)], writes=[yB])

            rloads(0)
            if y_src is not None:
                rstd(yt[0][0], yt[0][1], pss[0], rs[0])
            for ti in range(NRT):
                t0 = ti * RT
                s = 1 if t0 < NCTX else 0
                bi = ti % 2
                x, xB = xt[bi]
                if ti + 1 < NRT:
                    rloads(ti + 1)
                if y_src is not None:
                    y, yB = yt[bi]
                    r, rB = rs[bi]
                    t, tB = tmp.next()
                    S.op("dve", lambda e: e.tensor_tensor(t[:], y[:], bct(r[:]), ALU.mult), reads=[yB, rB], writes=[tB])
                    S.op("dve", lambda e: e.tensor_tensor(t[:], t[:], bcv(gi, s), ALU.mult), reads=[tB, vecB], writes=[tB], chain=True)
                    S.op("dve", lambda e: e.tensor_tensor(x[:], x[:], t[:], ALU.add), reads=[tB, xB], writes=[xB], chain=True)
                    if x_dst is not None:
                        S.dma("sp", view(x_dst, t0), x[:], reads=[xB], writes=[DB(dkey, ti)])
                    if final and t0 >= NCTX:
                        S.dma("sp", view(yT, t0 - NCTX), x[:], reads=[xB], writes=[DB("yT", ti)])
                if ai is not None:
                    rstd(x, xB, pss2[bi], rs2[bi])
                if y_src is not None and ti + 1 < NRT:
                    nb = (ti + 1) % 2
                    rstd(yt[nb][0], yt[nb][1], pss[nb], rs[nb])
                if ai is not None:
                    r2, r2B = rs2[bi]
                    hti = 0 if t0 < NCTX else 1 + (t0 - 256) // 512
                    t, tB = tmp.next()
                    S.op("dve", lambda e: e.tensor_tensor(t[:], x[:], bct(r2[:]), ALU.mult), reads=[xB, r2B], writes=[tB], chain=True)
                    S.op("dve", lambda e: e.tensor_tensor(t[:], t[:], bcv(ai, s, la), ALU.mult), reads=[tB, vecB], writes=[tB], chain=True)
                    S.op("dve", lambda e: e.tensor_tensor(H[:, :, t0:t0 + RT], t[:], bcv(ai + 1, s, la), ALU.add), reads=[tB, vecB],
                         writes=[HB[kc][hti] for kc in range(KC)], chain=True)
            S.barrier()

    def phase_dense(wsrc, ncols, out_dram, okey, nkc=KC):
        with ExitStack() as st:
            wb = [sb("dw%d" % i, [128, nkc, 512], BF16, st) for i in range(2)]
            wbB = [Buf(), Buf()]
            pp = Ring([(ps("dps%d" % i, [128, 512], F32, st), Buf()) for i in range(4)])
            stg = Ring([(sb("dstg%d" % i, [128, 512], F32, st), Buf()) for i in range(4)])
            nblk = ncols // 512
            ev = 0
            for blk in range(nblk):
                w, wB = wb[blk % 2], wbB[blk % 2]
                S.dma("pool", w[:], wsrc[:, blk * 512:(blk + 1) * 512].rearrange("(kc p) n -> p kc n", p=128), writes=[wB])
                for sub in range(4):
                    cc = blk * 4 + sub
                    for ti, (t0, sz) in enumerate(TILES):
                        p, pB = pp.next()
                        for kc in range(nkc):
                            S.op("pe", mm(p[:, 0:sz], w[:, kc, sub * 128:(sub + 1) * 128], H[:, kc, t0:t0 + sz], kc == 0, kc == nkc - 1),
                                 reads=[wB, HB[kc][ti]], writes=[pB], inc=(kc == nkc - 1))
                        g, gB = stg.next()
                        if ev % 2 == 0:
                            S.op("act", lambda e: e.copy(g[:, 0:sz], p[:, 0:sz]), reads=[pB], writes=[gB])
                        else:
                            S.op("dve", lambda e: e.tensor_copy(g[:, 0:sz], p[:, 0:sz]), reads=[pB], writes=[gB])
                        ev += 1
                        S.dma("sp", out_dram[cc * 128:(cc + 1) * 128, t0:t0 + sz], g[:, 0:sz], reads=[gB], writes=[DB(okey, cc, ti)])
            S.barrier()

    HT = 256
    HTILES = [(0, 256)] + [(256 + HT * i, HT) for i in range(NLAT // HT)]
    NCHAIN = 4
    NCK = HT // 32
    NCHK = NT // 32
    HPS = KC * NT
    QTd = dram_scr("QTd", [2, 1024, NT], BF16)
    KTd = dram_scr("KTd", [2, 1024, NT], BF16)
    KXd = dram_scr("KXd", [2, 8, NCHK, 32, 128], BF16)
    VXd = dram_scr("VXd", [8, NCHK, 32, 128], BF16)
    DAd = dram_scr("DAd", [2, 1024, NCHK], F32)
    SGd = dram_scr("SGd", [1024, NT], F32)

    class HAlias:
        def __init__(self):
            self.off = 8 * NT

        def cm(self):
            ap = bass.AP(H, self.off, [[HPS, 128], [1, HT]])
            self.off += HT
            return ap

        def tm(self):
            ap = bass.AP(H, self.off, [[HPS, 32], [128, NCK], [1, 128]])
            self.off += NCK * 128
            return ap

    def h2t(t0):
        return 0 if t0 < NCTX else 1 + (t0 - NCTX) // 512

    def c3(ap):
        return ap.rearrange("p (c j) -> p c j", j=32)

    def phase_hgrn_pre(l):
        with ExitStack() as st:
            ha = HAlias()

            def r32(name, n):
                return Ring([(sb("%s%d" % (name, i), [128, HT], F32, st), Buf()) for i in range(n)])
            raw = r32("praw", 20)
            sgr = r32("psg", 4)
            logf = r32("plog", 4); kk = r32("pkk", 4); bc = r32("pbc", 4); bc2 = r32("pbc2", 2); d3 = r32("pd3", 4)
            ex = r32("pex", 6)
            qts = Ring([(ha.cm(), Buf()) for _ in range(6)])
            kts = Ring([(ha.cm(), Buf()) for _ in range(6)])
            khs = Ring([(ha.cm(), Buf()) for _ in range(6)])
            vbs = Ring([(ha.cm(), Buf()) for _ in range(3)])
            kxs = Ring([(ha.tm(), Buf()) for _ in range(4)])
            vxs = Ring([(ha.tm(), Buf()) for _ in range(3)])
            dAall = [(sb("pdA%d" % i, [128, NCHK], F32, st), Buf()) for i in range(4)]
            ptr = Ring([(ps("pptr%d" % i, [128, 1024], BF16, st), Buf()) for i in range(4)])

            def to_tokmajor(src, srcB, dst, dstB):
                p, pB = ptr.next()
                for c in range(NCK):
                    S.op("pe", lambda e: e.transpose(p[0:32, c * 128:(c + 1) * 128], src[:, c * 32:(c + 1) * 32], ident_bf[:]),
                         reads=[srcB, cbfB], writes=[pB], inc=(c == NCK - 1))
                S.op("dve", lambda e: e.tensor_copy(dst, p[0:32, 0:NCK * 128].rearrange("p (c v) -> p c v", v=128)), reads=[pB], writes=[dstB])

            def loadsP(h, ti):
                t0, sz = HTILES[ti]
                pt = h2t(t0)
                q_, qB = raw.next(); v_, vB = raw.next()
                S.dma("sp", q_[:], PROJ[h * 128:(h + 1) * 128, t0:t0 + sz], reads=[DB("proj", h, pt)], writes=[qB])
                vr = 3072 + h * 128
                S.dma("sp", v_[:], PROJ[vr:vr + 128, t0:t0 + sz], reads=[DB("proj", vr // 128, pt)], writes=[vB])
                F = []
                for d in range(2):
                    f_, fB = raw.next()
                    fr = 1024 * (1 + d) + h * 128
                    S.dma("sp", f_[:], PROJ[fr:fr + 128, t0:t0 + sz], reads=[DB("proj", fr // 128, pt)], writes=[fB])
                    F.append((f_, fB))
                g_, gB = raw.next()
                gr = 4096 + h * 128
                S.dma("sp", g_[:], PROJ[gr:gr + 128, t0:t0 + sz], reads=[DB("proj", gr // 128, pt)], writes=[gB])
                return (q_, qB, v_, vB, F, g_, gB)

            def stageA(h, ti, LD):
                t0, sz = HTILES[ti]
                pt = h2t(t0)
                c0 = t0 // 32
                q_, qB, v_, vB, F, g_, gB = LD
                for d in range(2):
                    f_, fB = F[d]
                    S.op("act", lambda e: e.activation(f_[:], f_[:], AF.Sigmoid), reads=[fB], writes=[fB])
                sq_, sqB_ = sgr.next(); sg_, sgB_ = sgr.next()
                S.op("act", lambda e: e.activation(sq_[:], q_[:], AF.Sigmoid), reads=[qB], writes=[sqB_])
                S.op("act", lambda e: e.activation(sg_[:], g_[:], AF.Sigmoid), reads=[gB], writes=[sgB_])
                S.op("dve", lambda e: e.tensor_tensor(q_[:], q_[:], sq_[:], ALU.mult), reads=[qB, sqB_], writes=[qB])
                S.op("dve", lambda e: e.tensor_tensor(g_[:], g_[:], sg_[:], ALU.mult), reads=[gB, sgB_], writes=[gB])
                S.dma("sp", SGd[h * 128:(h + 1) * 128, t0:t0 + sz], g_[:], reads=[gB], writes=[DB("sg", h, ti)])
                vb, vbB = vbs.next()
                S.op("pool", lambda e: e.tensor_copy(vb, v_[:]), reads=[vB], writes=[vbB])
                vx, vxB = vxs.next()
                to_tokmajor(vb, vbB, vx, vxB)
                S.dma("sp", VXd[h, c0:c0 + NCK].rearrange("c s k -> s c k"), vx, reads=[vxB], writes=[DB("vx", h, ti)])
                X = []
                for d in range(2):
                    f_, fB = F[d]
                    lb_ap = lbv[:, 0, d, l, h:h + 1]
                    oml_ap = lbv[:, 1, d, l, h:h + 1]
                    noml_ap = lbv[:, 2, d, l, h:h + 1]
                    lf, lfB = logf.next(); k_, kB = kk.next(); b_, bB = bc.next()
                    S.op("act", lambda e: e.activation(lf[:], f_[:], AF.Ln, bias=lb_ap, scale=oml_ap), reads=[fB, smallB], writes=[lfB])
                    S.op("dve", lambda e: e.tensor_scalar(k_[:], f_[:], noml_ap, oml_ap, ALU.mult, ALU.add), reads=[fB, smallB], writes=[kB])
                    S.op("dve", lambda e: e.tensor_tensor_scan(b_[:], rmask[:, 0:sz], lf[:], 0.0, ALU.mult, ALU.add),
                         reads=[lfB, cstB], writes=[bB])
                    tot = c3(b_[:])[:, :, 31:32]
                    if d == 0:
                        bx, bxB = b_, bB
                    else:
                        bx, bxB = bc2.next()
                        S.op("dve", lambda e: e.tensor_tensor(bx[:], lf[:], b_[:], ALU.subtract), reads=[lfB, bB], writes=[bxB], chain=True)
                        S.op("dve", lambda e: e.tensor_tensor(c3(bx[:]), c3(bx[:]), tot.broadcast_to([128, NCK, 32]), ALU.add),
                             reads=[bxB, bB], writes=[bxB], chain=True)
                    dd, ddB = d3.next()
                    S.op("dve", lambda e: e.tensor_tensor(c3(dd[:]), tot.broadcast_to([128, NCK, 32]), c3(bx[:]), ALU.subtract),
                         reads=[bxB, bB], writes=[ddB], chain=True)
                    X.append(dict(k=k_, kB=kB, b=b_, bB=bB, bx=bx, bxB=bxB, dd=dd, ddB=ddB))
                return dict(h=h, ti=ti, q=q_, qB=qB, X=X)

            def stageB(C):
                h, ti, q_, qB, X = C["h"], C["ti"], C["q"], C["qB"], C["X"]
                t0, sz = HTILES[ti]
                c0 = t0 // 32
                for d in range(2):
                    Z = X[d]
                    da, daB = dAall[d + 2 * (h % 2)]
                    S.op("act", lambda e: e.activation(da[:, c0:c0 + NCK], c3(Z["b"][:])[:, :, 31], AF.Exp), reads=[Z["bB"]], writes=[daB])
                    qt_, qtB = qts.next(); kt_, ktB = kts.next(); kh_, khB = khs.next()
                    x1, x1B = ex.next()
                    S.op("act", lambda e: e.activation(x1[:], Z["bx"][:], AF.Exp), reads=[Z["bxB"]], writes=[x1B])
                    x2, x2B = ex.next()
                    S.op("act", lambda e: e.activation(x2[:], Z["bx"][:], AF.Exp, scale=-1.0), reads=[Z["bxB"]], writes=[x2B])
                    x3, x3B = ex.next()
                    S.op("act", lambda e: e.activation(x3[:], Z["dd"][:], AF.Exp), reads=[Z["ddB"]], writes=[x3B])
                    S.op("dve", lambda e: e.tensor_tensor(qt_, q_[:], x1[:], ALU.mult), reads=[qB, x1B], writes=[qtB], chain=True)
                    S.op("dve", lambda e: e.tensor_tensor(kt_, Z["k"][:], x2[:], ALU.mult), reads=[Z["kB"], x2B], writes=[ktB], chain=True)
                    S.op("dve", lambda e: e.tensor_tensor(kh_, Z["k"][:], x3[:], ALU.mult), reads=[Z["kB"], x3B], writes=[khB], chain=True)
                    S.dma("sp", QTd[d, h * 128:(h + 1) * 128, t0:t0 + sz], qt_, reads=[qtB], writes=[DB("qt", d, h, ti)])
                    S.dma("sp", KTd[d, h * 128:(h + 1) * 128, t0:t0 + sz], kt_, reads=[ktB], writes=[DB("kt", d, h, ti)])
                    kx, kxB = kxs.next()
                    to_tokmajor(kh_, khB, kx, kxB)
                    S.dma("sp", KXd[d, h, c0:c0 + NCK].rearrange("c s k -> s c k"), kx, reads=[kxB], writes=[DB("kx", d, h, ti)])
                if ti == len(HTILES) - 1:
                    for d in range(2):
                        da, daB = dAall[d + 2 * (h % 2)]
                        S.dma("sp", DAd[d, h * 128:(h + 1) * 128, :], da[:], reads=[daB], writes=[DB("da", d, h)])

            items = [(h, ti) for h in range(8) for ti in range(len(HTILES))]
            lds = {}
            for i in range(min(2, len(items))):
                lds[i] = loadsP(*items[i])
            prev = None
            for i, (h, ti) in enumerate(items):
                if i + 2 < len(items):
                    lds[i + 2] = loadsP(*items[i + 2])
                cur = stageA(h, ti, lds.pop(i))
                if prev is not None:
                    stageB(prev)
                prev = cur
            stageB(prev)
            S.barrier()

    def phase_hgrn_scan(l):
        with ExitStack() as st:
            ha = HAlias()

            def r32(name, n):
                return Ring([(sb("%s%d" % (name, i), [128, HT], F32, st), Buf()) for i in range(n)])
            graw = r32("hg", 3 * NCHAIN); ofl = r32("ho", 2 * NCHAIN)
            osum = r32("hos", 2 * NCHAIN); rr = r32("hrr", 2); ofs = r32("hofs", 3)
            sqb = Ring([(sb("hsq%d" % i, [128, HT], BF16, st), Buf()) for i in range(2)])
            sets = [[dict(qt=(ha.cm(), Buf()), kt=(ha.cm(), Buf()), kx=(ha.tm(), Buf()), vx=(ha.tm(), Buf()))
                     for _ in range(NCHAIN)] for _ in range(2)]
            dAc = [(sb("hdA%d" % i, [128, NCHK], F32, st), Buf()) for i in range(NCHAIN)]
            Sst = [(sb("hS%d" % i, [128, 128], F32, st), Buf()) for i in range(NCHAIN)]
            Sbf2 = [[(sb("hSb%d_%d" % (i, k), [128, 128], BF16, st), Buf()) for k in range(2)] for i in range(NCHAIN)]
            par = [0] * NCHAIN
            attm = [Ring([(sb("hat%d_%d" % (i, k), [32, 32], BF16, st), Buf()) for k in range(2)]) for i in range(NCHAIN)]
            pob = [ps("hpo%d" % i, [128, 512], F32, st) for i in range(2)]
            pobB = [Buf(), Buf()]
            po = [(pob[i // 2][:, (i % 2) * HT:(i % 2 + 1) * HT], pobB[i // 2]) for i in range(NCHAIN)]
            pcb = [ps("hpc%d" % i, [128, 512], F32, st) for i in range(NCHAIN)]
            pcB = [Buf() for _ in range(NCHAIN)]
            pa = [Ring([(pcb[i][0:32, k * 32:(k + 1) * 32], pcB[i]) for k in range(2)]) for i in range(NCHAIN)]
            pp = [(pcb[i][:, 128:256], pcB[i]) for i in range(NCHAIN)]
            prb = ps("hpr", [128, 512], F32, st)
            prB = Buf()
            pR = Ring([(prb[:, k * HT:(k + 1) * HT], prB) for k in range(2)])

            for d in range(2):
                mask = mask_f if d == 0 else mask_b
                order = list(range(len(HTILES))) if d == 0 else [0] + list(range(len(HTILES) - 1, 0, -1))
                for hg in range(8 // NCHAIN):
                    heads = [hg * NCHAIN + i for i in range(NCHAIN)]
                    for ci in range(NCHAIN):
                        h = heads[ci]
                        S.op("dve", lambda e: e.memset(Sst[ci][0][:], 0.0), writes=[Sst[ci][1]])
                        S.op("dve", lambda e: e.memset(Sbf2[ci][par[ci]][0][:], 0.0), writes=[Sbf2[ci][par[ci]][1]])
                        S.dma("sp", dAc[ci][0][:], DAd[d, h * 128:(h + 1) * 128, :], reads=[DB("da", d, h)], writes=[dAc[ci][1]])
                    fins = {}

                    def loads(si):
                        ti = order[si]
                        t0, sz = HTILES[ti]
                        c0 = t0 // 32
                        pt = h2t(t0)
                        fin = {}
                        for ci in range(NCHAIN):
                            h = heads[ci]
                            Z = sets[si % 2][ci]
                            S.dma("sp", Z["qt"][0], QTd[d, h * 128:(h + 1) * 128, t0:t0 + sz], reads=[DB("qt", d, h, ti)], writes=[Z["qt"][1]])
                            S.dma("sp", Z["kt"][0], KTd[d, h * 128:(h + 1) * 128, t0:t0 + sz], reads=[DB("kt", d, h, ti)], writes=[Z["kt"][1]])
                            S.dma("sp", Z["kx"][0], KXd[d, h, c0:c0 + NCK].rearrange("c s k -> s c k"), reads=[DB("kx", d, h, ti)],
                                  writes=[Z["kx"][1]])
                            S.dma("sp", Z["vx"][0], VXd[h, c0:c0 + NCK].rearrange("c s k -> s c k"), reads=[DB("vx", h, ti)],
                                  writes=[Z["vx"][1]])
                            if d == 1:
                                ol, olB = ofl.next(); g_, gB = graw.next()
                                S.dma("sp", ol[:], OF[h * 128:(h + 1) * 128, t0:t0 + sz], reads=[DB("of", h, ti)], writes=[olB])
                                S.dma("sp", g_[:], SGd[h * 128:(h + 1) * 128, t0:t0 + sz], reads=[DB("sg", h, ti)], writes=[gB])
                                fin[ci] = dict(ol=ol, olB=olB, g=g_, gB=gB)
                        fins[si] = fin

                    pending = []

                    def readout(item):
                        h, t0, sz, pt, os_, osB, Fn = item
                        sq_, sqB = sqb.next()
                        S.op("act", lambda e: e.activation(sq_[:], os_[:], AF.Square), reads=[osB], writes=[sqB])
                        pr, prB_ = pR.next()
                        S.op("pe", mm(pr, ones_bf[:], sq_[:], True, True), reads=[sqB, cbfB], writes=[prB_])
                        r_, rB = rr.next()
                        S.op("act", lambda e: e.activation(r_[:], pr, AF.Ln, bias=EPS, scale=1.0 / 128), reads=[prB_], writes=[rB])
                        S.op("act", lambda e: e.activation(r_[:], r_[:], AF.Exp, scale=-0.5), reads=[rB], writes=[rB])
                        S.op("dve", lambda e: e.tensor_tensor(os_[:], os_[:], r_[:], ALU.mult), reads=[osB, rB], writes=[osB], chain=True)
                        S.op("dve", lambda e: e.scalar_tensor_tensor(H[:, h, t0:t0 + sz], os_[:], hgw[:, l:l + 1], Fn["g"][:],
                                                                     ALU.mult, ALU.mult),
                             reads=[osB, Fn["gB"], smallB], writes=[HB[h][pt]], chain=True)

                    loads(0)
                    for si, ti in enumerate(order):
                        t0, sz = HTILES[ti]
                        c0 = t0 // 32
                        pt = h2t(t0)
                        if si + 1 < len(order):
                            loads(si + 1)
                        fin = fins.pop(si)
                        Zs = sets[si % 2]
                        corder = range(NCK) if d == 0 else range(NCK - 1, -1, -1)
                        for c in corder:
                            cs = slice(c * 32, (c + 1) * 32)
                            cur = {}
                            if pending and (c % 2 == 1):
                                readout(pending.pop(0))
                            for ci in range(NCHAIN):
                                Z = Zs[ci]
                                pa_, paB = pa[ci].next()
                                S.op("pe", mm(pa_, Z["kt"][0][:, cs], Z["qt"][0][:, cs], True, True), reads=[Z["kt"][1], Z["qt"][1]], writes=[paB])
                                am, amB = attm[ci].next()
                                S.op("dve", lambda e: e.tensor_tensor(am[:], pa_, mask, ALU.mult), reads=[paB, cstB], writes=[amB])
                                cur[ci] = (am, amB)
                            for ci in range(NCHAIN):
                                Z = Zs[ci]
                                vt, vtB = Z["vx"]; kx, kxB = Z["kx"]
                                pp_, ppB = pp[ci]
                                S.op("pe", mm(pp_, kx[:, c, :], vt[:, c, :], True, True), reads=[kxB, vtB], writes=[ppB])
                            for ci in range(NCHAIN):
                                Z = Zs[ci]
                                am, amB = cur[ci]
                                po_, poB = po[ci]
                                vt, vtB = Z["vx"]
                                sb_, sbB = Sbf2[ci][par[ci]]
                                S.op("pe", mm(po_[:, cs], vt[:, c, :], am[:], True, False), reads=[vtB, amB], writes=[poB])
                                S.op("pe", mm(po_[:, cs], sb_[:], Z["qt"][0][:, cs], False, True), reads=[sbB, Z["qt"][1]], writes=[poB])
                            for ci in range(NCHAIN):
                                pp_, ppB = pp[ci]
                                S.op("dve", lambda e: e.scalar_tensor_tensor(Sst[ci][0][:], Sst[ci][0][:], dAc[ci][0][:, c0 + c:c0 + c + 1], pp_,
                                                                             ALU.mult, ALU.add),
                                     reads=[Sst[ci][1], dAc[ci][1], ppB], writes=[Sst[ci][1]], chain=True)
                                par[ci] ^= 1
                                sn_, snB = Sbf2[ci][par[ci]]
                                if ci % 2 == 0:
                                    S.op("act", lambda e: e.copy(sn_[:], Sst[ci][0][:]), reads=[Sst[ci][1]], writes=[snB])
                                else:
                                    S.op("pool", lambda e: e.tensor_copy(sn_[:], Sst[ci][0][:]), reads=[Sst[ci][1]], writes=[snB])
                        for ci in range(NCHAIN):
                            h = heads[ci]
                            po_, poB = po[ci]
                            if d == 0:
                                o_, oB = ofs.next()
                                S.op("act", lambda e: e.copy(o_[:], po_), reads=[poB], writes=[oB])
                                S.dma("sp", OF[h * 128:(h + 1) * 128, t0:t0 + sz], o_[:], reads=[oB], writes=[DB("of", h, ti)])
                            else:
                                Fn = fin[ci]
                                os_, osB = osum.next()
                                S.op("dve", lambda e: e.tensor_tensor(os_[:], po_, Fn["ol"][:], ALU.add), reads=[poB, Fn["olB"]], writes=[osB])
                                pending.append((h, t0, sz, pt, os_, osB, Fn))
                    while pending:
                        readout(pending.pop(0))
            S.barrier()

    UOFF_C = 15
    UOFF_L = 15 + 256 + 15
    ULEN = 15 + 256 + 15 + 4096 + 15

    def phase_conv(l):
        with ExitStack() as st:
            Ub = [(sb("cvU%d" % i, [128, ULEN], BF16, st), Buf()) for i in range(2)]
            Dg = [(sb("cvD%d" % i, [128, 31, 128], BF16, st), Buf()) for i in range(2)]
            araw = Ring([(sb("cva%d" % i, [128, 512], F32, st), Buf()) for i in range(2)])
            braw = Ring([(sb("cvb%d" % i, [128, 512], F32, st), Buf()) for i in range(2)])
            accr = Ring([(sb("cvacc%d" % i, [128, 512], F32, st), Buf()) for i in range(2)])
            abf = Ring([(sb("cvab%d" % i, [128, 512], BF16, st), Buf()) for i in range(2)])
            dd = Ring([(sb("cvd%d" % i, [128, 512], F32, st), Buf()) for i in range(2)])
            sqb = Ring([(sb("cvs%d" % i, [128, 512], BF16, st), Buf()) for i in range(2)])
            rsd = Ring([(sb("cvr%d" % i, [128, 512], F32, st), Buf()) for i in range(2)])
            pc = Ring([(ps("cvpc%d" % i, [128, 512], F32, st), Buf()) for i in range(2)])
            pm = Ring([(ps("cvpm%d" % i, [128, 512], F32, st), Buf()) for i in range(2)])
            pv = Ring([(ps("cvpv%d" % i, [128, 512], F32, st), Buf()) for i in range(2)])
            cwl = sb("cwl", [128, 8, 31], F32, st)
            cwlB = Buf()
            S.dma("sp", cwl[:], conv_wT[:, l], writes=[cwlB])
            for i in range(2):
                S.op("dve", lambda e: e.memset(Ub[i][0][:], 0.0), writes=[Ub[i][1]])
            def diag(c2):
                Dm2, DB2 = Dg[c2 % 2]
                S.op("dve", lambda e: e.tensor_tensor(Dm2[:], ident_bf[:].unsqueeze(1).broadcast_to([128, 31, 128]),
                                                      cwl[:, c2, :].unsqueeze(2).broadcast_to([128, 31, 128]), ALU.mult),
                     reads=[cbfB, cwlB], writes=[DB2])

            def uprep(c2, ti):
                U2, UB2 = Ub[c2 % 2]
                t0, sz = TILES[ti]
                ar = 5120 + c2 * 128
                br = 6144 + c2 * 128
                a_, aB = araw.next(); b_, bB = braw.next()
                S.dma("sp", a_[:, 0:sz], PROJ[ar:ar + 128, t0:t0 + sz], reads=[DB("proj", ar // 128, ti)], writes=[aB])
                S.dma("sp", b_[:, 0:sz], PROJ[br:br + 128, t0:t0 + sz], reads=[DB("proj", br // 128, ti)], writes=[bB])
                S.op("act", lambda e: e.activation(b_[:, 0:sz], b_[:, 0:sz], AF.Sigmoid), reads=[bB], writes=[bB])
                uo = UOFF_C if ti == 0 else UOFF_L + (t0 - 256)
                S.op("dve", lambda e: e.tensor_tensor(U2[:, uo:uo + sz], a_[:, 0:sz], b_[:, 0:sz], ALU.mult), reads=[aB, bB], writes=[UB2])

            diag(0)
            for ti in range(len(TILES)):
                uprep(0, ti)
            for cc in range(8):
                U, UB = Ub[cc % 2]
                Dm, DB_ = Dg[cc % 2]
                nxt = cc + 1 < 8
                if nxt:
                    diag(cc + 1)

                def convmm(ti, hooks=()):
                    t0, sz = TILES[ti]
                    uo = UOFF_C if ti == 0 else UOFF_L + (t0 - 256)
                    p, pB = pc.next()
                    hooks = list(hooks)
                    for k in range(31):
                        S.op("pe", mm(p[:, 0:sz], Dm[:, k, :], U[:, uo + k - 15:uo + k - 15 + sz], k == 0, k == 30),
                             reads=[DB_, UB], writes=[pB], inc=(k == 30))
                        if k in (9, 19, 29) and hooks:
                            hooks.pop(0)()
                    while hooks:
                        hooks.pop(0)()
                    acc_, accB = accr.next()
                    S.op("act", lambda e: e.activation(acc_[:, 0:sz], p[:, 0:sz], AF.Identity, bias=cb[:, l, cc:cc + 1], scale=1.0),
                         reads=[pB, smallB], writes=[accB])
                    return (acc_, accB)

                def lnorm_parts(ti, A):
                    t0, sz = TILES[ti]
                    acc_, accB = A
                    acc = acc_[:, 0:sz]
                    ab_, abB = abf.next()
                    d_, dB = dd.next()
                    s_, sB = sqb.next()
                    r_, rB = rsd.next()
                    p1, p1B = pm.next()
                    p2, p2B = pv.next()
                    S.op("act", lambda e: e.copy(ab_[:, 0:sz], acc), reads=[accB], writes=[abB])

                    def part1():
                        S.op("pe", mm(p1[:, 0:sz], ones_bf[:], ab_[:, 0:sz], True, True), reads=[abB, cbfB], writes=[p1B])
                        S.op("dve", lambda e: e.scalar_tensor_tensor(d_[:, 0:sz], p1[:, 0:sz], -1.0 / 128, acc, ALU.mult, ALU.add),
                             reads=[p1B, accB], writes=[dB])
                        S.op("act", lambda e: e.activation(s_[:, 0:sz], d_[:, 0:sz], AF.Square), reads=[dB], writes=[sB])

                    def part2():
                        S.op("pe", mm(p2[:, 0:sz], ones_bf[:], s_[:, 0:sz], True, True), reads=[sB, cbfB], writes=[p2B])
                        S.op("act", lambda e: e.activation(r_[:, 0:sz], p2[:, 0:sz], AF.Ln, bias=LN_EPS, scale=1.0 / 128), reads=[p2B], writes=[rB])
                        S.op("act", lambda e: e.activation(r_[:, 0:sz], r_[:, 0:sz], AF.Exp, scale=-0.5), reads=[rB], writes=[rB])
                        S.op("dve", lambda e: e.tensor_tensor(d_[:, 0:sz], d_[:, 0:sz], r_[:, 0:sz], ALU.mult), reads=[dB, rB], writes=[dB], chain=True)
                        S.op("dve", lambda e: e.tensor_scalar(d_[:, 0:sz], d_[:, 0:sz], clw[:, l, cc:cc + 1], clb[:, l, cc:cc + 1], ALU.mult, ALU.add),
                             reads=[dB, smallB], writes=[dB], chain=True)
                        S.op("act", lambda e: e.activation(r_[:, 0:sz], d_[:, 0:sz], AF.Exp, scale=-1.0), reads=[dB], writes=[rB])

                    def part3():
                        S.op("dve", lambda e: e.tensor_scalar(r_[:, 0:sz], r_[:, 0:sz], 1.0, None, ALU.add), reads=[rB], writes=[rB])
                        S.op("dve", lambda e: e.reciprocal(r_[:, 0:sz], r_[:, 0:sz]), reads=[rB], writes=[rB], chain=True)
                        S.op("dve", lambda e: e.tensor_tensor(H[:, 8 + cc, t0:t0 + sz], d_[:, 0:sz], r_[:, 0:sz], ALU.mult),
                             reads=[dB, rB], writes=[HB[8 + cc][ti]], chain=True)
                    return [part1, part2, part3]

                prevA = convmm(0, [lambda: uprep(cc + 1, 0)] if nxt else [])
                for ti in range(1, len(TILES)):
                    hooks = lnorm_parts(ti - 1, prevA)
                    if nxt:
                        hooks.append(lambda ti=ti: uprep(cc + 1, ti))
                    prevA = convmm(ti, hooks)
                for part in lnorm_parts(len(TILES) - 1, prevA):
                    part()
            S.barrier()

    GOFF_C = 64
    GOFF_L = 64 + 256 + 64
    GLEN = 64 + 256 + 64 + 4096 + 128

    def phase_ffn_up(l):
        with ExitStack() as st:
            wg = [sb("fwg%d" % i, [128, KC, 256], BF16, st) for i in range(2)]
            wv = [sb("fwv%d" % i, [128, KC, 256], BF16, st) for i in range(2)]
            wgB = [Buf(), Buf()]; wvB = [Buf(), Buf()]
            G = [sb("fG%d" % i, [128, GLEN], F32, st) for i in range(1)]
            GB = [Buf()]
            gc = Ring([(sb("fgc%d" % i, [128, 512], F32, st), Buf()) for i in range(2)])
            gl = Ring([(sb("fgl%d" % i, [128, 512], F32, st), Buf()) for i in range(2)])
            ao = Ring([(sb("fao%d" % i, [128, 512], BF16, st), Buf()) for i in range(2)])
            pg = Ring([(ps("fpg%d" % i, [128, 512], F32, st), Buf()) for i in range(3)])
            pv = Ring([(ps("fpv%d" % i, [128, 512], F32, st), Buf()) for i in range(3)])
            fcwl = sb("fcwl", [128, FC, 9], F32, st)
            fcwB = Buf()
            S.dma("sp", fcwl[:], ffn_cwT[:, l], writes=[fcwB])
            GBt = [Buf() for _ in TILES]
            S.op("dve", lambda e: e.memset(G[0][:], 0.0), writes=GBt)
            g_ = G[0]
            for j in range(FC):
                bi = (j // 2) % 2
                wsub = slice((j % 2) * 128, (j % 2 + 1) * 128)
                if j % 2 == 0:
                    S.dma("pool", wg[bi][:], ffn_up[l][:, j * 128:(j + 2) * 128].rearrange("(kc p) n -> p kc n", p=128), writes=[wgB[bi]])
                    S.dma("pool", wv[bi][:], ffn_up[l][:, DFF + j * 128:DFF + (j + 2) * 128].rearrange("(kc p) n -> p kc n", p=128),
                          writes=[wvB[bi]])

                def gate(ti):
                    t0, sz = TILES[ti]
                    p, pB = pg.next()
                    for kc in range(KC):
                        S.op("pe", mm(p[:, 0:sz], wg[bi][:, kc, wsub], H[:, kc, t0:t0 + sz], kc == 0, kc == KC - 1),
                             reads=[wgB[bi], HB[kc][ti]], writes=[pB], inc=(kc == KC - 1))
                    go = GOFF_C if ti == 0 else GOFF_L + (t0 - 256)
                    S.op("act", lambda e: e.copy(g_[:, go:go + sz], p[:, 0:sz]), reads=[pB], writes=[GBt[ti]])

                def rest(ti):
                    t0, sz = TILES[ti]
                    c_, cB = gc.next()
                    go = GOFF_C if ti == 0 else GOFF_L + (t0 - 256)
                    gr = [GBt[x] for x in (ti - 1, ti, ti + 1) if 1 <= x <= 8] if ti > 0 else [GBt[0]]

                    def wt(k):
                        return fcwl[:, j, k:k + 1]
                    S.op("dve", lambda e: e.tensor_scalar(c_[:, 0:sz], g_[:, go:go + sz], wt(4), fcb[:, l, j:j + 1], ALU.mult, ALU.add),
                         reads=gr + [smallB, fcwB], writes=[cB])
                    if ti == 0:
                        for k, sh in ((3, -1), (5, 1)):
                            S.op("dve", lambda e: e.scalar_tensor_tensor(c_[:, 0:sz], g_[:, go + sh:go + sh + sz], wt(k), c_[:, 0:sz],
                                                                         ALU.mult, ALU.add), reads=gr + [smallB, fcwB, cB], writes=[cB])
                    else:
                        c3 = c_[:, 0:sz].rearrange("p (r w) -> p r w", w=64)
                        for dr in range(3):
                            for dw in range(3):
                                if dr == 1 and dw == 1:
                                    continue
                                wlo = 1 if dw == 0 else 0
                                whi = 63 if dw == 2 else 64
                                src = g_[:, go + (dr - 1) * 64 + (dw - 1) + wlo: go + (dr - 1) * 64 + (dw - 1) + wlo + sz]
                                src3 = src.rearrange("p (r w) -> p r w", w=64)[:, :, 0:whi - wlo]
                                S.op("dve", lambda e: e.scalar_tensor_tensor(c3[:, :, wlo:whi], src3, wt(dr * 3 + dw), c3[:, :, wlo:whi],
                                                                             ALU.mult, ALU.add), reads=gr + [smallB, fcwB, cB], writes=[cB],
                                     chain=True)
                    l_, lB = gl.next()
                    S.op("act", lambda e: e.activation(l_[:, 0:sz], c_[:, 0:sz], AF.Gelu), reads=[cB], writes=[lB])
                    p, pB = pv.next()
                    for kc in range(KC):
                        S.op("pe", mm(p[:, 0:sz], wv[bi][:, kc, wsub], H[:, kc, t0:t0 + sz], kc == 0, kc == KC - 1),
                             reads=[wvB[bi], HB[kc][ti]], writes=[pB], inc=(kc == KC - 1))
                    def fin():
                        a_, aB = ao.next()
                        S.op("dve", lambda e: e.tensor_tensor(a_[:, 0:sz], p[:, 0:sz], l_[:, 0:sz], ALU.mult), reads=[pB, lB], writes=[aB])
                        S.dma("sp", ACTD[j * 128:(j + 1) * 128, t0:t0 + sz], a_[:, 0:sz], reads=[aB], writes=[DB("act", j, ti)])
                    return fin

                gate(0)
                gate(1)
                pend = rest(0)
                for ti in range(2, 9):
                    gate(ti)
                    f = rest(ti - 1)
                    pend()
                    pend = f
                f = rest(8)
                pend()
                f()
            S.barrier()

    SUPER = [(0, 1280, [0, 1, 2]), (1280, 1024, [3, 4]), (2304, 1024, [5, 6]), (3328, 1024, [7, 8])]

    def phase_ffn_down(l):
        with ExitStack() as st:
            Htn = H
            pstride = H[:].ap[0][0]
            wd = [sb("fdw%d" % i, [128, FC, 256], BF16, st) for i in range(2)]
            wdB = [Buf(), Buf()]
            pp = Ring([(ps("fdp%d" % i, [128, 512], F32, st), Buf()) for i in range(4)])
            stg = Ring([(sb("fds%d" % i, [128, 512], F32, st), Buf()) for i in range(4)])
            AbB = [Buf() for _ in range(FC)]
            ev = 0
            wi = 0
            for (s0, slen, tis) in SUPER:
                Ab = bass.AP(Htn, 0, [[pstride, 128], [slen, FC], [1, slen]])
                for j in range(FC):
                    S.dma("sp", Ab[:, j, :], ACTD[j * 128:(j + 1) * 128, s0:s0 + slen],
                          reads=[DB("act", j, ti) for ti in tis], writes=[AbB[j]])
                for oc in range(KC):
                    if oc % 2 == 0:
                        w, wB = wd[wi % 2], wdB[wi % 2]
                        wi += 1
                        S.dma("pool", w[:], ffn_down[l][:, oc * 128:(oc + 2) * 128].rearrange("(kc p) n -> p kc n", p=128), writes=[wB])
                    osub = slice((oc % 2) * 128, (oc % 2 + 1) * 128)
                    for ti in tis:
                        t0, sz = TILES[ti]
                        p, pB = pp.next()
                        for j in range(FC):
                            S.op("pe", mm(p[:, 0:sz], w[:, j, osub], Ab[:, j, t0 - s0:t0 - s0 + sz], j == 0, j == FC - 1),
                                 reads=[wB, AbB[j]], writes=[pB], inc=(j == FC - 1))
                        g, gB = stg.next()
                        if ev % 2 == 0:
                            S.op("act", lambda e: e.copy(g[:, 0:sz], p[:, 0:sz]), reads=[pB], writes=[gB])
                        else:
                            S.op("dve", lambda e: e.tensor_copy(g[:, 0:sz], p[:, 0:sz]), reads=[pB], writes=[gB])
                        ev += 1
                        S.dma("sp", Y[oc * 128:(oc + 1) * 128, t0:t0 + sz], g[:, 0:sz], reads=[gB], writes=[DB("y2", oc, ti)])
            S.barrier()

    phases = []
    for l in range(DEPTH):
        phases.append(("mod%d" % l, lambda l=l: phase_mod(l)))
    for l in range(DEPTH):
        xsrc = xT if l == 0 else XB
        if l == 0:
            phases.append(("norm1_%d" % l, lambda l=l, xsrc=xsrc: phase_resnorm(l, xsrc, "x%d" % l, None, None, None, None, None, 0, False)))
        phases.append(("win%d" % l, lambda l=l: phase_dense(w_in[l], INC, PROJ, "proj")))
        phases.append(("hgpre%d" % l, lambda l=l: phase_hgrn_pre(l)))
        phases.append(("hgrn%d" % l, lambda l=l: phase_hgrn_scan(l)))
        phases.append(("conv%d" % l, lambda l=l: phase_conv(l)))
        phases.append(("wout%d" % l, lambda l=l: phase_dense(w_out[l], D, Y, "y")))
        phases.append(("res1_%d" % l, lambda l=l, xsrc=xsrc: phase_resnorm(l, xsrc, "xa", Y, "ya", 2, X1, "x1", 3, False)))
        phases.append(("ffnup%d" % l, lambda l=l: phase_ffn_up(l)))
        phases.append(("ffndn%d" % l, lambda l=l: phase_ffn_down(l)))
        if l == 0:
            phases.append(("res2_%d" % l, lambda l=l: phase_resnorm(l, X1, "x1b", Y, "y2b", 5, XB, "xb", 0, False, la=1)))
        else:
            phases.append(("res2_%d" % l, lambda l=l: phase_resnorm(l, X1, "x1b", Y, "y2b", 5, None, None, None, True)))
    S.barrier()
    for name, fn in phases:
        fn()
        if stop_after is not None and name == stop_after:
            if "HD" in debug:
                HD = dram_scr("HD", [D, NT], BF16)
                for kc in range(KC):
                    S.dma("sp", HD[kc * 128:(kc + 1) * 128, :], H[:, kc, :], reads=[HB[kc][ti] for ti in range(9)], writes=[DB("hd", kc)])
            break
    S.barrier()
    es.close()
    return nc, S


def _pc(a, inner):
    a = np.asarray(a, np.float32)
    sh = a.shape
    a = a.reshape(sh[:-1] + (sh[-1] // 128, 128))
    return np.ascontiguousarray(np.moveaxis(a, -1, 0))


def make_consts():
    c = np.zeros((128, 832), np.float32)
    c[:, 0:128] = np.eye(128, dtype=np.float32)
    c[:, 128:256] = 1.0
    s = np.arange(32)[:, None]
    t = np.arange(32)[None, :]
    c[0:32, 256:288] = (s <= t)
    c[0:32, 288:320] = (s >= t)
    r = np.ones(512, np.float32)
    r[::32] = 0.0
    c[:, 320:832] = r[None, :]
    return c


def make_in_maps(inp):
    x = np.asarray(inp["x"], np.float32)
    ctx = np.asarray(inp["ctx"], np.float32)
    c = np.asarray(inp["c"], np.float32)
    c_ctx = np.asarray(inp["c_ctx"], np.float32)
    shared = {
        "w_mod": np.ascontiguousarray(inp["w_mod"], np.float32),
        "b_modT": _pc(inp["b_mod"], 96),
        "norm_wT": _pc(inp["norm_w"], KC),
        "w_in": np.ascontiguousarray(inp["w_in"], np.float32),
        "lbT": _pc(inp["lb_logits"], 8),
        "hgwT": np.ascontiguousarray(np.asarray(inp["hg_norm_w"], np.float32).T),
        "conv_wT": np.ascontiguousarray(np.moveaxis(_pc(inp["conv_w"], 8), 2, 3)),
        "conv_bT": _pc(inp["conv_b"], 8),
        "conv_lnwT": _pc(inp["conv_ln_w"], 8),
        "conv_lnbT": _pc(inp["conv_ln_b"], 8),
        "w_out": np.ascontiguousarray(inp["w_out"], np.float32),
        "ffn_up": np.ascontiguousarray(inp["ffn_up"], np.float32),
        "ffn_cwT": np.ascontiguousarray(np.moveaxis(_pc(np.asarray(inp["ffn_conv_w"], np.float32).reshape(2, 9, DFF), FC), 2, 3)),
        "ffn_cbT": _pc(inp["ffn_conv_b"], FC),
        "ffn_down": np.ascontiguousarray(inp["ffn_down"], np.float32),
        "consts": make_consts(),
    }
    maps = []
    for b in range(2):
        m = dict(shared)
        m["xT"] = np.ascontiguousarray(np.concatenate([ctx[b], x[b]], axis=0).T)
        cc = np.stack([c[b], c_ctx], axis=-1)
        m["cT"] = np.ascontiguousarray(np.moveaxis(cc.reshape(KC, 128, 2), 1, 0))
        maps.append(m)
    return maps


_CACHE = {}


def kernel(**inputs):
    if "nc" not in _CACHE:
        _CACHE["nc"] = build_program()[0]
    nc = _CACHE["nc"]
    maps = make_in_maps(inputs)
    res = run_bass_kernel_spmd(nc, maps, core_ids=[0, 1])
    out = np.stack([np.ascontiguousarray(res.results[b]["yT"].T) for b in range(2)], axis=0)
    return out.astype(np.float32)
```
